# Optimizing a Trainium2 kernel written in Bass

```python
import math
import jax, jax.numpy as jnp
from jax import lax
import numpy as np

D_MODEL = 1024
BATCH = 4
SEQ = 8192
DEPTH = 2

HEAD_DIM = 64
SWA_HEADS = 6
SWA_KV_HEADS = 2
SWA_GROUP = SWA_HEADS // SWA_KV_HEADS
WINDOW = 128
POOL_WINDOWS = (2, 4, 8, 16)
POOL_GROUP_DIM = 64
POOL_DIM = POOL_GROUP_DIM * len(POOL_WINDOWS)
MLA_HEADS = 6
Q_LORA = 256
KV_LORA = 128
QK_NOPE = 64
QK_ROPE = 32
V_HEAD = 64
QK_HEAD = QK_NOPE + QK_ROPE
ROPE_THETA = 10000.0
Q_BLOCK = 128
N_BUCKETS = 32
MAX_DISTANCE = 128
SWA_Q_DIM = SWA_HEADS * HEAD_DIM
SWA_KV_DIM = SWA_KV_HEADS * HEAD_DIM
MLA_OUT_DIM = MLA_HEADS * V_HEAD
IN_SPLITS = (SWA_Q_DIM, SWA_KV_DIM, SWA_KV_DIM, POOL_DIM, Q_LORA, KV_LORA, QK_ROPE)
IN_DIM = sum(IN_SPLITS)
MIX_DIM = SWA_Q_DIM + POOL_DIM + MLA_OUT_DIM
D_FF = -(-8 * D_MODEL // (3 * 256)) * 256
EPS = 1e-6

kernel_name = "hybrid_swa_pool_mla_block"


def rms_norm(x, g):
    xf = x.astype(jnp.float32)
    y = xf * lax.rsqrt(jnp.mean(xf * xf, axis=-1, keepdims=True) + EPS)
    return (y * g.astype(jnp.float32)).astype(x.dtype)


def t5_causal_bucket(dist):
    n = jnp.maximum(dist, 0)
    max_exact = N_BUCKETS // 2
    nf = jnp.maximum(n, 1).astype(jnp.float32)
    large = max_exact + (jnp.log(nf / max_exact) / math.log(MAX_DISTANCE / max_exact)
                         * (N_BUCKETS - max_exact)).astype(jnp.int32)
    large = jnp.minimum(large, N_BUCKETS - 1)
    return jnp.where(n < max_exact, n, large)


def apply_rope(x, cos, sin):
    half = x.shape[-1] // 2
    x1, x2 = x[..., :half], x[..., half:]
    return jnp.concatenate([x1 * cos - x2 * sin, x1 * sin + x2 * cos], axis=-1).astype(x.dtype)


def swa_mixer(q, k, v, q_gain, k_gain, sinks, rel_bias):
    B, S = q.shape[:2]
    nb = S // WINDOW
    q = rms_norm(q.reshape(B, S, SWA_HEADS, HEAD_DIM), q_gain)
    k = rms_norm(k.reshape(B, S, SWA_KV_HEADS, HEAD_DIM), k_gain)
    v = v.reshape(B, S, SWA_KV_HEADS, HEAD_DIM)
    qb = q.reshape(B, nb, WINDOW, SWA_KV_HEADS, SWA_GROUP, HEAD_DIM)

    def band(t):
        tb = t.reshape(B, nb, WINDOW, SWA_KV_HEADS, HEAD_DIM)
        prev = jnp.pad(tb, ((0, 0), (1, 0), (0, 0), (0, 0), (0, 0)))[:, :-1]
        return jnp.concatenate([prev, tb], axis=2)

    kb, vb = band(k), band(v)
    s = jnp.einsum('bnqhgd,bnkhd->bnhgqk', qb, kb).astype(jnp.float32) * (HEAD_DIM ** -0.5)
    q_loc = jnp.arange(WINDOW)
    k_loc = jnp.arange(2 * WINDOW)
    dist = q_loc[:, None] + WINDOW - k_loc[None, :]
    band_ok = (dist >= 0) & (dist < WINDOW)
    bias = rel_bias[t5_causal_bucket(dist)].astype(jnp.float32)
    bias = bias.transpose(2, 0, 1).reshape(SWA_KV_HEADS, SWA_GROUP, WINDOW, 2 * WINDOW)
    key_abs = jnp.arange(nb)[:, None] * WINDOW - WINDOW + k_loc[None, :]
    mask = band_ok[None] & (key_abs >= 0)[:, None, :]
    s = jnp.where(mask[None, :, None, None], s + bias[None, None], -jnp.inf)
    sk = sinks.astype(jnp.float32).reshape(1, 1, SWA_KV_HEADS, SWA_GROUP, 1, 1)
    lse = jnp.logaddexp(jax.nn.logsumexp(s, axis=-1, keepdims=True), sk)
    p = jnp.exp(s - lse)
    o = jnp.einsum('bnhgqk,bnkhd->bnqhgd', p.astype(v.dtype), vb)
    return o.reshape(B, S, SWA_Q_DIM)


def pool_mixer(u, pool_w, pool_scale):
    S = u.shape[1]
    uf = u.astype(jnp.float32)
    cs = jnp.pad(lax.cumsum(uf, axis=1), ((0, 0), (1, 0), (0, 0)))
    t = jnp.arange(S)
    outs = []
    for g, w in enumerate(POOL_WINDOWS):
        sl = slice(g * POOL_GROUP_DIM, (g + 1) * POOL_GROUP_DIM)
        c = cs[..., sl]
        lo = jnp.pad(c[:, :S + 1 - w], ((0, 0), (w - 1, 0), (0, 0)))
        count = jnp.minimum(t + 1, w).astype(jnp.float32)[None, :, None]
        d = (c[:, 1:] - lo) / count - uf[..., sl]
        outs.append(jnp.einsum('bsc,cd->bsd', d, pool_w[g].astype(jnp.float32)))
    y = jnp.concatenate(outs, axis=-1) * pool_scale.astype(jnp.float32)
    return y.astype(u.dtype)


def mla_mixer(c_q, c_kv, k_rope, positions, q_a_gain, w_qb, kv_a_gain, w_kvb,
              q_nope_gain, q_rope_gain, k_nope_gain, k_rope_gain):
    B, S = c_q.shape[:2]
    q = (rms_norm(c_q, q_a_gain) @ w_qb).reshape(B, S, MLA_HEADS, QK_HEAD)
    kv = (rms_norm(c_kv, kv_a_gain) @ w_kvb).reshape(B, S, MLA_HEADS, QK_NOPE + V_HEAD)
    q_nope = rms_norm(q[..., :QK_NOPE], q_nope_gain)
    q_rope = rms_norm(q[..., QK_NOPE:], q_rope_gain)
    k_nope = rms_norm(kv[..., :QK_NOPE], k_nope_gain)
    v = kv[..., QK_NOPE:]
    k_rope = rms_norm(k_rope, k_rope_gain)
    inv_freq = ROPE_THETA ** (-jnp.arange(0, QK_ROPE, 2, dtype=jnp.float32) / QK_ROPE)
    ang = positions.astype(jnp.float32)[..., None] * inv_freq
    cos, sin = jnp.cos(ang), jnp.sin(ang)
    q_rope = apply_rope(q_rope, cos[:, :, None, :], sin[:, :, None, :])
    k_rope = apply_rope(k_rope, cos, sin)
    nb = S // Q_BLOCK
    qn = q_nope.reshape(B, nb, Q_BLOCK, MLA_HEADS, QK_NOPE).transpose(1, 0, 2, 3, 4)
    qr = q_rope.reshape(B, nb, Q_BLOCK, MLA_HEADS, QK_ROPE).transpose(1, 0, 2, 3, 4)
    q_idx = jnp.arange(S).reshape(nb, Q_BLOCK)
    k_idx = jnp.arange(S)
    scale = QK_HEAD ** -0.5

    def block(args):
        qn_b, qr_b, qi = args
        s = (jnp.einsum('bqhd,bkhd->bhqk', qn_b, k_nope)
             + jnp.einsum('bqhd,bkd->bhqk', qr_b, k_rope)).astype(jnp.float32) * scale
        s = jnp.where(k_idx[None, :] <= qi[:, None], s, -jnp.inf)
        p = jax.nn.softmax(s, axis=-1)
        return jnp.einsum('bhqk,bkhd->bqhd', p.astype(v.dtype), v)

    o = lax.map(block, (qn, qr, q_idx))
    return o.transpose(1, 0, 2, 3, 4).reshape(B, S, MLA_OUT_DIM)


def setup_inputs(seed: int = 0) -> dict:
    key = jax.random.key(seed)
    ks = iter(jax.random.split(key, 32))
    f32 = jnp.float32

    def dense(shape, fan_in):
        return jax.random.normal(next(ks), shape, f32) * fan_in ** -0.5

    def gain(shape):
        return 1.0 + 0.02 * jax.random.normal(next(ks), shape, f32)

    x = jax.random.normal(next(ks), (BATCH, SEQ, D_MODEL), f32)
    offsets = jax.random.randint(next(ks), (BATCH, 1), 0, 4096, dtype=jnp.int32)
    positions = offsets + jnp.arange(SEQ, dtype=jnp.int32)[None, :]
    return {
        "x": x,
        "positions": positions,
        "rel_bias": 0.5 * jax.random.normal(next(ks), (N_BUCKETS, SWA_HEADS), f32),
        "attn_norm": gain((DEPTH, D_MODEL)),
        "w_in": dense((DEPTH, D_MODEL, IN_DIM), D_MODEL),
        "swa_q_gain": gain((DEPTH, HEAD_DIM)),
        "swa_k_gain": gain((DEPTH, HEAD_DIM)),
        "swa_sinks": 0.5 * jax.random.normal(next(ks), (DEPTH, SWA_HEADS), f32),
        "pool_w": dense((DEPTH, len(POOL_WINDOWS), POOL_GROUP_DIM, POOL_GROUP_DIM), POOL_GROUP_DIM),
        "pool_scale": gain((DEPTH, POOL_DIM)),
        "mla_q_a_gain": gain((DEPTH, Q_LORA)),
        "mla_w_qb": dense((DEPTH, Q_LORA, MLA_HEADS * QK_HEAD), Q_LORA),
        "mla_kv_a_gain": gain((DEPTH, KV_LORA)),
        "mla_w_kvb": dense((DEPTH, KV_LORA, MLA_HEADS * (QK_NOPE + V_HEAD)), KV_LORA),
        "mla_q_nope_gain": gain((DEPTH, QK_NOPE)),
        "mla_q_rope_gain": gain((DEPTH, QK_ROPE)),
        "mla_k_nope_gain": gain((DEPTH, QK_NOPE)),
        "mla_k_rope_gain": gain((DEPTH, QK_ROPE)),
        "w_out": dense((DEPTH, MIX_DIM, D_MODEL), MIX_DIM),
        "ffn_norm": gain((DEPTH, D_MODEL)),
        "w_gate": dense((DEPTH, D_MODEL, D_FF), D_MODEL),
        "w_up": dense((DEPTH, D_MODEL, D_FF), D_MODEL),
        "w_down": dense((DEPTH, D_FF, D_MODEL), D_FF),
    }


def reference(x, positions, rel_bias, attn_norm, w_in, swa_q_gain, swa_k_gain, swa_sinks,
              pool_w, pool_scale, mla_q_a_gain, mla_w_qb, mla_kv_a_gain, mla_w_kvb,
              mla_q_nope_gain, mla_q_rope_gain, mla_k_nope_gain, mla_k_rope_gain,
              w_out, ffn_norm, w_gate, w_up, w_down):
    split_idx = list(np.cumsum(IN_SPLITS)[:-1])
    for l in range(DEPTH):
        h = rms_norm(x, attn_norm[l])
        proj = h @ w_in[l]
        q_a, k_a, v_a, u_b, cq_c, ckv_c, kr_c = jnp.split(proj, split_idx, axis=-1)
        out_a = swa_mixer(q_a, k_a, v_a, swa_q_gain[l], swa_k_gain[l], swa_sinks[l], rel_bias)
        out_b = pool_mixer(u_b, pool_w[l], pool_scale[l])
        out_c = mla_mixer(cq_c, ckv_c, kr_c, positions, mla_q_a_gain[l], mla_w_qb[l],
                          mla_kv_a_gain[l], mla_w_kvb[l], mla_q_nope_gain[l],
                          mla_q_rope_gain[l], mla_k_nope_gain[l], mla_k_rope_gain[l])
        mixed = jnp.concatenate([out_a, out_b, out_c], axis=-1)
        x = x + mixed @ w_out[l]
        h = rms_norm(x, ffn_norm[l])
        x = x + (jax.nn.silu(h @ w_gate[l]) * (h @ w_up[l])) @ w_down[l]
    return x
```

```python
import contextlib
import math
import numpy as np
import ml_dtypes
import concourse.bass as bass
import concourse.mybir as mybir
from concourse.bass_utils import run_bass_kernel_spmd

F32 = mybir.dt.float32
BF16 = mybir.dt.bfloat16
I32 = mybir.dt.int32
AF = mybir.ActivationFunctionType
ALU = mybir.AluOpType

NCORES = 8
D = 1024
S = 8192
T = 4096
NB = T // 128
DEPTH = 2
DFF = 2816
NFC = DFF // 128
EPS = 1e-6
NEG = -30000.0
IN_DIM = 1312
TWO_PI = 2.0 * math.pi
CW1 = 6.28125
CW2 = TWO_PI - CW1
MAGIC = 12582912.0


class Buf:
    __slots__ = ("name", "last_w", "readers", "sem", "ndma", "chan_readers", "inc")

    def __init__(self, name, inc=16):
        self.name = name
        self.inc = inc
        self.last_w = None
        self.readers = []
        self.sem = None
        self.ndma = 0
        self.chan_readers = []


class Prog:
    COMPUTE = ("pe", "act", "dve", "pool")

    def __init__(self, nc):
        self.nc = nc
        self.ins = []
        self.eng = {"pe": nc.tensor, "act": nc.scalar, "dve": nc.vector,
                    "pool": nc.gpsimd, "sp": nc.sync}
        self.last_on = {}
        self.all_chans = []
        self.pending_bar = {}

    def barrier(self):
        deps = set(self.last_on.values())
        for c in self.all_chans:
            if c.last_w is not None:
                deps.add(c.last_w)
        for s in self.eng:
            self.pending_bar[s] = set(deps) | self.pending_bar.get(s, set())

    def _rec(self, stream, fn, reads, writes, chan):
        j = len(self.ins)
        raw = set()
        oth = set()
        for b in reads:
            if b.last_w is not None:
                raw.add(b.last_w)
            if chan is None:
                b.readers = [r for r in b.readers
                             if not (self.ins[r]["chan"] is None and self.ins[r]["stream"] == stream)]
            else:
                b.readers = [r for r in b.readers if self.ins[r]["chan"] is not chan]
            b.readers.append(j)
        for b in writes:
            if b.last_w is not None:
                oth.add(b.last_w)
            for r in b.readers:
                if r != j:
                    oth.add(r)
            b.last_w = j
            b.readers = []
        if stream in self.pending_bar:
            raw |= self.pending_bar.pop(stream)
        if chan is not None:
            if chan.ndma == 0:
                self.all_chans.append(chan)
            for r in chan.chan_readers:
                oth.add(r)
            chan.chan_readers = []
            chan.ndma += 1
            chan.last_w = j
        self.ins.append({"stream": stream, "fn": fn, "raw": raw, "oth": oth,
                         "chan": chan, "signal": False})
        for d in raw | oth:
            c = self.ins[d]["chan"]
            if c is not None:
                if chan is None:
                    c.chan_readers = [r for r in c.chan_readers
                                      if not (self.ins[r]["chan"] is None and self.ins[r]["stream"] == stream)]
                c.chan_readers.append(j)
        if chan is None:
            self.last_on[stream] = j
        return j

    def op(self, stream, fn, reads=(), writes=()):
        return self._rec(stream, fn, list(reads), list(writes), None)

    def dma(self, stream, out_ap, in_ap, reads, writes, chan, **kw):
        eng = self.eng[stream]
        return self._rec(stream, lambda: eng.dma_start(out=out_ap, in_=in_ap, **kw),
                         list(reads), list(writes), chan)

    def emit(self, stack):
        nc = self.nc
        ins = self.ins
        n = len(ins)
        need = [None] * n
        for j, r in enumerate(ins):
            deps = set(r["raw"])
            for d in r["oth"]:
                di = ins[d]
                if not (di["chan"] is None and r["chan"] is None and di["stream"] == r["stream"] == "pe"):
                    deps.add(d)
            need[j] = deps
            for d in deps:
                ins[d]["signal"] = True
        sems = {s: stack.enter_context(nc.semaphore("prog_" + s)) for s in self.COMPUTE}
        chan_cnt = {}
        ordn = {s: 0 for s in self.COMPUTE}
        sig_ord = [0] * n
        waited = {s: {} for s in self.eng}
        nwaits = 0
        for j, r in enumerate(ins):
            st = r["stream"]
            e = self.eng[st]
            wants = {}
            for d in need[j]:
                di = ins[d]
                if di["chan"] is not None:
                    c = di["chan"]
                    key = ("c", id(c))
                    val = c.inc * chan_cnt[id(c)]
                    semh = c.sem
                else:
                    key = ("e", di["stream"])
                    val = sig_ord[d]
                    semh = sems[di["stream"]]
                if key not in wants or wants[key][1] < val:
                    wants[key] = (semh, val)
            for key, (semh, val) in wants.items():
                if waited[st].get(key, 0) < val:
                    e.wait_ge(semh, val)
                    waited[st][key] = val
                    nwaits += 1
            bi = r["fn"]()
            if r["chan"] is not None:
                c = r["chan"]
                if c.sem is None:
                    c.sem = stack.enter_context(nc.semaphore("ch_" + c.name))
                chan_cnt[id(c)] = chan_cnt.get(id(c), 0) + 1
                bi.then_inc(c.sem, c.inc)
            elif r["signal"]:
                ordn[st] += 1
                sig_ord[j] = ordn[st]
                bi.then_inc(sems[st], 1)
        for c in self.all_chans:
            nc.sync.wait_ge(c.sem, c.inc * chan_cnt[id(c)])
        self.stats = {"n": n, "waits": nwaits, "signals": dict(ordn), "nchan": len(self.all_chans)}


class Tl:
    def __init__(self, t, name):
        self.t = t
        self.b = Buf(name)

    def __getitem__(self, idx):
        return self.t[idx]


class DT:
    def __init__(self, h, name):
        self.h = h
        self.name = name
        self.bufs = {}

    def ap(self):
        return self.h.ap()

    def b(self, key=None):
        if key not in self.bufs:
            self.bufs[key] = Buf("%s/%s" % (self.name, key))
        return self.bufs[key]

    def raw(self, offset, pat):
        return bass.AP(tensor=self.h, offset=offset, ap=pat)


class K:
    def __init__(self, nc, cfg):
        self.nc = nc
        self.cfg = cfg
        self.P = Prog(nc)
        self.dts = {}
        self.chans = {}
        self.es = None
        self.rr = {"ew": 0, "bank": 0, "cast": 0}

    def dram(self, name, shape, dt, kind=None):
        if name in self.dts:
            return self.dts[name]
        if kind is None:
            if name in self.cfg["ext_in"]:
                kind = "ExternalInput"
            elif name in self.cfg["ext_out"]:
                kind = "ExternalOutput"
            else:
                kind = "Internal"
        h = self.nc.dram_tensor(name, list(shape), dt, kind=kind)
        d = DT(h, name)
        self.dts[name] = d
        return d

    def sb(self, name, shape, dt):
        self.rr["uid"] = self.rr.get("uid", 0) + 1
        name = "%s_u%d" % (name, self.rr["uid"])
        t = self.es.enter_context(self.nc.sbuf_tensor(name, list(shape), dt))
        return Tl(t, name)

    def chan(self, name):
        if name not in self.chans:
            self.chans[name] = Buf(name)
        return self.chans[name]

    def balloc(self):
        if not hasattr(self, "_bfree"):
            self._bfree = list(range(8))
        assert self._bfree, "out of PSUM banks"
        return self.PB[self._bfree.pop(0)]

    def bfree(self, pb):
        i = self.PB.index(pb)
        assert i not in self._bfree
        self._bfree.append(i)

    def bank(self):
        i = self.rr["bank"]
        self.rr["bank"] = (i + 1) % 8
        return self.PB[i]

    def DMA(self, out_ap, in_ap, R, W, chan, q="sp", **kw):
        self.P.dma(q, out_ap, in_ap, R, W, self.chan(chan + ("_sw" if q == "pool" else "")), **kw)

    def ALLGATHER(self, send, recv, name):
        nc = self.nc
        if name not in self.chans:
            self.chans[name] = Buf(name, inc=1)
        groups = [[2 * i, 2 * i + 1] for i in range(NCORES // 2)]
        self.P._rec("pool", lambda: nc.gpsimd.collective_compute(
            "AllGather", ALU.bypass, replica_groups=groups,
            ins=[send.ap().opt()], outs=[recv.ap().opt()]), [send.b()], [recv.b()], self.chans[name])

    def MM(self, out_ap, lhsT, rhs, R, W, start=True, stop=True):
        nc = self.nc
        self.P.op("pe", lambda: nc.tensor.matmul(out_ap, lhsT=lhsT, rhs=rhs, start=start, stop=stop), R, W)

    def TR(self, out_ap, in_ap, ident, R, W):
        nc = self.nc
        self.P.op("pe", lambda: nc.tensor.transpose(out=out_ap, in_=in_ap, identity=ident), R, W)

    def ACT(self, out_ap, in_ap, func, R, W, bias=0.0, scale=1.0):
        nc = self.nc
        self.P.op("act", lambda: nc.scalar.activation(out=out_ap, in_=in_ap, func=func, bias=bias, scale=scale), R, W)

    def _ve(self, eng):
        return self.nc.vector if eng == "dve" else self.nc.gpsimd

    def TT(self, eng, out_ap, in0, in1, op, R, W):
        e = self._ve(eng)
        self.P.op(eng, lambda: e.tensor_tensor(out=out_ap, in0=in0, in1=in1, op=op), R, W)

    def TS(self, eng, out_ap, in0, s1, s2, op0, op1, R, W):
        e = self._ve(eng)
        if op1 is None:
            self.P.op(eng, lambda: e.tensor_scalar(out=out_ap, in0=in0, scalar1=s1, scalar2=None, op0=op0), R, W)
        else:
            self.P.op(eng, lambda: e.tensor_scalar(out=out_ap, in0=in0, scalar1=s1, scalar2=s2, op0=op0, op1=op1), R, W)

    def STT(self, eng, out_ap, in0, scalar, in1, op0, op1, R, W):
        e = self._ve(eng)
        self.P.op(eng, lambda: e.scalar_tensor_tensor(out=out_ap, in0=in0, scalar=scalar, in1=in1, op0=op0, op1=op1), R, W)

    def CP(self, eng, out_ap, in_ap, R, W):
        if eng == "act":
            nc = self.nc
            self.P.op("act", lambda: nc.scalar.copy(out=out_ap, in_=in_ap), R, W)
        else:
            e = self._ve(eng)
            self.P.op(eng, lambda: e.tensor_copy(out=out_ap, in_=in_ap), R, W)

    def MEMSET(self, eng, ap, val, W):
        e = self._ve(eng)
        self.P.op(eng, lambda: e.memset(ap, val), [], W)

    def RECIP(self, out_ap, in_ap, R, W):
        nc = self.nc
        self.P.op("dve", lambda: nc.vector.reciprocal(out=out_ap, in_=in_ap), R, W)

    def ew(self):
        i = self.rr["ew"]
        self.rr["ew"] = i + 1
        return "dve" if i % 2 == 0 else "pool"


CB_ONESD, CB_B64, CB_ONES256, CB_ONES128, CB_ONES32, CB_B96, CB_P96, CB_P32, CB_TRI = (
    0, 128, 256, 384, 512, 544, 640, 736, 768)
NCB = 896
CF_ID, CF_ONES, CF_IFQ, CF_IFK, CF_OH, CF_MV = 0, 128, 256, 257, 258, 258 + 383
NCF = 258 + 383 + 383
NROLE = 34


def _t5_bucket(n):
    n = np.maximum(n, 0)
    nf = np.maximum(n, 1).astype(np.float32)
    large = 16 + (np.log(nf / np.float32(16)) / np.float32(math.log(8.0)) * np.float32(16)).astype(np.int32)
    large = np.minimum(large, 31)
    return np.where(n < 16, n, large)


def host_consts():
    cb = np.zeros((128, NCB), np.float32)
    cb[:, CB_ONESD:CB_ONESD + 128] = 1.0 / 1024
    cb[0:64, CB_B64:CB_B64 + 64] = 1.0 / 64
    cb[64:128, CB_B64 + 64:CB_B64 + 128] = 1.0 / 64
    cb[:, CB_ONES256:CB_ONES256 + 128] = 1.0 / 256
    cb[:, CB_ONES128:CB_ONES128 + 128] = 1.0 / 128
    cb[0:32, CB_ONES32:CB_ONES32 + 32] = 1.0 / 32
    cb[0:64, CB_B96:CB_B96 + 64] = 1.0 / 64
    cb[64:96, CB_B96 + 64:CB_B96 + 96] = 1.0 / 32
    for i in range(16):
        cb[80 + i, CB_P96 + 64 + i] = -1.0
        cb[64 + i, CB_P96 + 80 + i] = 1.0
        cb[16 + i, CB_P32 + i] = -1.0
        cb[i, CB_P32 + 16 + i] = 1.0
    p = np.arange(128)[:, None]
    f = np.arange(128)[None, :]
    cb[:, CB_TRI:CB_TRI + 128] = (p <= f).astype(np.float32)
    cf = np.zeros((128, NCF), np.float32)
    cf[:, CF_ID:CF_ID + 128] = np.eye(128, dtype=np.float32)
    cf[:, CF_ONES:CF_ONES + 128] = 1.0
    inv_freq = (np.float32(10000.0) ** (-np.arange(0, 32, 2, dtype=np.float32) / np.float32(32))).astype(np.float32)
    for i in range(16):
        cf[64 + i, CF_IFQ] = inv_freq[i]
        cf[80 + i, CF_IFQ] = inv_freq[i]
        cf[i, CF_IFK] = inv_freq[i]
        cf[16 + i, CF_IFK] = inv_freq[i]
    dist = np.arange(383) - 127
    valid = (dist >= 0) & (dist < 128)
    bk = _t5_bucket(dist)
    for i in range(383):
        if valid[i]:
            cf[bk[i], CF_OH + i] = 1.0
    cf[:, CF_MV:CF_MV + 383] = np.where(valid, 0.0, NEG)[None, :]
    return cb.astype(ml_dtypes.bfloat16), cf


def host_role(core):
    second = (core % 2) == 1
    r = np.zeros((128, NROLE), np.float32)
    r[:, 0] = 0.0 if second else NEG
    r[:, 1] = 1.0 if second else 0.0
    wins = [2, 4, 8, 16]
    for c in range(2):
        for half in range(2):
            w = wins[2 * c + half]
            for t in range(16):
                cnt = float(w) if second else float(min(t + 1, w))
                r[half * 64:(half + 1) * 64, 2 + c * 16 + t] = 1.0 / cnt
    return r


WEIGHT_NAMES = [("rel_bias", [32, 6]), ("attn_norm", [2, 1024]), ("w_in", [2, 1024, 1312]),
                ("swa_q_gain", [2, 64]), ("swa_k_gain", [2, 64]), ("swa_sinks", [2, 6]),
                ("pool_w", [2, 4, 64, 64]), ("pool_scale", [2, 256]), ("mla_q_a_gain", [2, 256]),
                ("mla_w_qb", [2, 256, 576]), ("mla_kv_a_gain", [2, 128]), ("mla_w_kvb", [2, 128, 768]),
                ("mla_q_nope_gain", [2, 64]), ("mla_q_rope_gain", [2, 32]), ("mla_k_nope_gain", [2, 64]),
                ("mla_k_rope_gain", [2, 32]), ("w_out", [2, 1024, 1024]), ("ffn_norm", [2, 1024]),
                ("w_gate", [2, 1024, 2816]), ("w_up", [2, 1024, 2816]), ("w_down", [2, 2816, 1024])]


def layer_tensors(k, l):
    d = {}
    d["qsT"] = k.dram("qsT_%d" % l, [384, T], BF16)
    d["ksT"] = k.dram("ksT_%d" % l, [128, 128 + T], BF16)
    d["vs"] = k.dram("vs_%d" % l, [128 + T, 130], BF16)
    d["uT"] = k.dram("uT_%d" % l, [256, 16 + T], F32)
    d["qmT"] = k.dram("qmT_%d" % l, [6, 96, T], BF16)
    d["ksend"] = [k.dram("ksend%d_%d" % (i, l), [416, T // 2], BF16) for i in range(2)]
    d["krecv"] = [k.dram("krecv%d_%d" % (i, l), [832, T // 2], BF16) for i in range(2)]
    d["vsend"] = [k.dram("vsend%d_%d" % (i, l), [T // 2, 390], BF16) for i in range(2)]
    d["vrecv"] = [k.dram("vrecv%d_%d" % (i, l), [T, 390], BF16) for i in range(2)]
    d["ssend"] = k.dram("ssend_%d" % l, [128, 258], BF16)
    d["srecv"] = k.dram("srecv_%d" % l, [256, 258], BF16)
    d["usend"] = k.dram("usend_%d" % l, [256, 16], F32)
    d["urecv"] = k.dram("urecv_%d" % l, [512, 16], F32)
    return d


def col_gain(k, dst_tile, col, src_dt, l, n, reps, chan):
    for r in range(reps):
        src = src_dt.raw(l * n, [[1, n], [1, 1]])
        k.DMA(dst_tile[r * n:(r + 1) * n, col:col + 1], src, [src_dt.b()], [dst_tile.b], chan)


def rsqrt_from_ms(k, ms_ap, M, N, R, tmp, out_ap, Wt):
    k.ACT(tmp[0:M, 0:N], ms_ap, AF.Ln, R, [tmp.b], bias=k.epsb[0:M, 0:1], scale=1.0)
    k.ACT(out_ap, tmp[0:M, 0:N], AF.Exp, [tmp.b], Wt, scale=-0.5)


def phase_setup(k):
    nc = k.nc
    W = k.W
    k.DMA(k.cb[:, :], k.in_cb.ap(), [k.in_cb.b()], [k.cb.b], "const")
    k.DMA(k.cf[:, :], k.in_cf.ap(), [k.in_cf.b()], [k.cf.b], "const")
    k.DMA(k.role[:, :], k.in_role.ap(), [k.in_role.b()], [k.role.b], "const")
    k.MEMSET("pool", k.epsb[:, :], EPS, [k.epsb.b])
    with contextlib.ExitStack() as es:
        k.es = es
        csq = k.dram("csq", [2, 96, T], F32)
        csk = k.dram("csk", [2, 32, T], F32)
        posi = k.sb("posi", [96, T], I32)
        ang = k.sb("ang", [96, T], F32)
        kf = k.sb("kf", [96, T], F32)
        r1 = k.sb("r1", [96, T], F32)
        r2 = k.sb("r2", [96, T], F32)
        src = k.in_pos.raw(0, [[0, 96], [1, T]])
        k.DMA(posi[:, :], src, [k.in_pos.b()], [posi.b], "ld0")
        for (npart, ifcol, dst) in ((96, CF_IFQ, csq), (32, CF_IFK, csk)):
            sl = slice(0, npart)
            k.CP("dve", ang[sl, :], posi[sl, :], [posi.b], [ang.b])
            k.TS("dve", ang[sl, :], ang[sl, :], k.cf[sl, ifcol:ifcol + 1], None, ALU.mult, None, [ang.b, k.cf.b], [ang.b])
            k.TS("dve", kf[sl, :], ang[sl, :], 1.0 / TWO_PI, MAGIC, ALU.mult, ALU.add, [ang.b], [kf.b])
            k.TS("dve", kf[sl, :], kf[sl, :], MAGIC, None, ALU.subtract, None, [kf.b], [kf.b])
            k.STT("dve", r1[sl, :], kf[sl, :], -CW1, ang[sl, :], ALU.mult, ALU.add, [kf.b, ang.b], [r1.b])
            k.STT("dve", r1[sl, :], kf[sl, :], -CW2, r1[sl, :], ALU.mult, ALU.add, [kf.b, r1.b], [r1.b])
            k.TS("dve", r2[sl, :], r1[sl, :], 3.1415925, -3.1415925, ALU.min, ALU.max, [r1.b], [r2.b])
            k.ACT(r2[sl, :], r2[sl, :], AF.Sin, [r2.b], [r2.b])
            k.DMA(dst.ap()[1, :, :], r2[sl, :], [r2.b], [dst.b()], "st0")
            k.TS("dve", r1[sl, :], r1[sl, :], math.pi / 2, None, ALU.add, None, [r1.b], [r1.b])
            k.TS("dve", kf[sl, :], r1[sl, :], math.pi, -TWO_PI, ALU.is_gt, ALU.mult, [r1.b], [kf.b])
            k.TT("dve", r1[sl, :], r1[sl, :], kf[sl, :], ALU.add, [r1.b, kf.b], [r1.b])
            k.TS("dve", r2[sl, :], r1[sl, :], 3.1415925, -3.1415925, ALU.min, ALU.max, [r1.b, r2.b], [r2.b])
            k.ACT(r2[sl, :], r2[sl, :], AF.Sin, [r2.b], [r2.b])
            k.DMA(dst.ap()[0, :, :], r2[sl, :], [r2.b], [dst.b()], "st0")
        tv = k.dram("tv", [6, 128, 383], F32)
        rb = k.sb("rb", [32, 6], F32)
        lh = k.sb("lh", [32, 128], F32)
        tvs = k.sb("tvs", [128, 383], F32)
        k.DMA(rb[:, :], W["rel_bias"].ap(), [W["rel_bias"].b()], [rb.b], "ld1")
        for h in range(6):
            k.TS("dve", lh[:, :], k.cf[0:32, CF_ONES:CF_ONES + 128], rb[:, h:h + 1], None, ALU.mult, None,
                 [k.cf.b, rb.b], [lh.b])
            pb = k.bank()
            k.MM(pb[:, 0:383], lh[:, :], k.cf[0:32, CF_OH:CF_OH + 383], [lh.b, k.cf.b], [pb.b])
            k.TT("dve", tvs[:, :], pb[:, 0:383], k.cf[:, CF_MV:CF_MV + 383], ALU.add, [pb.b, k.cf.b], [tvs.b])
            k.DMA(tv.ap()[h, :, :], tvs[:, :], [tvs.b], [tv.b()], "st1")
    k.P.barrier()


def load_cast_rows(k, dst_tile, dst_ap_fn, src_dt, src_ap_fn, nchunks, ncols, gain_tile, stg, stgname):
    W_ = stg[0].t.shape[1]
    for c in range(nchunks):
        for p0 in range(0, ncols, W_):
            p1 = min(ncols, p0 + W_)
            i = k.rr["cast"]
            k.rr["cast"] = i + 1
            s = stg[i % len(stg)]
            k.DMA(s[:, 0:p1 - p0], src_ap_fn(c)[:, p0:p1], [src_dt.b()], [s.b], "%s%d" % (stgname, i % len(stg)),
                  q=("sp" if i % 2 == 0 else "pool"))
            eng = ("dve", "act", "pool")[i % 3]
            out_ap = dst_ap_fn(c)[:, p0:p1]
            if gain_tile is None:
                k.CP(eng, out_ap, s[:, 0:p1 - p0], [s.b], [dst_tile.b])
            elif eng == "act":
                k.ACT(out_ap, s[:, 0:p1 - p0], AF.Copy, [s.b, gain_tile.b], [dst_tile.b], scale=gain_tile[:, c:c + 1])
            else:
                k.TS(eng, out_ap, s[:, 0:p1 - p0], gain_tile[:, c:c + 1], None, ALU.mult, None,
                     [s.b, gain_tile.b], [dst_tile.b])


def norm_group(k, ps, M, N, bmat_ap, gain_ap, out_ap, Wout, sqg, lnt, rst, extra_R=()):
    k.ACT(sqg[0:M, 0:N], ps[0:M, 0:N], AF.Square, [ps.b], [sqg.b])
    pb = k.bank()
    k.MM(pb[0:M, 0:N], bmat_ap, sqg[0:M, 0:N], [sqg.b, k.cb.b], [pb.b])
    rsqrt_from_ms(k, pb[0:M, 0:N], M, N, [pb.b], lnt, rst[0:M, 0:N], [rst.b])
    k.STT("dve", out_ap, ps[0:M, 0:N], gain_ap, rst[0:M, 0:N], ALU.mult, ALU.mult,
          [ps.b, rst.b] + list(extra_R), Wout)


def run_batch(k, groups, N, sqg, lng, rsg):
    pbs = []
    for g in groups:
        pb = k.bank()
        g["mm"](pb)
        pbs.append(pb)
    for i, g in enumerate(groups):
        M = g["M"]
        k.ACT(sqg[i][0:M, 0:N], pbs[i][0:M, 0:N], AF.Square, [pbs[i].b], [sqg[i].b])
    keys = []
    for g in groups:
        if g["ms"] not in keys:
            keys.append(g["ms"])
    rs_of = {}
    for ki, key in enumerate(keys):
        mem = [i for i, g in enumerate(groups) if g["ms"] == key]
        M = groups[mem[0]]["M"]
        pm = k.bank()
        for n_, i in enumerate(mem):
            k.MM(pm[0:M, 0:N], groups[i]["bmat"], sqg[i][0:M, 0:N], [sqg[i].b, k.cb.b], [pm.b],
                 start=(n_ == 0), stop=(n_ == len(mem) - 1))
        rs_of[key] = (ki, pm, M)
    for key in keys:
        ki, pm, M = rs_of[key]
        k.ACT(lng[ki][0:M, 0:N], pm[0:M, 0:N], AF.Ln, [pm.b], [lng[ki].b], bias=k.epsb[0:M, 0:1], scale=1.0)
    for key in keys:
        ki, pm, M = rs_of[key]
        k.ACT(rsg[ki][0:M, 0:N], lng[ki][0:M, 0:N], AF.Exp, [lng[ki].b], [rsg[ki].b], scale=-0.5)
    for i, g in enumerate(groups):
        ki, pm, M = rs_of[g["ms"]]
        k.STT("dve", g["out"], pbs[i][0:M, 0:N], g["gain"], rsg[ki][0:M, 0:N], ALU.mult, ALU.mult,
              [pbs[i].b, rsg[ki].b] + list(g["gainR"]), g["outW"])


def phase_A(k, l, xin_mode, xT_src, xT_dst, LT):
    nc = k.nc
    W = k.W
    NT = 512
    with contextlib.ExitStack() as es:
        k.es = es
        win = k.sb("win", [128, 8, IN_DIM], BF16)
        wqb = k.sb("wqb", [128, 2, 576], BF16)
        wkn = k.sb("wkn", [128, 384], BF16)
        wv = k.sb("wv", [128, 384], BF16)
        gA = k.sb("gA", [128, 16], F32)
        gl = k.sb("gl", [128, 8], F32)
        an = W["attn_norm"]
        k.DMA(gA[:, 0:8], an.raw(l * 1024, [[1, 128], [128, 8]]), [an.b()], [gA.b], "ldg", allow_slow_non_contiguous=True)
        qa = W["mla_q_a_gain"]
        k.DMA(gA[:, 8:10], qa.raw(l * 256, [[1, 128], [128, 2]]), [qa.b()], [gA.b], "ldg", allow_slow_non_contiguous=True)
        kva = W["mla_kv_a_gain"]
        k.DMA(gA[:, 10:11], kva.raw(l * 128, [[1, 128], [1, 1]]), [kva.b()], [gA.b], "ldg")
        col_gain(k, gl, 0, W["swa_q_gain"], l, 64, 2, "ldg")
        col_gain(k, gl, 1, W["swa_k_gain"], l, 64, 2, "ldg")
        col_gain(k, gl, 2, W["mla_k_nope_gain"], l, 64, 2, "ldg")
        k.DMA(gl[0:64, 3:4], W["mla_q_nope_gain"].raw(l * 64, [[1, 64], [1, 1]]), [W["mla_q_nope_gain"].b()], [gl.b], "ldg")
        k.DMA(gl[64:96, 3:4], W["mla_q_rope_gain"].raw(l * 32, [[1, 32], [1, 1]]), [W["mla_q_rope_gain"].b()], [gl.b], "ldg")
        k.DMA(gl[0:32, 4:5], W["mla_k_rope_gain"].raw(l * 32, [[1, 32], [1, 1]]), [W["mla_k_rope_gain"].b()], [gl.b], "ldg")
        wi = W["w_in"]
        for c in range(8):
            k.DMA(win[:, c, :], wi.ap()[l, c * 128:(c + 1) * 128, :], [wi.b()], [win.b], "wld", q="pool")
        wq = W["mla_w_qb"]
        for c in range(2):
            k.DMA(wqb[:, c, :], wq.ap()[l, c * 128:(c + 1) * 128, :], [wq.b()], [wqb.b], "wld", q="pool")
        wk = W["mla_w_kvb"]
        wkv_v = wk.ap()[l, :, :].rearrange("p (h t d) -> p t h d", h=6, t=2, d=64)
        k.DMA(wkn[:, :].rearrange("p (h d) -> p h d", h=6), wkv_v[:, 0, :, :], [wk.b()], [wkn.b], "wld", q="pool")
        k.DMA(wv[:, :].rearrange("p (h d) -> p h d", h=6), wkv_v[:, 1, :, :], [wk.b()], [wv.b], "wld", q="pool")

        xtok = k.sb("xtok", [128, 4, D], F32) if xin_mode == "transpose" else None
        xT = [k.sb("xT%d" % i, [128, 8, NT], F32) for i in range(2)]
        sqb = k.sb("sqb", [128, 8, NT], BF16)
        hT = k.sb("hT", [128, 8, NT], BF16)
        lnt = k.sb("lnt", [128, NT], F32)
        rstd = k.sb("rstd", [128, NT], F32)
        sqg = [k.sb("sqg%d" % i, [128, NT], BF16) for i in range(4)]
        lng = [k.sb("lng%d" % i, [128, NT], F32) for i in range(4)]
        rsg = [k.sb("rsg%d" % i, [128, NT], F32) for i in range(4)]
        qs_st = k.sb("qs_st", [128, 3, NT], BF16)
        ks_st = k.sb("ks_st", [128, NT], BF16)
        u_st = k.sb("u_st", [128, 2, NT], F32)
        cqn = k.sb("cqn", [128, 2, NT], BF16)
        ckvn = k.sb("ckvn", [128, NT], BF16)
        krn = k.sb("krn", [32, NT], BF16)
        kr_st = k.sb("kr_st", [32, NT], BF16)
        csk_t = k.sb("csk_t", [32, 2, NT], F32)
        csq_t = k.sb("csq_t", [96, 2, NT], F32)
        t1 = [k.sb("t1_%d" % i, [96, NT], F32) for i in range(3)]
        t2 = [k.sb("t2_%d" % i, [96, NT], F32) for i in range(3)]
        qn = [k.sb("qn%d" % i, [96, NT], BF16) for i in range(3)]
        qm_st = k.sb("qm_st", [96, 6, NT], BF16)
        kn_st = k.sb("kn_st", [128, 3, NT], BF16)
        vs_st = k.sb("vs_st", [128, 4, 130], BF16)
        vm_st = k.sb("vm_st", [128, 4, 390], BF16)
        k.MEMSET("pool", vs_st[:, :, :], 1.0, [vs_st.b])
        k.MEMSET("pool", vm_st[:, :, :], 1.0, [vm_st.b])
        csq = k.dts["csq"]
        csk = k.dts["csk"]
        xin = k.in_x
        B64 = k.cb[:, CB_B64:CB_B64 + 128]

        for j in range(T // NT):
            c0 = j * NT
            xt = xT[j % 2]
            hx = j // 4
            ch = c0 - hx * (T // 2)
            if xin_mode == "transpose":
                for tb in range(4):
                    r0 = c0 + tb * 128
                    k.DMA(xtok[:, tb, :], xin.ap()[r0:r0 + 128, :], [xin.b()], [xtok.b], "xtok")
                for kc in range(8):
                    pb = k.bank()
                    for tb in range(4):
                        k.TR(pb[:, tb * 128:(tb + 1) * 128], xtok[:, tb, kc * 128:(kc + 1) * 128],
                             k.cf[:, CF_ID:CF_ID + 128], [xtok.b, k.cf.b], [pb.b])
                    k.CP("act" if kc % 2 == 0 else "dve", xt[:, kc, :], pb[:, :], [pb.b], [xt.b])
                k.DMA(xT_dst.ap()[:, :, c0:c0 + NT].rearrange("k p t -> p k t"), xt[:, :, :], [xt.b],
                      [xT_dst.b(j)], "xTst%d" % (j % 2))
            else:
                k.DMA(xt[:, :, :], xT_src.ap()[:, :, c0:c0 + NT].rearrange("k p t -> p k t"),
                      [xT_src.b(j)], [xt.b], "xTld%d" % (j % 2))
            k.DMA(csk_t[:, :, :], csk.ap()[:, :, c0:c0 + NT].rearrange("a p t -> p a t"), [csk.b()], [csk_t.b], "cskld")
            k.DMA(csq_t[:, :, :], csq.ap()[:, :, c0:c0 + NT].rearrange("a p t -> p a t"), [csq.b()], [csq_t.b], "csqld")
            for kc in range(8):
                if kc % 2 == 0:
                    k.ACT(sqb[:, kc, :], xt[:, kc, :], AF.Square, [xt.b], [sqb.b])
                else:
                    k.TT("pool", sqb[:, kc, :], xt[:, kc, :], xt[:, kc, :], ALU.mult, [xt.b], [sqb.b])
            pb = k.bank()
            for kc in range(8):
                k.MM(pb[:, 0:NT], k.cb[:, CB_ONESD:CB_ONESD + 128], sqb[:, kc, :], [k.cb.b, sqb.b], [pb.b],
                     start=(kc == 0), stop=(kc == 7))
            rsqrt_from_ms(k, pb[:, 0:NT], 128, NT, [pb.b], lnt, rstd[:, :], [rstd.b])
            for kc in range(8):
                k.STT("dve", hT[:, kc, :], xt[:, kc, :], gA[:, kc:kc + 1], rstd[:, :], ALU.mult, ALU.mult,
                      [xt.b, rstd.b, gA.b], [hT.b])

            def proj_fn(col0, M):
                def f(pbx):
                    for kc in range(8):
                        k.MM(pbx[0:M, 0:NT], win[:, kc, col0:col0 + M], hT[:, kc, :], [win.b, hT.b], [pbx.b],
                             start=(kc == 0), stop=(kc == 7))
                return f

            run_batch(k, [dict(mm=proj_fn(c * 128, 128), M=128, bmat=B64, gain=gl[:, 0:1], gainR=[gl.b],
                               out=qs_st[:, c, :], outW=[qs_st.b], ms="q%d" % c) for c in range(3)],
                      NT, sqg, lng, rsg)
            k.DMA(LT["qsT"].ap()[:, c0:c0 + NT].rearrange("(c p) t -> p c t", p=128), qs_st[:, :, :], [qs_st.b],
                  [LT["qsT"].b()], "qsst")
            pv = k.bank()
            for tb in range(4):
                for kc in range(8):
                    k.MM(pv[:, tb * 128:(tb + 1) * 128], hT[:, kc, tb * 128:(tb + 1) * 128], win[:, kc, 512:640],
                         [win.b, hT.b], [pv.b], start=(kc == 0), stop=(kc == 7))
            k.CP("act", vs_st[:, :, :].rearrange("p t (h e) -> p t h e", h=2)[:, :, :, 0:64],
                 pv[:, :].rearrange("p (t h d) -> p t h d", t=4, h=2), [pv.b], [vs_st.b])
            k.DMA(LT["vs"].ap()[128 + c0:128 + c0 + NT, :].rearrange("(t p) e -> p t e", p=128), vs_st[:, :, :],
                  [vs_st.b], [LT["vs"].b()], "vsst")
            run_batch(k, [dict(mm=proj_fn(384, 128), M=128, bmat=B64, gain=gl[:, 1:2], gainR=[gl.b],
                               out=ks_st[:, :], outW=[ks_st.b], ms="k"),
                          dict(mm=proj_fn(896, 128), M=128, bmat=k.cb[:, CB_ONES256:CB_ONES256 + 128],
                               gain=gA[:, 8:9], gainR=[gA.b], out=cqn[:, 0, :], outW=[cqn.b], ms="cq"),
                          dict(mm=proj_fn(1024, 128), M=128, bmat=k.cb[:, CB_ONES256:CB_ONES256 + 128],
                               gain=gA[:, 9:10], gainR=[gA.b], out=cqn[:, 1, :], outW=[cqn.b], ms="cq")],
                      NT, sqg, lng, rsg)
            k.DMA(LT["ksT"].ap()[:, 128 + c0:128 + c0 + NT], ks_st[:, :], [ks_st.b], [LT["ksT"].b()], "ksst")
            for c in range(2):
                pu = k.bank()
                proj_fn(640 + c * 128, 128)(pu)
                k.CP("act" if c == 0 else "dve", u_st[:, c, :], pu[:, 0:NT], [pu.b], [u_st.b])
            k.DMA(LT["uT"].ap()[:, 16 + c0:16 + c0 + NT].rearrange("(c p) t -> p c t", p=128), u_st[:, :, :],
                  [u_st.b], [LT["uT"].b()], "ust")
            run_batch(k, [dict(mm=proj_fn(1152, 128), M=128, bmat=k.cb[:, CB_ONES128:CB_ONES128 + 128],
                               gain=gA[:, 10:11], gainR=[gA.b], out=ckvn[:, :], outW=[ckvn.b], ms="ckv"),
                          dict(mm=proj_fn(1280, 32), M=32, bmat=k.cb[0:32, CB_ONES32:CB_ONES32 + 32],
                               gain=gl[0:32, 4:5], gainR=[gl.b], out=krn[:, :], outW=[krn.b], ms="kr")],
                      NT, sqg, lng, rsg)
            prot = k.bank()
            k.MM(prot[0:32, 0:NT], k.cb[0:32, CB_P32:CB_P32 + 32], krn[:, :], [k.cb.b, krn.b], [prot.b])
            k.TT("dve", t1[0][0:32, :], prot[0:32, 0:NT], csk_t[:, 1, :], ALU.mult, [prot.b, csk_t.b], [t1[0].b])
            k.TT("pool", t2[0][0:32, :], krn[:, :], csk_t[:, 0, :], ALU.mult, [krn.b, csk_t.b], [t2[0].b])
            k.TT("pool", kr_st[:, :], t1[0][0:32, :], t2[0][0:32, :], ALU.add, [t1[0].b, t2[0].b], [kr_st.b])
            k.DMA(LT["ksend"][hx].ap()[384:416, ch:ch + NT], kr_st[:, :], [kr_st.b], [LT["ksend"][hx].b()], "krst")
            for hb in range(2):
                def qmm(h):
                    def f(ph):
                        for c in range(2):
                            k.MM(ph[0:96, 0:NT], wqb[:, c, h * 96:(h + 1) * 96], cqn[:, c, :], [wqb.b, cqn.b], [ph.b],
                                 start=(c == 0), stop=(c == 1))
                    return f
                run_batch(k, [dict(mm=qmm(hb * 3 + a), M=96, bmat=k.cb[0:96, CB_B96:CB_B96 + 96],
                                   gain=gl[0:96, 3:4], gainR=[gl.b], out=qn[a][:, :], outW=[qn[a].b], ms="qm%d" % a)
                              for a in range(3)], NT, sqg, lng, rsg)
                prots = []
                for a in range(3):
                    prot = k.bank()
                    k.MM(prot[0:96, 0:NT], k.cb[0:96, CB_P96:CB_P96 + 96], qn[a][:, :], [k.cb.b, qn[a].b], [prot.b])
                    prots.append(prot)
                for a in range(3):
                    k.TT("dve", t1[a][:, :], prots[a][0:96, 0:NT], csq_t[:, 1, :], ALU.mult, [prots[a].b, csq_t.b], [t1[a].b])
                    k.TT("pool", t2[a][:, :], qn[a][:, :], csq_t[:, 0, :], ALU.mult, [qn[a].b, csq_t.b], [t2[a].b])
                for a in range(3):
                    k.TT("pool", qm_st[:, hb * 3 + a, :], t1[a][:, :], t2[a][:, :], ALU.add, [t1[a].b, t2[a].b], [qm_st.b])
            k.DMA(LT["qmT"].ap()[:, :, c0:c0 + NT].rearrange("h p t -> p h t"), qm_st[:, :, :], [qm_st.b],
                  [LT["qmT"].b()], "qmst")
            def knmm(c):
                def f(pn):
                    k.MM(pn[:, 0:NT], wkn[:, c * 128:(c + 1) * 128], ckvn[:, :], [wkn.b, ckvn.b], [pn.b])
                return f
            run_batch(k, [dict(mm=knmm(c), M=128, bmat=B64, gain=gl[:, 2:3], gainR=[gl.b],
                               out=kn_st[:, c, :], outW=[kn_st.b], ms="kn%d" % c) for c in range(3)],
                      NT, sqg, lng, rsg)
            k.DMA(LT["ksend"][hx].ap()[0:384, ch:ch + NT].rearrange("(c p) t -> p c t", p=128), kn_st[:, :, :],
                  [kn_st.b], [LT["ksend"][hx].b()], "knst")
            for tb in range(4):
                pvm = k.bank()
                k.MM(pvm[:, 0:384], ckvn[:, tb * 128:(tb + 1) * 128], wv[:, :], [ckvn.b, wv.b], [pvm.b])
                k.CP("act" if tb % 2 == 0 else "dve",
                     vm_st[:, tb, :].rearrange("p (h e) -> p h e", h=6)[:, :, 0:64],
                     pvm[:, 0:384].rearrange("p (h d) -> p h d", h=6), [pvm.b], [vm_st.b])
            k.DMA(LT["vsend"][hx].ap()[ch:ch + NT, :].rearrange("(t p) e -> p t e", p=128), vm_st[:, :, :],
                  [vm_st.b], [LT["vsend"][hx].b()], "vmst")
        k.DMA(LT["ssend"].ap()[:, 0:128], LT["ksT"].ap()[:, T:T + 128], [LT["ksT"].b()], [LT["ssend"].b()], "xcp")
        k.DMA(LT["ssend"].ap()[:, 128:258], LT["vs"].ap()[T:T + 128, :], [LT["vs"].b()], [LT["ssend"].b()], "xcp")
        k.DMA(LT["usend"].ap()[:, :], LT["uT"].ap()[:, T:T + 16], [LT["uT"].b()], [LT["usend"].b()], "xcp")
    k.P.barrier()


class Item:
    def __init__(self, name, p1=None, p2=None, p3=None, needs=()):
        self.name = name
        self.st = [p1, p2, p3]
        self.needs = list(needs)


def run_pipeline(items):
    pos = {it.name: i for i, it in enumerate(items)}
    for i, it in enumerate(items):
        for n_ in it.needs:
            assert pos[n_] <= i - 2, (it.name, n_, pos[n_], i)
    n = len(items)
    for t in range(n + 2):
        if 0 <= t - 2 < n and items[t - 2].st[2] is not None:
            items[t - 2].st[2]()
        if t < n and items[t].st[0] is not None:
            items[t].st[0]()
        if 0 <= t - 1 < n and items[t - 1].st[1] is not None:
            items[t - 1].st[1]()


def phase_A2(k, l, xin_mode, xT_src, xT_dst, LT):
    W = k.W
    NT = 512
    with contextlib.ExitStack() as es:
        k.es = es
        win = k.sb("win", [128, 8, IN_DIM], BF16)
        wqb = k.sb("wqb", [128, 2, 576], BF16)
        wkn = k.sb("wkn", [128, 384], BF16)
        wv = k.sb("wv", [128, 384], BF16)
        gA = k.sb("gA", [128, 16], F32)
        gl = k.sb("gl", [128, 8], F32)
        an = W["attn_norm"]
        k.DMA(gA[:, 0:8], an.raw(l * 1024, [[1, 128], [128, 8]]), [an.b()], [gA.b], "ldg", allow_slow_non_contiguous=True)
        qa = W["mla_q_a_gain"]
        k.DMA(gA[:, 8:10], qa.raw(l * 256, [[1, 128], [128, 2]]), [qa.b()], [gA.b], "ldg", allow_slow_non_contiguous=True)
        kva = W["mla_kv_a_gain"]
        k.DMA(gA[:, 10:11], kva.raw(l * 128, [[1, 128], [1, 1]]), [kva.b()], [gA.b], "ldg")
        col_gain(k, gl, 0, W["swa_q_gain"], l, 64, 2, "ldg")
        col_gain(k, gl, 1, W["swa_k_gain"], l, 64, 2, "ldg")
        col_gain(k, gl, 2, W["mla_k_nope_gain"], l, 64, 2, "ldg")
        k.DMA(gl[0:64, 3:4], W["mla_q_nope_gain"].raw(l * 64, [[1, 64], [1, 1]]), [W["mla_q_nope_gain"].b()], [gl.b], "ldg")
        k.DMA(gl[64:96, 3:4], W["mla_q_rope_gain"].raw(l * 32, [[1, 32], [1, 1]]), [W["mla_q_rope_gain"].b()], [gl.b], "ldg")
        k.DMA(gl[0:32, 4:5], W["mla_k_rope_gain"].raw(l * 32, [[1, 32], [1, 1]]), [W["mla_k_rope_gain"].b()], [gl.b], "ldg")
        wi = W["w_in"]
        for c in range(8):
            k.DMA(win[:, c, :], wi.ap()[l, c * 128:(c + 1) * 128, :], [wi.b()], [win.b], "wld", q="pool")
        wq = W["mla_w_qb"]
        for c in range(2):
            k.DMA(wqb[:, c, :], wq.ap()[l, c * 128:(c + 1) * 128, :], [wq.b()], [wqb.b], "wld", q="pool")
        wk = W["mla_w_kvb"]
        wkv_v = wk.ap()[l, :, :].rearrange("p (h t d) -> p t h d", h=6, t=2, d=64)
        k.DMA(wkn[:, :].rearrange("p (h d) -> p h d", h=6), wkv_v[:, 0, :, :], [wk.b()], [wkn.b], "wld", q="pool")
        k.DMA(wv[:, :].rearrange("p (h d) -> p h d", h=6), wkv_v[:, 1, :, :], [wk.b()], [wv.b], "wld", q="pool")

        xtok = [k.sb("xtok%d" % i, [128, 4, D], F32) for i in range(2)] if xin_mode == "transpose" else None
        xT = [k.sb("xT%d" % i, [128, 8, NT], F32) for i in range(1 if xin_mode == "transpose" else 2)]
        sqb = [k.sb("sqb%d" % i, [128, 8, NT], BF16) for i in range(1)]
        hT = [k.sb("hT%d" % i, [128, 8, NT], BF16) for i in range(2)]
        lnt = k.sb("lnt", [128, NT], F32)
        rstd = k.sb("rstd", [128, NT], F32)
        sqg = [k.sb("sqg%d" % i, [128, NT], BF16) for i in range(4)]
        lng = [k.sb("lng%d" % i, [128, NT], F32) for i in range(4)]
        rsg = [k.sb("rsg%d" % i, [128, NT], F32) for i in range(4)]
        qs_st = k.sb("qs_st", [128, 3, NT], BF16)
        ks_st = k.sb("ks_st", [128, NT], BF16)
        u_st = k.sb("u_st", [128, 2, NT], F32)
        cqn = [k.sb("cqn%d" % i, [128, 2, NT], BF16) for i in range(2)]
        ckvn = [k.sb("ckvn%d" % i, [128, NT], BF16) for i in range(2)]
        krn = [k.sb("krn%d" % i, [32, NT], BF16) for i in range(2)]
        kr_st = k.sb("kr_st", [32, NT], BF16)
        csk_t = [k.sb("csk_t%d" % i, [32, 2, NT], F32) for i in range(2)]
        csq_t = [k.sb("csq_t%d" % i, [96, 2, NT], F32) for i in range(2)]
        t1 = [k.sb("t1_%d" % i, [96, NT], F32) for i in range(4)]
        t2 = [k.sb("t2_%d" % i, [96, NT], F32) for i in range(4)]
        qn = [k.sb("qn%d" % i, [96, NT], BF16) for i in range(6)]
        qm_st = k.sb("qm_st", [96, 6, NT], BF16)
        kn_st = k.sb("kn_st", [128, 3, NT], BF16)
        vs_st = k.sb("vs_st", [128, 4, 130], BF16)
        vm_st = k.sb("vm_st", [128, 4, 390], BF16)
        k.MEMSET("pool", vs_st[:, :, :], 1.0, [vs_st.b])
        k.MEMSET("pool", vm_st[:, :, :], 1.0, [vm_st.b])
        csq = k.dts["csq"]
        csk = k.dts["csk"]
        xin = k.in_x
        B64 = k.cb[:, CB_B64:CB_B64 + 128]
        IDN = k.cf[:, CF_ID:CF_ID + 128]
        items = []
        fronts = []
        mains = []
        nrm_ctr = [0]
        krt = k.sb("krt", [32, NT], F32)

        def norm_item(name, groups, needs, after_p3=None):
            par = nrm_ctr[0] % 2
            nrm_ctr[0] += 1
            st = {}

            def p1():
                st["pbs"] = []
                for g in groups:
                    pb = k.balloc()
                    g["mm"](pb)
                    st["pbs"].append(pb)

            def p2():
                for i, g in enumerate(groups):
                    M = g["M"]
                    sq = sqg[par * 2 + i]
                    k.ACT(sq[0:M, 0:NT], st["pbs"][i][0:M, 0:NT], AF.Square, [st["pbs"][i].b], [sq.b])
                keys = []
                for g in groups:
                    if g["ms"] not in keys:
                        keys.append(g["ms"])
                st["rs"] = {}
                for ki, key in enumerate(keys):
                    mem = [i for i, g in enumerate(groups) if g["ms"] == key]
                    M = groups[mem[0]]["M"]
                    pm = k.balloc()
                    for n_, i in enumerate(mem):
                        sq = sqg[par * 2 + i]
                        k.MM(pm[0:M, 0:NT], groups[i]["bmat"], sq[0:M, 0:NT], [sq.b, k.cb.b], [pm.b],
                             start=(n_ == 0), stop=(n_ == len(mem) - 1))
                    st["rs"][key] = (par * 2 + ki, pm, M)

            def p3():
                for key, (si, pm, M) in st["rs"].items():
                    k.ACT(lng[si][0:M, 0:NT], pm[0:M, 0:NT], AF.Ln, [pm.b], [lng[si].b], bias=k.epsb[0:M, 0:1], scale=1.0)
                for key, (si, pm, M) in st["rs"].items():
                    k.ACT(rsg[si][0:M, 0:NT], lng[si][0:M, 0:NT], AF.Exp, [lng[si].b], [rsg[si].b], scale=-0.5)
                for i, g in enumerate(groups):
                    si, pm, M = st["rs"][g["ms"]]
                    k.STT("dve", g["out"], st["pbs"][i][0:M, 0:NT], g["gain"], rsg[si][0:M, 0:NT], ALU.mult, ALU.mult,
                          [st["pbs"][i].b, rsg[si].b] + list(g["gainR"]), g["outW"])
                for pb in st["pbs"]:
                    k.bfree(pb)
                for key, (si, pm, M) in st["rs"].items():
                    k.bfree(pm)
                if after_p3 is not None:
                    after_p3()
            items.append(Item(name, p1, p2, p3, needs))

        def load_x(jj):
            cc = jj * NT
            if xin_mode == "transpose":
                xk_ = xtok[jj % 2]
                for tb in range(4):
                    r0 = cc + tb * 128
                    k.DMA(xk_[:, tb, :], xin.ap()[r0:r0 + 128, :], [xin.b()], [xk_.b], "xtok%d" % (jj % 2))
            else:
                xx = xT[jj % 2]
                k.DMA(xx[:, :, :], xT_src.ap()[:, :, cc:cc + NT].rearrange("k p t -> p k t"),
                      [xT_src.b(jj)], [xx.b], "xTld%d" % (jj % 2))

        load_x(0)
        for j in range(T // NT):
            items = []
            c0 = j * NT
            jp = j % 2
            xt = xT[0] if xin_mode == "transpose" else xT[jp]
            h_ = hT[jp]
            sq_ = sqb[0]
            hx = j // 4
            ch = c0 - hx * (T // 2)
            T_ = "t%d_" % j

            if xin_mode == "transpose":
                xk = xtok[jp]
                for pair in range(4):
                    st = {}

                    def p1(pair=pair, st=st, xk=xk, c0=c0, j=j):
                        if pair == 0 and j + 1 < T // NT:
                            load_x(j + 1)
                        st["pb"] = []
                        for kc in (2 * pair, 2 * pair + 1):
                            pb = k.balloc()
                            for tb in range(4):
                                k.TR(pb[:, tb * 128:(tb + 1) * 128], xk[:, tb, kc * 128:(kc + 1) * 128], IDN,
                                     [xk.b, k.cf.b], [pb.b])
                            st["pb"].append(pb)

                    def p2(pair=pair, st=st, xt=xt, sq_=sq_, j=j, c0=c0):
                        for n_, kc in enumerate((2 * pair, 2 * pair + 1)):
                            k.CP("act" if n_ == 0 else "dve", xt[:, kc, :], st["pb"][n_][:, :], [st["pb"][n_].b], [xt.b])
                            k.bfree(st["pb"][n_])
                        for n_, kc in enumerate((2 * pair, 2 * pair + 1)):
                            if n_ == 0:
                                k.ACT(sq_[:, kc, :], xt[:, kc, :], AF.Square, [xt.b], [sq_.b])
                            else:
                                k.TT("pool", sq_[:, kc, :], xt[:, kc, :], xt[:, kc, :], ALU.mult, [xt.b], [sq_.b])
                        if pair == 3:
                            k.DMA(xT_dst.ap()[:, :, c0:c0 + NT].rearrange("k p t -> p k t"), xt[:, :, :], [xt.b],
                                  [xT_dst.b(j)], "xTst0")
                    items.append(Item(T_ + "F%d" % pair, p1, p2, None))
            else:
                for pair in range(4):
                    def p1(pair=pair, xt=xt, j=j, c0=c0):
                        if pair == 0 and j + 1 < T // NT:
                            load_x(j + 1)

                    def p2(pair=pair, xt=xt, sq_=sq_):
                        for n_, kc in enumerate((2 * pair, 2 * pair + 1)):
                            if n_ == 0:
                                k.ACT(sq_[:, kc, :], xt[:, kc, :], AF.Square, [xt.b], [sq_.b])
                            else:
                                k.TT("pool", sq_[:, kc, :], xt[:, kc, :], xt[:, kc, :], ALU.mult, [xt.b], [sq_.b])
                    items.append(Item(T_ + "F%d" % pair, p1, p2, None))
            st5 = {}

            def f5p1(st5=st5, sq_=sq_, jp=jp, c0=c0):
                k.DMA(csk_t[jp][:, :, :], csk.ap()[:, :, c0:c0 + NT].rearrange("a p t -> p a t"), [csk.b()],
                      [csk_t[jp].b], "cskld%d" % jp)
                k.DMA(csq_t[jp][:, :, :], csq.ap()[:, :, c0:c0 + NT].rearrange("a p t -> p a t"), [csq.b()],
                      [csq_t[jp].b], "csqld%d" % jp)
                pb = k.balloc()
                for kc in range(8):
                    k.MM(pb[:, 0:NT], k.cb[:, CB_ONESD:CB_ONESD + 128], sq_[:, kc, :], [k.cb.b, sq_.b], [pb.b],
                         start=(kc == 0), stop=(kc == 7))
                st5["pb"] = pb

            def f5p2(st5=st5):
                rsqrt_from_ms(k, st5["pb"][:, 0:NT], 128, NT, [st5["pb"].b], lnt, rstd[:, :], [rstd.b])
                k.bfree(st5["pb"])

            def f5p3(xt=xt, h_=h_):
                for kc in range(8):
                    k.STT("dve", h_[:, kc, :], xt[:, kc, :], gA[:, kc:kc + 1], rstd[:, :], ALU.mult, ALU.mult,
                          [xt.b, rstd.b, gA.b], [h_.b])
            items.append(Item(T_ + "F5", f5p1, f5p2, f5p3, needs=[T_ + "F%d" % p for p in range(4)]))
            fronts.append(items)
            items = []

            def proj_fn(col0, M, h_=h_):
                def f(pbx):
                    for kc in range(8):
                        k.MM(pbx[0:M, 0:NT], win[:, kc, col0:col0 + M], h_[:, kc, :], [win.b, h_.b], [pbx.b],
                             start=(kc == 0), stop=(kc == 7))
                return f

            cq_ = cqn[jp]
            ckv_ = ckvn[jp]
            kr_ = krn[jp]
            O256 = k.cb[:, CB_ONES256:CB_ONES256 + 128]
            norm_item(T_ + "cq", [dict(mm=proj_fn(896, 128), M=128, bmat=O256, gain=gA[:, 8:9], gainR=[gA.b],
                                       out=cq_[:, 0, :], outW=[cq_.b], ms="cq"),
                                  dict(mm=proj_fn(1024, 128), M=128, bmat=O256, gain=gA[:, 9:10], gainR=[gA.b],
                                       out=cq_[:, 1, :], outW=[cq_.b], ms="cq")], [T_ + "F5"])
            norm_item(T_ + "ckvkr", [dict(mm=proj_fn(1152, 128), M=128, bmat=k.cb[:, CB_ONES128:CB_ONES128 + 128],
                                          gain=gA[:, 10:11], gainR=[gA.b], out=ckv_[:, :], outW=[ckv_.b], ms="ckv"),
                                     dict(mm=proj_fn(1280, 32), M=32, bmat=k.cb[0:32, CB_ONES32:CB_ONES32 + 32],
                                          gain=gl[0:32, 4:5], gainR=[gl.b], out=kr_[:, :], outW=[kr_.b], ms="kr")],
                      [T_ + "F5"])
            norm_item(T_ + "q01", [dict(mm=proj_fn(c * 128, 128), M=128, bmat=B64, gain=gl[:, 0:1], gainR=[gl.b],
                                        out=qs_st[:, c, :], outW=[qs_st.b], ms="q%d" % c) for c in range(2)],
                      [T_ + "F5"])

            def st_qk(c0=c0):
                k.DMA(LT["qsT"].ap()[:, c0:c0 + NT].rearrange("(c p) t -> p c t", p=128), qs_st[:, :, :], [qs_st.b],
                      [LT["qsT"].b()], "qsst")
                k.DMA(LT["ksT"].ap()[:, 128 + c0:128 + c0 + NT], ks_st[:, :], [ks_st.b], [LT["ksT"].b()], "ksst")
            norm_item(T_ + "q2k", [dict(mm=proj_fn(256, 128), M=128, bmat=B64, gain=gl[:, 0:1], gainR=[gl.b],
                                        out=qs_st[:, 2, :], outW=[qs_st.b], ms="q2"),
                                   dict(mm=proj_fn(384, 128), M=128, bmat=B64, gain=gl[:, 1:2], gainR=[gl.b],
                                        out=ks_st[:, :], outW=[ks_st.b], ms="k")], [T_ + "F5"], after_p3=st_qk)
            stu = {}

            def up1(stu=stu, proj_fn=proj_fn):
                stu["pb"] = []
                for c in range(2):
                    pu = k.balloc()
                    proj_fn(640 + c * 128, 128)(pu)
                    stu["pb"].append(pu)

            def up2(stu=stu, c0=c0):
                for c in range(2):
                    k.CP("act" if c == 0 else "dve", u_st[:, c, :], stu["pb"][c][:, 0:NT], [stu["pb"][c].b], [u_st.b])
                    k.bfree(stu["pb"][c])
                k.DMA(LT["uT"].ap()[:, 16 + c0:16 + c0 + NT].rearrange("(c p) t -> p c t", p=128), u_st[:, :, :],
                      [u_st.b], [LT["uT"].b()], "ust")
            items.append(Item(T_ + "u", up1, up2, None, needs=[T_ + "F5"]))
            for hb in range(3):
                def qmm(h, cq_=cq_):
                    def f(ph):
                        for c in range(2):
                            k.MM(ph[0:96, 0:NT], wqb[:, c, h * 96:(h + 1) * 96], cq_[:, c, :], [wqb.b, cq_.b], [ph.b],
                                 start=(c == 0), stop=(c == 1))
                    return f
                norm_item(T_ + "qm%d" % hb,
                          [dict(mm=qmm(hb * 2 + a), M=96, bmat=k.cb[0:96, CB_B96:CB_B96 + 96], gain=gl[0:96, 3:4],
                                gainR=[gl.b], out=qn[hb * 2 + a][:, :], outW=[qn[hb * 2 + a].b], ms="qm%d" % a)
                           for a in range(2)], [T_ + "cq"])
            def knmm(c, ckv_=ckv_):
                def f(pn):
                    k.MM(pn[:, 0:NT], wkn[:, c * 128:(c + 1) * 128], ckv_[:, :], [wkn.b, ckv_.b], [pn.b])
                return f
            norm_item(T_ + "kn01", [dict(mm=knmm(c), M=128, bmat=B64, gain=gl[:, 2:3], gainR=[gl.b],
                                         out=kn_st[:, c, :], outW=[kn_st.b], ms="kn%d" % c) for c in range(2)],
                      [T_ + "ckvkr"])

            def st_kn(hx=hx, ch=ch):
                k.DMA(LT["ksend"][hx].ap()[0:384, ch:ch + NT].rearrange("(c p) t -> p c t", p=128), kn_st[:, :, :],
                      [kn_st.b], [LT["ksend"][hx].b()], "knst")
            norm_item(T_ + "kn2", [dict(mm=knmm(2), M=128, bmat=B64, gain=gl[:, 2:3], gainR=[gl.b],
                                        out=kn_st[:, 2, :], outW=[kn_st.b], ms="kn2")], [T_ + "ckvkr"], after_p3=st_kn)
            for ri in range(3):
                strp = {}

                def rp1(ri=ri, strp=strp, kr_=kr_):
                    strp["pb"] = []
                    for a in range(2):
                        prot = k.balloc()
                        q_ = qn[ri * 2 + a]
                        k.MM(prot[0:96, 0:NT], k.cb[0:96, CB_P96:CB_P96 + 96], q_[:, :], [k.cb.b, q_.b], [prot.b])
                        strp["pb"].append(prot)
                    if ri == 0:
                        prot = k.balloc()
                        k.MM(prot[0:32, 0:NT], k.cb[0:32, CB_P32:CB_P32 + 32], kr_[:, :], [k.cb.b, kr_.b], [prot.b])
                        strp["kr"] = prot

                def rp2(ri=ri, strp=strp, kr_=kr_, jp=jp):
                    for a in range(2):
                        q_ = qn[ri * 2 + a]
                        ts = (ri % 2) * 2 + a
                        k.TT("dve", t1[ts][:, :], strp["pb"][a][0:96, 0:NT], csq_t[jp][:, 1, :], ALU.mult,
                             [strp["pb"][a].b, csq_t[jp].b], [t1[ts].b])
                        k.TT("pool", t2[ts][:, :], q_[:, :], csq_t[jp][:, 0, :], ALU.mult, [q_.b, csq_t[jp].b], [t2[ts].b])
                        k.bfree(strp["pb"][a])
                    if ri == 0:
                        k.TT("dve", krt[:, :], strp["kr"][0:32, 0:NT], csk_t[jp][:, 1, :], ALU.mult,
                             [strp["kr"].b, csk_t[jp].b], [krt.b])
                        k.bfree(strp["kr"])

                def rp3(ri=ri, kr_=kr_, jp=jp, hx=hx, ch=ch, c0=c0):
                    for a in range(2):
                        ts = (ri % 2) * 2 + a
                        k.TT("pool", qm_st[:, ri * 2 + a, :], t1[ts][:, :], t2[ts][:, :], ALU.add, [t1[ts].b, t2[ts].b],
                             [qm_st.b])
                    if ri == 0:
                        k.STT("dve", kr_st[:, :], kr_[:, :], 1.0, csk_t[jp][:, 0, :], ALU.mult, ALU.mult,
                              [kr_.b, csk_t[jp].b], [kr_st.b])
                        k.TT("dve", kr_st[:, :], kr_st[:, :], krt[:, :], ALU.add, [kr_st.b, krt.b], [kr_st.b])
                        k.DMA(LT["ksend"][hx].ap()[384:416, ch:ch + NT], kr_st[:, :], [kr_st.b], [LT["ksend"][hx].b()],
                              "krst")
                    if ri == 2:
                        k.DMA(LT["qmT"].ap()[:, :, c0:c0 + NT].rearrange("h p t -> p h t"), qm_st[:, :, :], [qm_st.b],
                              [LT["qmT"].b()], "qmst")
                needs = [T_ + "qm%d" % ri] + ([T_ + "ckvkr"] if ri == 0 else [])
                items.append(Item(T_ + "rope%d" % ri, rp1, rp2, rp3, needs=needs))
            stv = {}

            def vp1(stv=stv, ckv_=ckv_):
                stv["pb"] = []
                for tb in range(4):
                    pvm = k.balloc()
                    k.MM(pvm[:, 0:384], ckv_[:, tb * 128:(tb + 1) * 128], wv[:, :], [ckv_.b, wv.b], [pvm.b])
                    stv["pb"].append(pvm)

            def vp2(stv=stv, hx=hx, ch=ch):
                for tb in range(4):
                    k.CP("act" if tb % 2 == 0 else "dve",
                         vm_st[:, tb, :].rearrange("p (h e) -> p h e", h=6)[:, :, 0:64],
                         stv["pb"][tb][:, 0:384].rearrange("p (h d) -> p h d", h=6), [stv["pb"][tb].b], [vm_st.b])
                    k.bfree(stv["pb"][tb])
                k.DMA(LT["vsend"][hx].ap()[ch:ch + NT, :].rearrange("(t p) e -> p t e", p=128), vm_st[:, :, :],
                      [vm_st.b], [LT["vsend"][hx].b()], "vmst")
            items.append(Item(T_ + "vm", vp1, vp2, None, needs=[T_ + "ckvkr"]))
            stw = {}

            def wp1(stw=stw, h_=h_):
                pv = k.balloc()
                for tb in range(4):
                    for kc in range(8):
                        k.MM(pv[:, tb * 128:(tb + 1) * 128], h_[:, kc, tb * 128:(tb + 1) * 128], win[:, kc, 512:640],
                             [win.b, h_.b], [pv.b], start=(kc == 0), stop=(kc == 7))
                stw["pb"] = pv

            def wp2(stw=stw, c0=c0):
                pv = stw["pb"]
                k.CP("act", vs_st[:, :, :].rearrange("p t (h e) -> p t h e", h=2)[:, :, :, 0:64],
                     pv[:, :].rearrange("p (t h d) -> p t h d", t=4, h=2), [pv.b], [vs_st.b])
                k.bfree(pv)
                k.DMA(LT["vs"].ap()[128 + c0:128 + c0 + NT, :].rearrange("(t p) e -> p t e", p=128), vs_st[:, :, :],
                      [vs_st.b], [LT["vs"].b()], "vsst")
            items.append(Item(T_ + "vs", wp1, wp2, None, needs=[T_ + "F5"]))
            mains.append(items)
        NTL = T // NT
        seq = []
        f0 = fronts[0]
        seq += [f0[0], f0[1], f0[2], f0[3], Item("nop0a"), f0[4], Item("nop0b")]
        for j in range(NTL):
            m = {it.name.split("_", 1)[1]: it for it in mains[j]}
            f = fronts[j + 1] if j + 1 < NTL else None
            order = ["cq", "F0", "ckvkr", "F1", "q01", "F2", "q2k", "F3", "u", "qm0", "F5", "qm1", "qm2",
                     "kn01", "kn2", "rope0", "rope1", "rope2", "vm", "vs"]
            for nm in order:
                if nm[0] == "F":
                    if f is not None:
                        seq.append(f[{"F0": 0, "F1": 1, "F2": 2, "F3": 3, "F5": 4}[nm]])
                else:
                    seq.append(m[nm])
        run_pipeline(seq)
        assert len(k._bfree) == 8, k._bfree
        k.DMA(LT["ssend"].ap()[:, 0:128], LT["ksT"].ap()[:, T:T + 128], [LT["ksT"].b()], [LT["ssend"].b()], "xcp")
        k.DMA(LT["ssend"].ap()[:, 128:258], LT["vs"].ap()[T:T + 128, :], [LT["vs"].b()], [LT["ssend"].b()], "xcp")
        k.DMA(LT["usend"].ap()[:, :], LT["uT"].ap()[:, T:T + 16], [LT["uT"].b()], [LT["usend"].b()], "xcp")
    k.P.barrier()


def softmax_finish_a(k, ot, esink_ap, esink_R, rrow, use_act):
    if use_act:
        if esink_ap is not None:
            k.ACT(rrow[64:65, :], ot[64:65, 0:512], AF.Ln, [ot.b] + esink_R, [rrow.b], bias=esink_ap, scale=1.0)
        else:
            k.ACT(rrow[64:65, :], ot[64:65, 0:512], AF.Ln, [ot.b], [rrow.b])
        k.ACT(rrow[64:65, :], rrow[64:65, :], AF.Exp, [rrow.b], [rrow.b], scale=-1.0)
    else:
        if esink_ap is not None:
            k.TS("dve", rrow[64:65, :], ot[64:65, 0:512], esink_ap, None, ALU.add, None, [ot.b] + esink_R, [rrow.b])
            k.RECIP(rrow[64:65, :], rrow[64:65, :], [rrow.b], [rrow.b])
        else:
            k.RECIP(rrow[64:65, :], ot[64:65, 0:512], [ot.b], [rrow.b])


def softmax_finish_b(k, ot, out_st, out_W, rrow, bcs, pb):
    k.MM(pb[0:64, 0:512], k.cf[64:65, CF_ONES:CF_ONES + 64], rrow[64:65, :], [k.cf.b, rrow.b], [pb.b])
    k.CP("act", bcs[:, :], pb[0:64, 0:512], [pb.b], [bcs.b])
    k.TT("dve", out_st, ot[0:64, 0:512], bcs[:, :], ALU.mult, [ot.b, bcs.b], out_W)


def phase_B_swa(k, l, LT, mixT):
    W = k.W
    with contextlib.ExitStack() as es:
        k.es = es
        qT = [k.sb("sqT%d" % i, [64, T], BF16) for i in range(6)]
        kT = [k.sb("skT%d" % i, [64, 128 + T], BF16) for i in range(2)]
        V = [k.sb("sV%d" % i, [128, NB + 1, 65], BF16) for i in range(2)]
        Tt = [k.sb("sTt%d" % i, [128, 2, 256], F32) for i in range(3)]
        esb = k.sb("esb", [128, 6], F32)
        tmp = [k.sb("stmp%d" % i, [128, 2, 256], F32) for i in range(3)]
        Pt = [k.sb("sP%d" % i, [128, 2, 256], BF16) for i in range(3)]
        rrow = [k.sb("srrow%d" % i, [65, 512], F32) for i in range(4)]
        bcs = k.sb("sbcs", [64, 512], F32)
        ost = [k.sb("sost%d" % i, [64, 512], BF16) for i in range(4)]
        sk = W["swa_sinks"]
        k.DMA(esb[:, :], sk.raw(l * 6, [[0, 128], [1, 6]]), [sk.b()], [esb.b], "ldg")
        k.ACT(esb[:, :], esb[:, :], AF.Exp, [esb.b], [esb.b])
        tv = k.dts["tv"]
        for kv in range(2):
            k.DMA(kT[kv][:, 128:], LT["ksT"].ap()[kv * 64:(kv + 1) * 64, 128:], [LT["ksT"].b()], [kT[kv].b], "skT%d" % kv)
            k.DMA(V[kv][:, 1:, :],
                  LT["vs"].ap()[128:, kv * 65:(kv + 1) * 65].rearrange("(b p) e -> p b e", p=128),
                  [LT["vs"].b()], [V[kv].b], "sV%d" % kv)
        for hq in range(6):
            k.DMA(qT[hq][:, :], LT["qsT"].ap()[hq * 64:(hq + 1) * 64, :], [LT["qsT"].b()], [qT[hq].b], "sqT%d" % hq)
            k.DMA(Tt[hq // 2][:, hq % 2, :], tv.raw(hq * 128 * 383 + 127, [[382, 128], [1, 256]]), [tv.b()],
                  [Tt[hq // 2].b], "sTt%d" % (hq // 2))
        for kv in range(2):
            k.DMA(kT[kv][:, 0:128], LT["srecv"].ap()[kv * 64:(kv + 1) * 64, 0:128], [LT["srecv"].b()], [kT[kv].b],
                  "skT%d" % kv)
            k.DMA(V[kv][:, 0, :], LT["srecv"].ap()[0:128, 128 + kv * 65:128 + (kv + 1) * 65], [LT["srecv"].b()],
                  [V[kv].b], "sV%d" % kv)
        SBK = [k.PB[0], k.PB[1], k.PB[2]]
        OT = [[k.PB[3], k.PB[4]], [k.PB[5], k.PB[6]]]
        BC = k.PB[7]
        steps = [(pi, kb) for pi in range(3) for kb in range(-1, NB)]
        LA = 2

        def geom(kb):
            qlo = max(kb, 0) * 128
            qhi = min(kb + 2, NB) * 128
            return qlo, qhi, qhi - qlo, (0 if kb >= 0 else 128)

        def S_emit(i):
            pi, kb = steps[i]
            qlo, qhi, N, tc = geom(kb)
            pb = SBK[i % 3]
            ks = kb + 1
            for e in range(2):
                hq = 2 * pi + e
                kv = hq // 3
                k.MM(pb[:, e * 256:e * 256 + N], kT[kv][:, ks * 128:(ks + 1) * 128], qT[hq][:, qlo:qhi],
                     [kT[kv].b, qT[hq].b], [pb.b])

        def finish_b(pi, qt):
            for e in range(2):
                hq = 2 * pi + e
                o = ost[e * 2 + qt % 2]
                softmax_finish_b(k, OT[e][qt % 2], o[:, :], [o.b], rrow[e * 2 + qt % 2], bcs, BC)
                k.DMA(mixT.ap()[hq * 64:(hq + 1) * 64, qt * 512:(qt + 1) * 512], o[:, :], [o.b],
                      [mixT.b("swa")], "sost%d" % (e * 2 + qt % 2))

        for i in range(min(LA, len(steps))):
            S_emit(i)
        pend = None
        for i, (pi, kb) in enumerate(steps):
            if i + LA < len(steps):
                S_emit(i + LA)
            ks = kb + 1
            qlo, qhi, N, tc = geom(kb)
            pb = SBK[i % 3]
            tm = tmp[i % 3]
            pt = Pt[i % 3]
            tt = Tt[pi]
            k.STT("dve", tm[:, :, 0:N], pb[:, :].rearrange("p (e n) -> p e n", e=2)[:, :, 0:N], 0.125,
                  tt[:, :, tc:tc + N], ALU.mult, ALU.add, [pb.b, tt.b], [tm.b])
            if kb == -1:
                k.ACT(pt[:, :, 0:N], tm[:, :, 0:N], AF.Exp, [tm.b, k.role.b], [pt.b], bias=k.role[:, 0:1])
            else:
                k.ACT(pt[:, :, 0:N], tm[:, :, 0:N], AF.Exp, [tm.b], [pt.b])
            for e in range(2):
                hq = 2 * pi + e
                kv = hq // 3
                if kb >= 0:
                    ot = OT[e][(kb // 4) % 2]
                    cc = (kb % 4) * 128
                    k.MM(ot[0:65, cc:cc + 128], V[kv][:, ks, :], pt[:, e, 0:128], [V[kv].b, pt.b], [ot.b],
                         start=False, stop=True)
                if kb + 1 < NB:
                    qn_ = kb + 1
                    ot2 = OT[e][(qn_ // 4) % 2]
                    cc = (qn_ % 4) * 128
                    k.MM(ot2[0:65, cc:cc + 128], V[kv][:, ks, :], pt[:, e, N - 128:N], [V[kv].b, pt.b], [ot2.b],
                         start=True, stop=False)
            if pend is not None and i >= pend[0]:
                _, ppi, pqt = pend
                pend = None
                finish_b(ppi, pqt)
            if kb >= 0 and kb % 4 == 3:
                qt = kb // 4
                for e in range(2):
                    hq = 2 * pi + e
                    softmax_finish_a(k, OT[e][qt % 2], esb[64:65, hq:hq + 1], [esb.b], rrow[e * 2 + qt % 2], True)
                pend = (i + 2, pi, qt)
        if pend is not None:
            finish_b(pend[1], pend[2])
    k.P.barrier()


def phase_B_pool(k, l, LT, mixT):
    W = k.W
    NT = 512
    with contextlib.ExitStack() as es:
        k.es = es
        pw32 = k.sb("pw32", [128, 2, 64], F32)
        pwbd = k.sb("pwbd", [128, 2, 128], BF16)
        psc = k.sb("psc", [128, 2], F32)
        ut = [k.sb("put%d" % i, [128, 2, 16 + NT], F32) for i in range(2)]
        s2s = [k.sb("ps2_%d" % i, [128, 2, 16 + NT], F32) for i in range(2)]
        s4s = [k.sb("ps4_%d" % i, [128, 2, 16 + NT], F32) for i in range(2)]
        s8s = [k.sb("ps8_%d" % i, [128, 2, 16 + NT], F32) for i in range(2)]
        s16s = [k.sb("ps16_%d" % i, [128, 2, 16 + NT], F32) for i in range(2)]
        dds = [k.sb("pdd%d" % i, [128, 2, NT], BF16) for i in range(2)]
        t16 = k.sb("pt16", [128, 16], F32)
        pst = [k.sb("ppst%d" % i, [128, 2, NT], BF16) for i in range(2)]
        pwd = W["pool_w"]
        k.DMA(pw32[:, :, :], pwd.raw(l * 4 * 64 * 64, [[64, 128], [128 * 64, 2], [1, 64]]), [pwd.b()], [pw32.b], "ldg")
        k.MEMSET("pool", pwbd[:, :, :], 0.0, [pwbd.b])
        for c in range(2):
            for half in range(2):
                sl = slice(half * 64, half * 64 + 64)
                k.CP("dve", pwbd[sl, c, half * 64:half * 64 + 64], pw32[sl, c, :], [pw32.b], [pwbd.b])
        ps_ = W["pool_scale"]
        k.DMA(psc[:, :], ps_.raw(l * 256, [[1, 128], [128, 2]]), [ps_.b()], [psc.b], "ldg", allow_slow_non_contiguous=True)
        wins = [2, 4, 8, 16]
        L = 16 + NT
        def load_u(jj):
            u_ = ut[jj % 2]
            cc = jj * NT
            if jj == 0:
                k.DMA(u_[:, :, 16:], LT["uT"].ap()[:, 16:L].rearrange("(c p) t -> p c t", p=128), [LT["uT"].b()],
                      [u_.b], "put0")
                k.DMA(u_[:, :, 0:16], LT["urecv"].ap()[0:256, :].rearrange("(c p) t -> p c t", p=128),
                      [LT["urecv"].b()], [u_.b], "put0")
            else:
                k.DMA(u_[:, :, :], LT["uT"].ap()[:, cc:cc + L].rearrange("(c p) t -> p c t", p=128),
                      [LT["uT"].b()], [u_.b], "put%d" % (jj % 2))

        load_u(0)
        for j in range(T // NT):
            c0 = j * NT
            u = ut[j % 2]
            if j + 1 < T // NT:
                load_u(j + 1)
            s2, s4, s8, s16, dd = s2s[j % 2], s4s[j % 2], s8s[j % 2], s16s[j % 2], dds[j % 2]
            if j == 0:
                k.TS("dve", u[:, :, 0:16], u[:, :, 0:16], k.role[:, 1:2], None, ALU.mult, None, [u.b, k.role.b], [u.b])
            k.TT("dve", s2[:, :, 1:L], u[:, :, 1:L], u[:, :, 0:L - 1], ALU.add, [u.b], [s2.b])
            k.TT("pool", s4[:, :, 3:L], s2[:, :, 3:L], s2[:, :, 1:L - 2], ALU.add, [s2.b], [s4.b])
            k.TT("dve", s8[:, :, 7:L], s4[:, :, 7:L], s4[:, :, 3:L - 4], ALU.add, [s4.b], [s8.b])
            k.TT("pool", s16[:, :, 15:L], s8[:, :, 15:L], s8[:, :, 7:L - 8], ALU.add, [s8.b], [s16.b])
            srcs = [s2, s4, s8, s16]
            for g in range(4):
                c = g // 2
                sl = slice((g % 2) * 64, (g % 2) * 64 + 64)
                sw = srcs[g]
                k.STT("dve", dd[sl, c, :], sw[sl, c, 16:L], 1.0 / wins[g], u[sl, c, 16:L], ALU.mult, ALU.subtract,
                      [sw.b, u.b], [dd.b])
                if j == 0:
                    k.TT("dve", t16[sl, :], sw[sl, c, 16:32], k.role[sl, 2 + c * 16:2 + c * 16 + 16], ALU.mult,
                         [sw.b, k.role.b], [t16.b])
                    k.TT("dve", dd[sl, c, 0:16], t16[sl, :], u[sl, c, 16:32], ALU.subtract, [t16.b, u.b, dd.b], [dd.b])
            o = pst[j % 2]
            for c in range(2):
                pb = k.bank()
                k.MM(pb[:, 0:NT], pwbd[:, c, :], dd[:, c, :], [pwbd.b, dd.b], [pb.b])
                k.TS("dve", o[:, c, :], pb[:, 0:NT], psc[:, c:c + 1], None, ALU.mult, None, [pb.b, psc.b], [o.b])
            k.DMA(mixT.ap()[384:640, c0:c0 + NT].rearrange("(c p) t -> p c t", p=128), o[:, :, :], [o.b],
                  [mixT.b("pool")], "ppst%d" % (j % 2))
    k.P.barrier()


def phase_B_mla(k, l, LT, mixT, pre_hook=None):
    scale = 96.0 ** -0.5
    NKB = 2 * NB
    with contextlib.ExitStack() as es:
        k.es = es
        Vall = k.sb("mV", [128, NKB, 390], BF16)
        KT = [k.sb("mKT%d" % i, [96, 2 * T], BF16) for i in range(2)]
        QT = [k.sb("mQT%d" % i, [96, T], BF16) for i in range(2)]
        Pt = [k.sb("mP%d" % i, [128, 512], BF16) for i in range(5)]
        rrow = [k.sb("mrrow%d" % i, [65, 512], F32) for i in range(2)]
        bcs = k.sb("mbcs", [64, 512], F32)
        ost = [k.sb("most%d" % i, [64, 512], BF16) for i in range(2)]
        for part in range(4):
            b0 = part * 16
            if part < 2:
                srcv = LT["vrecv"][part].ap()[0:T // 2, :]
                srcb = LT["vrecv"][part].b()
            else:
                srcv = LT["vsend"][part - 2].ap()
                srcb = LT["vsend"][part - 2].b()
            k.DMA(Vall[:, b0:b0 + 16, :], srcv.rearrange("(b p) e -> p b e", p=128),
                  [srcb], [Vall.b], "mV", q=("sp" if part % 2 == 0 else "pool"))
        SB_ = [k.PB[0], k.PB[1], k.PB[2], k.PB[3], k.PB[7]]
        OTs = [k.PB[4], k.PB[5]]
        LA = 3
        cnt = 0
        pend = None
        def flush_pend():
            nonlocal pend
            if pend is None:
                return
            pot, po, prr, ph, pj, pslot = pend
            pend = None
            softmax_finish_b(k, pot, po[:, :], [po.b], prr, bcs, k.PB[6])
            k.DMA(mixT.ap()[640 + ph * 64:640 + (ph + 1) * 64, pj * 512:(pj + 1) * 512], po[:, :], [po.b],
                  [mixT.b("mla")], "most%d" % pslot)

        def load_head(h):
            kt = KT[h % 2]
            qt_ = QT[h % 2]
            HT = T // 2
            for i in range(2):
                k.DMA(kt[0:64, i * HT:(i + 1) * HT], LT["krecv"][i].ap()[h * 64:(h + 1) * 64, :], [LT["krecv"][i].b()],
                      [kt.b], "mKT%d" % (h % 2))
                k.DMA(kt[0:64, T + i * HT:T + (i + 1) * HT], LT["ksend"][i].ap()[h * 64:(h + 1) * 64, :],
                      [LT["ksend"][i].b()], [kt.b], "mKT%d" % (h % 2))
                k.DMA(kt[64:96, i * HT:(i + 1) * HT], LT["krecv"][i].ap()[384:416, :], [LT["krecv"][i].b()],
                      [kt.b], "mKT%d" % (h % 2), q="pool")
                k.DMA(kt[64:96, T + i * HT:T + (i + 1) * HT], LT["ksend"][i].ap()[384:416, :],
                      [LT["ksend"][i].b()], [kt.b], "mKT%d" % (h % 2), q="pool")
            k.DMA(qt_[:, :], LT["qmT"].ap()[h, :, :], [LT["qmT"].b()], [qt_.b], "mQT%d" % (h % 2))

        load_head(0)
        if pre_hook is not None:
            pre_hook()
        for h in range(6):
            kt = KT[h % 2]
            qt_ = QT[h % 2]
            for j in range(T // 512):
                if j == 1 and h + 1 < 6:
                    load_head(h + 1)
                nkb = NB + 4 * j + 4
                ot = OTs[cnt % 2]
                o = ost[cnt % 2]
                cnt += 1

                def qlo_of(kb):
                    c = kb - (NB + 4 * j)
                    return 128 * max(c, 0)

                def QK(kb):
                    ql = qlo_of(kb)
                    sbk = SB_[kb % 5]
                    k.MM(sbk[:, ql:512], kt[:, kb * 128:(kb + 1) * 128], qt_[:, j * 512 + ql:(j + 1) * 512],
                         [kt.b, qt_.b], [sbk.b])

                for kb in range(min(LA, nkb)):
                    QK(kb)
                for kb in range(nkb):
                    if kb + LA < nkb:
                        QK(kb + LA)
                    if kb == 4:
                        flush_pend()
                    ql = qlo_of(kb)
                    sbk = SB_[kb % 5]
                    pt = Pt[kb % 5]
                    if kb < NB:
                        k.ACT(pt[:, ql:512], sbk[:, ql:512], AF.Exp, [sbk.b, k.role.b], [pt.b],
                              bias=k.role[:, 0:1], scale=scale)
                    else:
                        k.ACT(pt[:, ql:512], sbk[:, ql:512], AF.Exp, [sbk.b], [pt.b], scale=scale)
                    if kb >= NB + 4 * j:
                        k.TT("pool", pt[:, ql:ql + 128], pt[:, ql:ql + 128], k.cb[:, CB_TRI:CB_TRI + 128], ALU.mult,
                             [pt.b, k.cb.b], [pt.b])
                    k.MM(ot[0:65, ql:512], Vall[:, kb, h * 65:(h + 1) * 65], pt[:, ql:512], [Vall.b, pt.b], [ot.b],
                         start=(kb == 0), stop=(kb == nkb - 1))
                softmax_finish_a(k, ot, None, [], rrow[(cnt - 1) % 2], False)
                pend = (ot, o, rrow[(cnt - 1) % 2], h, j, (cnt - 1) % 2)
        flush_pend()
    k.P.barrier()


def phase_B_ffn(k, l, mixT, xT_src, xT_dst, final, pre=None):
    W = k.W
    NT = 256
    with contextlib.ExitStack() as es:
        k.es = es
        if pre is None:
            wout = k.sb("wout", [128, 8, D], BF16)
            wg = k.sb("wg", [128, 8, DFF], BF16)
        else:
            wout, wg = pre["wout"], pre["wg"]
        wu = k.sb("wu", [128, 8, DFF], BF16)
        wd = k.sb("wd", [128, NFC, D], BF16)
        gF = k.sb("gF", [128, 8], F32)
        fn = W["ffn_norm"]
        k.DMA(gF[:, 0:8], fn.raw(l * 1024, [[1, 128], [128, 8]]), [fn.b()], [gF.b], "ldg", allow_slow_non_contiguous=True)
        wlist = [(wu, W["w_up"], 8), (wd, W["w_down"], NFC)]
        if pre is None:
            wlist = [(wout, W["w_out"], 8), (wg, W["w_gate"], 8)] + wlist
        else:
            for (dst, srcd, nch) in ((wu, pre["wub"], 8), (wd, pre["wdb"], NFC)):
                for c in range(nch):
                    k.DMA(dst[:, c, :], srcd.ap()[c * 128:(c + 1) * 128, :], [srcd.b()], [dst.b],
                          "wldb%d" % (c % 2), q=("sp" if c % 2 == 0 else "pool"))
            wlist = []
        for (dst, src, nch) in wlist:
            for c in range(nch):
                k.DMA(dst[:, c, :], src.ap()[l, c * 128:(c + 1) * 128, :], [src.b()], [dst.b], "wld", q="pool")
        mix = [k.sb("fmix%d" % i, [128, 8, NT], BF16) for i in range(1)]
        xt = [k.sb("fxt%d" % i, [128, 8, NT], F32) for i in range(2)]
        h2 = k.sb("fh2", [128, 8, NT], BF16)
        lnt = k.sb("flnt", [128, NT], F32)
        rstd = k.sb("frstd", [128, NT], F32)
        sg = [k.sb("fsg%d" % i, [128, NT], F32) for i in range(2)]
        act = k.sb("fact", [128, NFC, NT], BF16)
        sqb = k.sb("fsqb", [128, 8, NT], BF16)
        ost = k.sb("fost", [128, D], F32) if final else None
        out = k.out if final else None
        NTI = T // NT

        def f_load(i):
            c0 = i * NT
            k.DMA(mix[0][:, :, :], mixT.ap()[:, c0:c0 + NT].rearrange("(c p) t -> p c t", p=128),
                  [mixT.b("swa"), mixT.b("pool"), mixT.b("mla")], [mix[0].b], "fmix0")
            x_ = xt[i % 2]
            k.DMA(x_[:, :, :], xT_src.ap()[:, :, c0:c0 + NT].rearrange("k p t -> p k t"),
                  [xT_src.b(i // 2)], [x_.b], "fxt%d" % (i % 2))

        def f_outproj_norm(i):
            m_ = mix[0]
            x_ = xt[i % 2]
            for m in range(8):
                pb = k.bank()
                for kc in range(8):
                    k.MM(pb[:, 0:NT], wout[:, kc, m * 128:(m + 1) * 128], m_[:, kc, :], [wout.b, m_.b], [pb.b],
                         start=(kc == 0), stop=(kc == 7))
                k.TT("dve", x_[:, m, :], x_[:, m, :], pb[:, 0:NT], ALU.add, [x_.b, pb.b], [x_.b])
            for kc in range(8):
                if kc % 2 == 0:
                    k.ACT(sqb[:, kc, :], x_[:, kc, :], AF.Square, [x_.b], [sqb.b])
                else:
                    k.TT("pool", sqb[:, kc, :], x_[:, kc, :], x_[:, kc, :], ALU.mult, [x_.b], [sqb.b])
            pb = k.bank()
            for kc in range(8):
                k.MM(pb[:, 0:NT], k.cb[:, CB_ONESD:CB_ONESD + 128], sqb[:, kc, :], [k.cb.b, sqb.b], [pb.b],
                     start=(kc == 0), stop=(kc == 7))
            rsqrt_from_ms(k, pb[:, 0:NT], 128, NT, [pb.b], lnt, rstd[:, :], [rstd.b])
            for kc in range(8):
                k.STT("dve", h2[:, kc, :], x_[:, kc, :], gF[:, kc:kc + 1], rstd[:, :],
                      ALU.mult, ALU.mult, [x_.b, rstd.b, gF.b], [h2.b])

        def f_gateup(i):
            for fc in range(NFC):
                pg = k.bank()
                pu = k.bank()
                for kc in range(8):
                    k.MM(pg[:, 0:NT], wg[:, kc, fc * 128:(fc + 1) * 128], h2[:, kc, :], [wg.b, h2.b], [pg.b],
                         start=(kc == 0), stop=(kc == 7))
                for kc in range(8):
                    k.MM(pu[:, 0:NT], wu[:, kc, fc * 128:(fc + 1) * 128], h2[:, kc, :], [wu.b, h2.b], [pu.b],
                         start=(kc == 0), stop=(kc == 7))
                s_ = sg[fc % 2]
                k.ACT(s_[:, :], pg[:, 0:NT], AF.Silu, [pg.b], [s_.b])
                k.TT("dve", act[:, fc, :], s_[:, :], pu[:, 0:NT], ALU.mult, [s_.b, pu.b], [act.b])

        def f_down_store(i):
            c0 = i * NT
            x_ = xt[i % 2]
            for m in range(8):
                pb = k.bank()
                for fc in range(NFC):
                    k.MM(pb[:, 0:NT], wd[:, fc, m * 128:(m + 1) * 128], act[:, fc, :], [wd.b, act.b], [pb.b],
                         start=(fc == 0), stop=(fc == NFC - 1))
                k.TT("dve", x_[:, m, :], x_[:, m, :], pb[:, 0:NT], ALU.add, [x_.b, pb.b], [x_.b])
            if not final:
                k.DMA(xT_dst.ap()[:, :, c0:c0 + NT].rearrange("k p t -> p k t"), x_[:, :, :], [x_.b],
                      [xT_dst.b(i // 2)], "fxst%d" % (i % 2))
            else:
                for tb in range(2):
                    for half in range(2):
                        pb = k.bank()
                        for mm in range(4):
                            m = half * 4 + mm
                            k.TR(pb[:, mm * 128:(mm + 1) * 128], x_[:, m, tb * 128:(tb + 1) * 128],
                                 k.cf[:, CF_ID:CF_ID + 128], [x_.b, k.cf.b], [pb.b])
                        k.CP("act" if half == 0 else "dve", ost[:, half * 512:(half + 1) * 512], pb[:, :],
                             [pb.b], [ost.b])
                    r0 = c0 + tb * 128
                    k.DMA(out.ap()[r0:r0 + 128, :], ost[:, :], [ost.b], [out.b()], "fost")

        f_load(0)
        f_outproj_norm(0)
        for i in range(NTI):
            if i + 1 < NTI:
                f_load(i + 1)
            f_gateup(i)
            if i + 1 < NTI:
                f_outproj_norm(i + 1)
            f_down_store(i)
    k.P.barrier()


def phase_exchange(k, l, LT):
    k.ALLGATHER(LT["ssend"], LT["srecv"], "ag_s")
    k.ALLGATHER(LT["usend"], LT["urecv"], "ag_u")
    for i in range(2):
        k.ALLGATHER(LT["ksend"][i], LT["krecv"][i], "ag_k%d" % i)
        k.ALLGATHER(LT["vsend"][i], LT["vrecv"][i], "ag_v%d" % i)


def build(cfg):
    nc = bass.Bass("TRN2", target_bir_lowering=False)
    k = K(nc, cfg)
    phases = cfg["phases"]
    k.in_x = k.dram("x", [T, D], F32, kind="ExternalInput")
    k.in_pos = k.dram("pos", [1, T], I32, kind="ExternalInput")
    k.in_cb = k.dram("cb", [128, NCB], BF16, kind="ExternalInput")
    k.in_cf = k.dram("cf", [128, NCF], F32, kind="ExternalInput")
    k.in_role = k.dram("role", [128, NROLE], F32, kind="ExternalInput")
    k.W = {n: k.dram(n, shp, F32, kind="ExternalInput") for n, shp in WEIGHT_NAMES}
    if "B1" in phases:
        k.out = k.dram("out", [T, D], F32, kind="ExternalOutput")
    xTv = [k.dram("xT_v%d" % i, [8, 128, T], F32) for i in range(2)]
    mixT = [k.dram("mixT_%d" % i, [D, T], BF16) for i in range(2)]
    LT = [layer_tensors(k, l) for l in range(DEPTH)]
    with contextlib.ExitStack() as outer:
        k.es = outer
        k.cb = k.sb("cbs", [128, NCB], BF16)
        k.cf = k.sb("cfs", [128, NCF], F32)
        k.role = k.sb("roles", [128, NROLE], F32)
        k.epsb = k.sb("epsb", [128, 1], F32)
        k.PB = [Tl(outer.enter_context(nc.psum_tensor("pb%d" % i, [128, 512], F32)), "pb%d" % i) for i in range(8)]
        for ph in phases:
            if ph == "setup":
                phase_setup(k)
            elif ph == "A0":
                (phase_A2 if cfg.get("pipeA", True) else phase_A)(k, 0, "transpose", None, xTv[0], LT[0])
            elif ph == "A1":
                (phase_A2 if cfg.get("pipeA", True) else phase_A)(k, 1, "load", xTv[1], None, LT[1])
            elif ph in ("X0", "X1"):
                phase_exchange(k, int(ph[1]), LT[int(ph[1])])
            elif ph in ("B0", "B1"):
                l = int(ph[1])
                sub = cfg.get("sub", ("swa", "pool", "mla", "ffn"))
                with contextlib.ExitStack() as es_pre:
                    k.es = es_pre
                    pre = {"wout": k.sb("wout", [128, 8, D], BF16), "wg": k.sb("wg", [128, 8, DFF], BF16)}

                    wub = k.dram("wu_bf16", [D, DFF], BF16)
                    wdb = k.dram("wd_bf16", [DFF, D], BF16)
                    pre["wub"] = wub
                    pre["wdb"] = wdb

                    def pre_hook(l=l, pre=pre, wub=wub, wdb=wdb):
                        for (dst, src, nch) in ((pre["wout"], k.W["w_out"], 8), (pre["wg"], k.W["w_gate"], 8)):
                            for c in range(nch):
                                k.DMA(dst[:, c, :], src.ap()[l, c * 128:(c + 1) * 128, :], [src.b()], [dst.b],
                                      "wld", q="pool")
                        for (dstd, src, nch) in ((wub, k.W["w_up"], 8), (wdb, k.W["w_down"], NFC)):
                            for c in range(nch):
                                k.DMA(dstd.ap()[c * 128:(c + 1) * 128, :], src.ap()[l, c * 128:(c + 1) * 128, :],
                                      [src.b()], [dstd.b()], "wcast", q="pool")
                    if "swa" in sub:
                        phase_B_swa(k, l, LT[l], mixT[l])
                    if "pool" in sub:
                        phase_B_pool(k, l, LT[l], mixT[l])
                    if "mla" in sub:
                        phase_B_mla(k, l, LT[l], mixT[l], pre_hook=pre_hook)
                    if "ffn" in sub:
                        phase_B_ffn(k, l, mixT[l], xTv[l], xTv[1] if l == 0 else None, final=(l == 1), pre=pre)
            k.es = outer
        sem_stack = contextlib.ExitStack()
        k.P.emit(sem_stack)
        sem_stack.close()
    return nc, k


def _core_inputs(inputs, cb, cf):
    maps = []
    for c in range(NCORES):
        b, hh = c // 2, c % 2
        m = {"x": np.ascontiguousarray(inputs["x"][b, hh * T:(hh + 1) * T, :]),
             "pos": np.ascontiguousarray(inputs["positions"][b, hh * T:(hh + 1) * T]).reshape(1, T).astype(np.int32),
             "cb": cb, "cf": cf, "role": host_role(c)}
        for n, _ in WEIGHT_NAMES:
            m[n] = np.ascontiguousarray(inputs[n])
        maps.append(m)
    return maps


def kernel(**inputs):
    cb, cf = host_consts()
    base = _core_inputs(inputs, cb, cf)
    ids = list(range(NCORES))
    cfg = {"phases": ["setup", "A0", "X0", "B0", "A1", "X1", "B1"], "ext_in": set(), "ext_out": set()}
    nc, _ = build(cfg)
    res = run_bass_kernel_spmd(nc, base, core_ids=ids).results
    out = np.zeros((4, S, D), np.float32)
    for c in range(NCORES):
        out[c // 2, (c % 2) * T:(c % 2 + 1) * T, :] = np.asarray(res[c]["out"], dtype=np.float32)
    return out
```

```python
import contextlib
import math
import numpy as np
import ml_dtypes
import concourse.bass as bass
import concourse.mybir as mybir
from concourse.bass_utils import run_bass_kernel_spmd

F32 = mybir.dt.float32
BF16 = mybir.dt.bfloat16
I32 = mybir.dt.int32
AF = mybir.ActivationFunctionType
ALU = mybir.AluOpType

NCORES = 8
D = 1024
S = 8192
T = 4096
NB = T // 128
DEPTH = 2
DFF = 2816
NFC = DFF // 128
EPS = 1e-6
NEG = -30000.0
IN_DIM = 1312
TWO_PI = 2.0 * math.pi
CW1 = 6.28125
CW2 = TWO_PI - CW1
MAGIC = 12582912.0


class Buf:
    __slots__ = ("name", "last_w", "readers", "sem", "ndma", "chan_readers", "inc")

    def __init__(self, name, inc=16):
        self.name = name
        self.inc = inc
        self.last_w = None
        self.readers = []
        self.sem = None
        self.ndma = 0
        self.chan_readers = []


class Prog:
    COMPUTE = ("pe", "act", "dve", "pool")

    def __init__(self, nc):
        self.nc = nc
        self.ins = []
        self.eng = {"pe": nc.tensor, "act": nc.scalar, "dve": nc.vector,
                    "pool": nc.gpsimd, "sp": nc.sync}
        self.last_on = {}
        self.all_chans = []
        self.pending_bar = {}

    def barrier(self):
        deps = set(self.last_on.values())
        for c in self.all_chans:
            if c.last_w is not None:
                deps.add(c.last_w)
        for s in self.eng:
            self.pending_bar[s] = set(deps) | self.pending_bar.get(s, set())

    def _rec(self, stream, fn, reads, writes, chan):
        j = len(self.ins)
        raw = set()
        oth = set()
        for b in reads:
            if b.last_w is not None:
                raw.add(b.last_w)
            if chan is None:
                b.readers = [r for r in b.readers
                             if not (self.ins[r]["chan"] is None and self.ins[r]["stream"] == stream)]
            else:
                b.readers = [r for r in b.readers if self.ins[r]["chan"] is not chan]
            b.readers.append(j)
        for b in writes:
            if b.last_w is not None:
                oth.add(b.last_w)
            for r in b.readers:
                if r != j:
                    oth.add(r)
            b.last_w = j
            b.readers = []
        if stream in self.pending_bar:
            raw |= self.pending_bar.pop(stream)
        if chan is not None:
            if chan.ndma == 0:
                self.all_chans.append(chan)
            for r in chan.chan_readers:
                oth.add(r)
            chan.chan_readers = []
            chan.ndma += 1
            chan.last_w = j
        self.ins.append({"stream": stream, "fn": fn, "raw": raw, "oth": oth,
                         "chan": chan, "signal": False})
        for d in raw | oth:
            c = self.ins[d]["chan"]
            if c is not None:
                if chan is None:
                    c.chan_readers = [r for r in c.chan_readers
                                      if not (self.ins[r]["chan"] is None and self.ins[r]["stream"] == stream)]
                c.chan_readers.append(j)
        if chan is None:
            self.last_on[stream] = j
        return j

    def op(self, stream, fn, reads=(), writes=()):
        return self._rec(stream, fn, list(reads), list(writes), None)

    def dma(self, stream, out_ap, in_ap, reads, writes, chan, **kw):
        eng = self.eng[stream]
        return self._rec(stream, lambda: eng.dma_start(out=out_ap, in_=in_ap, **kw),
                         list(reads), list(writes), chan)

    def emit(self, stack):
        nc = self.nc
        ins = self.ins
        n = len(ins)
        need = [None] * n
        for j, r in enumerate(ins):
            deps = set(r["raw"])
            for d in r["oth"]:
                di = ins[d]
                if not (di["chan"] is None and r["chan"] is None and di["stream"] == r["stream"] == "pe"):
                    deps.add(d)
            need[j] = deps
            for d in deps:
                ins[d]["signal"] = True
        sems = {s: stack.enter_context(nc.semaphore("prog_" + s)) for s in self.COMPUTE}
        chan_cnt = {}
        ordn = {s: 0 for s in self.COMPUTE}
        sig_ord = [0] * n
        waited = {s: {} for s in self.eng}
        nwaits = 0
        for j, r in enumerate(ins):
            st = r["stream"]
            e = self.eng[st]
            wants = {}
            for d in need[j]:
                di = ins[d]
                if di["chan"] is not None:
                    c = di["chan"]
                    key = ("c", id(c))
                    val = c.inc * chan_cnt[id(c)]
                    semh = c.sem
                else:
                    key = ("e", di["stream"])
                    val = sig_ord[d]
                    semh = sems[di["stream"]]
                if key not in wants or wants[key][1] < val:
                    wants[key] = (semh, val)
            for key, (semh, val) in wants.items():
                if waited[st].get(key, 0) < val:
                    e.wait_ge(semh, val)
                    waited[st][key] = val
                    nwaits += 1
            bi = r["fn"]()
            if r["chan"] is not None:
                c = r["chan"]
                if c.sem is None:
                    c.sem = stack.enter_context(nc.semaphore("ch_" + c.name))
                chan_cnt[id(c)] = chan_cnt.get(id(c), 0) + 1
                bi.then_inc(c.sem, c.inc)
            elif r["signal"]:
                ordn[st] += 1
                sig_ord[j] = ordn[st]
                bi.then_inc(sems[st], 1)
        for c in self.all_chans:
            nc.sync.wait_ge(c.sem, c.inc * chan_cnt[id(c)])
        self.stats = {"n": n, "waits": nwaits, "signals": dict(ordn), "nchan": len(self.all_chans)}


class Tl:
    def __init__(self, t, name):
        self.t = t
        self.b = Buf(name)

    def __getitem__(self, idx):
        return self.t[idx]


class DT:
    def __init__(self, h, name):
        self.h = h
        self.name = name
        self.bufs = {}

    def ap(self):
        return self.h.ap()

    def b(self, key=None):
        if key not in self.bufs:
            self.bufs[key] = Buf("%s/%s" % (self.name, key))
        return self.bufs[key]

    def raw(self, offset, pat):
        return bass.AP(tensor=self.h, offset=offset, ap=pat)


class K:
    def __init__(self, nc, cfg):
        self.nc = nc
        self.cfg = cfg
        self.P = Prog(nc)
        self.dts = {}
        self.chans = {}
        self.es = None
        self.rr = {"ew": 0, "bank": 0, "cast": 0}

    def dram(self, name, shape, dt, kind=None):
        if name in self.dts:
            return self.dts[name]
        if kind is None:
            if name in self.cfg["ext_in"]:
                kind = "ExternalInput"
            elif name in self.cfg["ext_out"]:
                kind = "ExternalOutput"
            else:
                kind = "Internal"
        h = self.nc.dram_tensor(name, list(shape), dt, kind=kind)
        d = DT(h, name)
        self.dts[name] = d
        return d

    def sb(self, name, shape, dt):
        self.rr["uid"] = self.rr.get("uid", 0) + 1
        name = "%s_u%d" % (name, self.rr["uid"])
        t = self.es.enter_context(self.nc.sbuf_tensor(name, list(shape), dt))
        return Tl(t, name)

    def chan(self, name):
        if name not in self.chans:
            self.chans[name] = Buf(name)
        return self.chans[name]

    def balloc(self):
        if not hasattr(self, "_bfree"):
            self._bfree = list(range(8))
        assert self._bfree, "out of PSUM banks"
        return self.PB[self._bfree.pop(0)]

    def bfree(self, pb):
        i = self.PB.index(pb)
        assert i not in self._bfree
        self._bfree.append(i)

    def bank(self):
        i = self.rr["bank"]
        self.rr["bank"] = (i + 1) % 8
        return self.PB[i]

    def DMA(self, out_ap, in_ap, R, W, chan, q="sp", **kw):
        self.P.dma(q, out_ap, in_ap, R, W, self.chan(chan + ("_sw" if q == "pool" else "")), **kw)

    def ALLGATHER(self, send, recv, name):
        nc = self.nc
        if name not in self.chans:
            self.chans[name] = Buf(name, inc=1)
        groups = [[2 * i, 2 * i + 1] for i in range(NCORES // 2)]
        self.P._rec("pool", lambda: nc.gpsimd.collective_compute(
            "AllGather", ALU.bypass, replica_groups=groups,
            ins=[send.ap().opt()], outs=[recv.ap().opt()]), [send.b()], [recv.b()], self.chans[name])

    def MM(self, out_ap, lhsT, rhs, R, W, start=True, stop=True):
        nc = self.nc
        self.P.op("pe", lambda: nc.tensor.matmul(out_ap, lhsT=lhsT, rhs=rhs, start=start, stop=stop), R, W)

    def TR(self, out_ap, in_ap, ident, R, W):
        nc = self.nc
        self.P.op("pe", lambda: nc.tensor.transpose(out=out_ap, in_=in_ap, identity=ident), R, W)

    def ACT(self, out_ap, in_ap, func, R, W, bias=0.0, scale=1.0):
        nc = self.nc
        self.P.op("act", lambda: nc.scalar.activation(out=out_ap, in_=in_ap, func=func, bias=bias, scale=scale), R, W)

    def _ve(self, eng):
        return self.nc.vector if eng == "dve" else self.nc.gpsimd

    def TT(self, eng, out_ap, in0, in1, op, R, W):
        e = self._ve(eng)
        self.P.op(eng, lambda: e.tensor_tensor(out=out_ap, in0=in0, in1=in1, op=op), R, W)

    def TS(self, eng, out_ap, in0, s1, s2, op0, op1, R, W):
        e = self._ve(eng)
        if op1 is None:
            self.P.op(eng, lambda: e.tensor_scalar(out=out_ap, in0=in0, scalar1=s1, scalar2=None, op0=op0), R, W)
        else:
            self.P.op(eng, lambda: e.tensor_scalar(out=out_ap, in0=in0, scalar1=s1, scalar2=s2, op0=op0, op1=op1), R, W)

    def STT(self, eng, out_ap, in0, scalar, in1, op0, op1, R, W):
        e = self._ve(eng)
        self.P.op(eng, lambda: e.scalar_tensor_tensor(out=out_ap, in0=in0, scalar=scalar, in1=in1, op0=op0, op1=op1), R, W)

    def CP(self, eng, out_ap, in_ap, R, W):
        if eng == "act":
            nc = self.nc
            self.P.op("act", lambda: nc.scalar.copy(out=out_ap, in_=in_ap), R, W)
        else:
            e = self._ve(eng)
            self.P.op(eng, lambda: e.tensor_copy(out=out_ap, in_=in_ap), R, W)

    def MEMSET(self, eng, ap, val, W):
        e = self._ve(eng)
        self.P.op(eng, lambda: e.memset(ap, val), [], W)

    def RECIP(self, out_ap, in_ap, R, W):
        nc = self.nc
        self.P.op("dve", lambda: nc.vector.reciprocal(out=out_ap, in_=in_ap), R, W)

    def ew(self):
        i = self.rr["ew"]
        self.rr["ew"] = i + 1
        return "dve" if i % 2 == 0 else "pool"


CB_ONESD, CB_B64, CB_ONES256, CB_ONES128, CB_ONES32, CB_B96, CB_P96, CB_P32, CB_TRI = (
    0, 128, 256, 384, 512, 544, 640, 736, 768)
NCB = 896
CF_ID, CF_ONES, CF_IFQ, CF_IFK, CF_OH, CF_MV = 0, 128, 256, 257, 258, 258 + 383
NCF = 258 + 383 + 383
NROLE = 34


def _t5_bucket(n):
    n = np.maximum(n, 0)
    nf = np.maximum(n, 1).astype(np.float32)
    large = 16 + (np.log(nf / np.float32(16)) / np.float32(math.log(8.0)) * np.float32(16)).astype(np.int32)
    large = np.minimum(large, 31)
    return np.where(n < 16, n, large)


def host_consts():
    cb = np.zeros((128, NCB), np.float32)
    cb[:, CB_ONESD:CB_ONESD + 128] = 1.0 / 1024
    cb[0:64, CB_B64:CB_B64 + 64] = 1.0 / 64
    cb[64:128, CB_B64 + 64:CB_B64 + 128] = 1.0 / 64
    cb[:, CB_ONES256:CB_ONES256 + 128] = 1.0 / 256
    cb[:, CB_ONES128:CB_ONES128 + 128] = 1.0 / 128
    cb[0:32, CB_ONES32:CB_ONES32 + 32] = 1.0 / 32
    cb[0:64, CB_B96:CB_B96 + 64] = 1.0 / 64
    cb[64:96, CB_B96 + 64:CB_B96 + 96] = 1.0 / 32
    for i in range(16):
        cb[80 + i, CB_P96 + 64 + i] = -1.0
        cb[64 + i, CB_P96 + 80 + i] = 1.0
        cb[16 + i, CB_P32 + i] = -1.0
        cb[i, CB_P32 + 16 + i] = 1.0
    p = np.arange(128)[:, None]
    f = np.arange(128)[None, :]
    cb[:, CB_TRI:CB_TRI + 128] = (p <= f).astype(np.float32)
    cf = np.zeros((128, NCF), np.float32)
    cf[:, CF_ID:CF_ID + 128] = np.eye(128, dtype=np.float32)
    cf[:, CF_ONES:CF_ONES + 128] = 1.0
    inv_freq = (np.float32(10000.0) ** (-np.arange(0, 32, 2, dtype=np.float32) / np.float32(32))).astype(np.float32)
    for i in range(16):
        cf[64 + i, CF_IFQ] = inv_freq[i]
        cf[80 + i, CF_IFQ] = inv_freq[i]
        cf[i, CF_IFK] = inv_freq[i]
        cf[16 + i, CF_IFK] = inv_freq[i]
    dist = np.arange(383) - 127
    valid = (dist >= 0) & (dist < 128)
    bk = _t5_bucket(dist)
    for i in range(383):
        if valid[i]:
            cf[bk[i], CF_OH + i] = 1.0
    cf[:, CF_MV:CF_MV + 383] = np.where(valid, 0.0, NEG)[None, :]
    return cb.astype(ml_dtypes.bfloat16), cf


def host_role(core):
    second = (core % 2) == 1
    r = np.zeros((128, NROLE), np.float32)
    r[:, 0] = 0.0 if second else NEG
    r[:, 1] = 1.0 if second else 0.0
    wins = [2, 4, 8, 16]
    for c in range(2):
        for half in range(2):
            w = wins[2 * c + half]
            for t in range(16):
                cnt = float(w) if second else float(min(t + 1, w))
                r[half * 64:(half + 1) * 64, 2 + c * 16 + t] = 1.0 / cnt
    return r


WEIGHT_NAMES = [("rel_bias", [32, 6]), ("attn_norm", [2, 1024]), ("w_in", [2, 1024, 1312]),
                ("swa_q_gain", [2, 64]), ("swa_k_gain", [2, 64]), ("swa_sinks", [2, 6]),
                ("pool_w", [2, 4, 64, 64]), ("pool_scale", [2, 256]), ("mla_q_a_gain", [2, 256]),
                ("mla_w_qb", [2, 256, 576]), ("mla_kv_a_gain", [2, 128]), ("mla_w_kvb", [2, 128, 768]),
                ("mla_q_nope_gain", [2, 64]), ("mla_q_rope_gain", [2, 32]), ("mla_k_nope_gain", [2, 64]),
                ("mla_k_rope_gain", [2, 32]), ("w_out", [2, 1024, 1024]), ("ffn_norm", [2, 1024]),
                ("w_gate", [2, 1024, 2816]), ("w_up", [2, 1024, 2816]), ("w_down", [2, 2816, 1024])]


def layer_tensors(k, l):
    d = {}
    d["qsT"] = k.dram("qsT_%d" % l, [384, T], BF16)
    d["ksT"] = k.dram("ksT_%d" % l, [128, 128 + T], BF16)
    d["vs"] = k.dram("vs_%d" % l, [128 + T, 130], BF16)
    d["uT"] = k.dram("uT_%d" % l, [256, 16 + T], F32)
    d["qmT"] = k.dram("qmT_%d" % l, [6, 96, T], BF16)
    d["ksend"] = [k.dram("ksend%d_%d" % (i, l), [416, T // 2], BF16) for i in range(2)]
    d["krecv"] = [k.dram("krecv%d_%d" % (i, l), [832, T // 2], BF16) for i in range(2)]
    d["vsend"] = [k.dram("vsend%d_%d" % (i, l), [T // 2, 390], BF16) for i in range(2)]
    d["vrecv"] = [k.dram("vrecv%d_%d" % (i, l), [T, 390], BF16) for i in range(2)]
    d["ssend"] = k.dram("ssend_%d" % l, [128, 258], BF16)
    d["srecv"] = k.dram("srecv_%d" % l, [256, 258], BF16)
    d["usend"] = k.dram("usend_%d" % l, [256, 16], F32)
    d["urecv"] = k.dram("urecv_%d" % l, [512, 16], F32)
    return d


def col_gain(k, dst_tile, col, src_dt, l, n, reps, chan):
    for r in range(reps):
        src = src_dt.raw(l * n, [[1, n], [1, 1]])
        k.DMA(dst_tile[r * n:(r + 1) * n, col:col + 1], src, [src_dt.b()], [dst_tile.b], chan)


def rsqrt_from_ms(k, ms_ap, M, N, R, tmp, out_ap, Wt):
    k.ACT(tmp[0:M, 0:N], ms_ap, AF.Ln, R, [tmp.b], bias=k.epsb[0:M, 0:1], scale=1.0)
    k.ACT(out_ap, tmp[0:M, 0:N], AF.Exp, [tmp.b], Wt, scale=-0.5)


def phase_setup(k):
    nc = k.nc
    W = k.W
    k.DMA(k.cb[:, :], k.in_cb.ap(), [k.in_cb.b()], [k.cb.b], "const")
    k.DMA(k.cf[:, :], k.in_cf.ap(), [k.in_cf.b()], [k.cf.b], "const")
    k.DMA(k.role[:, :], k.in_role.ap(), [k.in_role.b()], [k.role.b], "const")
    k.MEMSET("pool", k.epsb[:, :], EPS, [k.epsb.b])
    with contextlib.ExitStack() as es:
        k.es = es
        csq = k.dram("csq", [2, 96, T], F32)
        csk = k.dram("csk", [2, 32, T], F32)
        posi = k.sb("posi", [96, T], I32)
        ang = k.sb("ang", [96, T], F32)
        kf = k.sb("kf", [96, T], F32)
        r1 = k.sb("r1", [96, T], F32)
        r2 = k.sb("r2", [96, T], F32)
        src = k.in_pos.raw(0, [[0, 96], [1, T]])
        k.DMA(posi[:, :], src, [k.in_pos.b()], [posi.b], "ld0")
        for (npart, ifcol, dst) in ((96, CF_IFQ, csq), (32, CF_IFK, csk)):
            sl = slice(0, npart)
            k.CP("dve", ang[sl, :], posi[sl, :], [posi.b], [ang.b])
            k.TS("dve", ang[sl, :], ang[sl, :], k.cf[sl, ifcol:ifcol + 1], None, ALU.mult, None, [ang.b, k.cf.b], [ang.b])
            k.TS("dve", kf[sl, :], ang[sl, :], 1.0 / TWO_PI, MAGIC, ALU.mult, ALU.add, [ang.b], [kf.b])
            k.TS("dve", kf[sl, :], kf[sl, :], MAGIC, None, ALU.subtract, None, [kf.b], [kf.b])
            k.STT("dve", r1[sl, :], kf[sl, :], -CW1, ang[sl, :], ALU.mult, ALU.add, [kf.b, ang.b], [r1.b])
            k.STT("dve", r1[sl, :], kf[sl, :], -CW2, r1[sl, :], ALU.mult, ALU.add, [kf.b, r1.b], [r1.b])
            k.TS("dve", r2[sl, :], r1[sl, :], 3.1415925, -3.1415925, ALU.min, ALU.max, [r1.b], [r2.b])
            k.ACT(r2[sl, :], r2[sl, :], AF.Sin, [r2.b], [r2.b])
            k.DMA(dst.ap()[1, :, :], r2[sl, :], [r2.b], [dst.b()], "st0")
            k.TS("dve", r1[sl, :], r1[sl, :], math.pi / 2, None, ALU.add, None, [r1.b], [r1.b])
            k.TS("dve", kf[sl, :], r1[sl, :], math.pi, -TWO_PI, ALU.is_gt, ALU.mult, [r1.b], [kf.b])
            k.TT("dve", r1[sl, :], r1[sl, :], kf[sl, :], ALU.add, [r1.b, kf.b], [r1.b])
            k.TS("dve", r2[sl, :], r1[sl, :], 3.1415925, -3.1415925, ALU.min, ALU.max, [r1.b, r2.b], [r2.b])
            k.ACT(r2[sl, :], r2[sl, :], AF.Sin, [r2.b], [r2.b])
            k.DMA(dst.ap()[0, :, :], r2[sl, :], [r2.b], [dst.b()], "st0")
        tv = k.dram("tv", [6, 128, 383], F32)
        rb = k.sb("rb", [32, 6], F32)
        lh = k.sb("lh", [32, 128], F32)
        tvs = k.sb("tvs", [128, 383], F32)
        k.DMA(rb[:, :], W["rel_bias"].ap(), [W["rel_bias"].b()], [rb.b], "ld1")
        for h in range(6):
            k.TS("dve", lh[:, :], k.cf[0:32, CF_ONES:CF_ONES + 128], rb[:, h:h + 1], None, ALU.mult, None,
                 [k.cf.b, rb.b], [lh.b])
            pb = k.bank()
            k.MM(pb[:, 0:383], lh[:, :], k.cf[0:32, CF_OH:CF_OH + 383], [lh.b, k.cf.b], [pb.b])
            k.TT("dve", tvs[:, :], pb[:, 0:383], k.cf[:, CF_MV:CF_MV + 383], ALU.add, [pb.b, k.cf.b], [tvs.b])
            k.DMA(tv.ap()[h, :, :], tvs[:, :], [tvs.b], [tv.b()], "st1")
    k.P.barrier()


def load_cast_rows(k, dst_tile, dst_ap_fn, src_dt, src_ap_fn, nchunks, ncols, gain_tile, stg, stgname):
    W_ = stg[0].t.shape[1]
    for c in range(nchunks):
        for p0 in range(0, ncols, W_):
            p1 = min(ncols, p0 + W_)
            i = k.rr["cast"]
            k.rr["cast"] = i + 1
            s = stg[i % len(stg)]
            k.DMA(s[:, 0:p1 - p0], src_ap_fn(c)[:, p0:p1], [src_dt.b()], [s.b], "%s%d" % (stgname, i % len(stg)),
                  q=("sp" if i % 2 == 0 else "pool"))
            eng = ("dve", "act", "pool")[i % 3]
            out_ap = dst_ap_fn(c)[:, p0:p1]
            if gain_tile is None:
                k.CP(eng, out_ap, s[:, 0:p1 - p0], [s.b], [dst_tile.b])
            elif eng == "act":
                k.ACT(out_ap, s[:, 0:p1 - p0], AF.Copy, [s.b, gain_tile.b], [dst_tile.b], scale=gain_tile[:, c:c + 1])
            else:
                k.TS(eng, out_ap, s[:, 0:p1 - p0], gain_tile[:, c:c + 1], None, ALU.mult, None,
                     [s.b, gain_tile.b], [dst_tile.b])


def norm_group(k, ps, M, N, bmat_ap, gain_ap, out_ap, Wout, sqg, lnt, rst, extra_R=()):
    k.ACT(sqg[0:M, 0:N], ps[0:M, 0:N], AF.Square, [ps.b], [sqg.b])
    pb = k.bank()
    k.MM(pb[0:M, 0:N], bmat_ap, sqg[0:M, 0:N], [sqg.b, k.cb.b], [pb.b])
    rsqrt_from_ms(k, pb[0:M, 0:N], M, N, [pb.b], lnt, rst[0:M, 0:N], [rst.b])
    k.STT("dve", out_ap, ps[0:M, 0:N], gain_ap, rst[0:M, 0:N], ALU.mult, ALU.mult,
          [ps.b, rst.b] + list(extra_R), Wout)


def run_batch(k, groups, N, sqg, lng, rsg):
    pbs = []
    for g in groups:
        pb = k.bank()
        g["mm"](pb)
        pbs.append(pb)
    for i, g in enumerate(groups):
        M = g["M"]
        k.ACT(sqg[i][0:M, 0:N], pbs[i][0:M, 0:N], AF.Square, [pbs[i].b], [sqg[i].b])
    keys = []
    for g in groups:
        if g["ms"] not in keys:
            keys.append(g["ms"])
    rs_of = {}
    for ki, key in enumerate(keys):
        mem = [i for i, g in enumerate(groups) if g["ms"] == key]
        M = groups[mem[0]]["M"]
        pm = k.bank()
        for n_, i in enumerate(mem):
            k.MM(pm[0:M, 0:N], groups[i]["bmat"], sqg[i][0:M, 0:N], [sqg[i].b, k.cb.b], [pm.b],
                 start=(n_ == 0), stop=(n_ == len(mem) - 1))
        rs_of[key] = (ki, pm, M)
    for key in keys:
        ki, pm, M = rs_of[key]
        k.ACT(lng[ki][0:M, 0:N], pm[0:M, 0:N], AF.Ln, [pm.b], [lng[ki].b], bias=k.epsb[0:M, 0:1], scale=1.0)
    for key in keys:
        ki, pm, M = rs_of[key]
        k.ACT(rsg[ki][0:M, 0:N], lng[ki][0:M, 0:N], AF.Exp, [lng[ki].b], [rsg[ki].b], scale=-0.5)
    for i, g in enumerate(groups):
        ki, pm, M = rs_of[g["ms"]]
        k.STT("dve", g["out"], pbs[i][0:M, 0:N], g["gain"], rsg[ki][0:M, 0:N], ALU.mult, ALU.mult,
              [pbs[i].b, rsg[ki].b] + list(g["gainR"]), g["outW"])


def phase_A(k, l, xin_mode, xT_src, xT_dst, LT):
    nc = k.nc
    W = k.W
    NT = 512
    with contextlib.ExitStack() as es:
        k.es = es
        win = k.sb("win", [128, 8, IN_DIM], BF16)
        wqb = k.sb("wqb", [128, 2, 576], BF16)
        wkn = k.sb("wkn", [128, 384], BF16)
        wv = k.sb("wv", [128, 384], BF16)
        gA = k.sb("gA", [128, 16], F32)
        gl = k.sb("gl", [128, 8], F32)
        an = W["attn_norm"]
        k.DMA(gA[:, 0:8], an.raw(l * 1024, [[1, 128], [128, 8]]), [an.b()], [gA.b], "ldg", allow_slow_non_contiguous=True)
        qa = W["mla_q_a_gain"]
        k.DMA(gA[:, 8:10], qa.raw(l * 256, [[1, 128], [128, 2]]), [qa.b()], [gA.b], "ldg", allow_slow_non_contiguous=True)
        kva = W["mla_kv_a_gain"]
        k.DMA(gA[:, 10:11], kva.raw(l * 128, [[1, 128], [1, 1]]), [kva.b()], [gA.b], "ldg")
        col_gain(k, gl, 0, W["swa_q_gain"], l, 64, 2, "ldg")
        col_gain(k, gl, 1, W["swa_k_gain"], l, 64, 2, "ldg")
        col_gain(k, gl, 2, W["mla_k_nope_gain"], l, 64, 2, "ldg")
        k.DMA(gl[0:64, 3:4], W["mla_q_nope_gain"].raw(l * 64, [[1, 64], [1, 1]]), [W["mla_q_nope_gain"].b()], [gl.b], "ldg")
        k.DMA(gl[64:96, 3:4], W["mla_q_rope_gain"].raw(l * 32, [[1, 32], [1, 1]]), [W["mla_q_rope_gain"].b()], [gl.b], "ldg")
        k.DMA(gl[0:32, 4:5], W["mla_k_rope_gain"].raw(l * 32, [[1, 32], [1, 1]]), [W["mla_k_rope_gain"].b()], [gl.b], "ldg")
        wi = W["w_in"]
        for c in range(8):
            k.DMA(win[:, c, :], wi.ap()[l, c * 128:(c + 1) * 128, :], [wi.b()], [win.b], "wld", q="pool")
        wq = W["mla_w_qb"]
        for c in range(2):
            k.DMA(wqb[:, c, :], wq.ap()[l, c * 128:(c + 1) * 128, :], [wq.b()], [wqb.b], "wld", q="pool")
        wk = W["mla_w_kvb"]
        wkv_v = wk.ap()[l, :, :].rearrange("p (h t d) -> p t h d", h=6, t=2, d=64)
        k.DMA(wkn[:, :].rearrange("p (h d) -> p h d", h=6), wkv_v[:, 0, :, :], [wk.b()], [wkn.b], "wld", q="pool")
        k.DMA(wv[:, :].rearrange("p (h d) -> p h d", h=6), wkv_v[:, 1, :, :], [wk.b()], [wv.b], "wld", q="pool")

        xtok = k.sb("xtok", [128, 4, D], F32) if xin_mode == "transpose" else None
        xT = [k.sb("xT%d" % i, [128, 8, NT], F32) for i in range(2)]
        sqb = k.sb("sqb", [128, 8, NT], BF16)
        hT = k.sb("hT", [128, 8, NT], BF16)
        lnt = k.sb("lnt", [128, NT], F32)
        rstd = k.sb("rstd", [128, NT], F32)
        sqg = [k.sb("sqg%d" % i, [128, NT], BF16) for i in range(4)]
        lng = [k.sb("lng%d" % i, [128, NT], F32) for i in range(4)]
        rsg = [k.sb("rsg%d" % i, [128, NT], F32) for i in range(4)]
        qs_st = k.sb("qs_st", [128, 3, NT], BF16)
        ks_st = k.sb("ks_st", [128, NT], BF16)
        u_st = k.sb("u_st", [128, 2, NT], F32)
        cqn = k.sb("cqn", [128, 2, NT], BF16)
        ckvn = k.sb("ckvn", [128, NT], BF16)
        krn = k.sb("krn", [32, NT], BF16)
        kr_st = k.sb("kr_st", [32, NT], BF16)
        csk_t = k.sb("csk_t", [32, 2, NT], F32)
        csq_t = k.sb("csq_t", [96, 2, NT], F32)
        t1 = [k.sb("t1_%d" % i, [96, NT], F32) for i in range(3)]
        t2 = [k.sb("t2_%d" % i, [96, NT], F32) for i in range(3)]
        qn = [k.sb("qn%d" % i, [96, NT], BF16) for i in range(3)]
        qm_st = k.sb("qm_st", [96, 6, NT], BF16)
        kn_st = k.sb("kn_st", [128, 3, NT], BF16)
        vs_st = k.sb("vs_st", [128, 4, 130], BF16)
        vm_st = k.sb("vm_st", [128, 4, 390], BF16)
        k.MEMSET("pool", vs_st[:, :, :], 1.0, [vs_st.b])
        k.MEMSET("pool", vm_st[:, :, :], 1.0, [vm_st.b])
        csq = k.dts["csq"]
        csk = k.dts["csk"]
        xin = k.in_x
        B64 = k.cb[:, CB_B64:CB_B64 + 128]

        for j in range(T // NT):
            c0 = j * NT
            xt = xT[j % 2]
            hx = j // 4
            ch = c0 - hx * (T // 2)
            if xin_mode == "transpose":
                for tb in range(4):
                    r0 = c0 + tb * 128
                    k.DMA(xtok[:, tb, :], xin.ap()[r0:r0 + 128, :], [xin.b()], [xtok.b], "xtok")
                for kc in range(8):
                    pb = k.bank()
                    for tb in range(4):
                        k.TR(pb[:, tb * 128:(tb + 1) * 128], xtok[:, tb, kc * 128:(kc + 1) * 128],
                             k.cf[:, CF_ID:CF_ID + 128], [xtok.b, k.cf.b], [pb.b])
                    k.CP("act" if kc % 2 == 0 else "dve", xt[:, kc, :], pb[:, :], [pb.b], [xt.b])
                k.DMA(xT_dst.ap()[:, :, c0:c0 + NT].rearrange("k p t -> p k t"), xt[:, :, :], [xt.b],
                      [xT_dst.b(j)], "xTst%d" % (j % 2))
            else:
                k.DMA(xt[:, :, :], xT_src.ap()[:, :, c0:c0 + NT].rearrange("k p t -> p k t"),
                      [xT_src.b(j)], [xt.b], "xTld%d" % (j % 2))
            k.DMA(csk_t[:, :, :], csk.ap()[:, :, c0:c0 + NT].rearrange("a p t -> p a t"), [csk.b()], [csk_t.b], "cskld")
            k.DMA(csq_t[:, :, :], csq.ap()[:, :, c0:c0 + NT].rearrange("a p t -> p a t"), [csq.b()], [csq_t.b], "csqld")
            for kc in range(8):
                if kc % 2 == 0:
                    k.ACT(sqb[:, kc, :], xt[:, kc, :], AF.Square, [xt.b], [sqb.b])
                else:
                    k.TT("pool", sqb[:, kc, :], xt[:, kc, :], xt[:, kc, :], ALU.mult, [xt.b], [sqb.b])
            pb = k.bank()
            for kc in range(8):
                k.MM(pb[:, 0:NT], k.cb[:, CB_ONESD:CB_ONESD + 128], sqb[:, kc, :], [k.cb.b, sqb.b], [pb.b],
                     start=(kc == 0), stop=(kc == 7))
            rsqrt_from_ms(k, pb[:, 0:NT], 128, NT, [pb.b], lnt, rstd[:, :], [rstd.b])
            for kc in range(8):
                k.STT("dve", hT[:, kc, :], xt[:, kc, :], gA[:, kc:kc + 1], rstd[:, :], ALU.mult, ALU.mult,
                      [xt.b, rstd.b, gA.b], [hT.b])

            def proj_fn(col0, M):
                def f(pbx):
                    for kc in range(8):
                        k.MM(pbx[0:M, 0:NT], win[:, kc, col0:col0 + M], hT[:, kc, :], [win.b, hT.b], [pbx.b],
                             start=(kc == 0), stop=(kc == 7))
                return f

            run_batch(k, [dict(mm=proj_fn(c * 128, 128), M=128, bmat=B64, gain=gl[:, 0:1], gainR=[gl.b],
                               out=qs_st[:, c, :], outW=[qs_st.b], ms="q%d" % c) for c in range(3)],
                      NT, sqg, lng, rsg)
            k.DMA(LT["qsT"].ap()[:, c0:c0 + NT].rearrange("(c p) t -> p c t", p=128), qs_st[:, :, :], [qs_st.b],
                  [LT["qsT"].b()], "qsst")
            pv = k.bank()
            for tb in range(4):
                for kc in range(8):
                    k.MM(pv[:, tb * 128:(tb + 1) * 128], hT[:, kc, tb * 128:(tb + 1) * 128], win[:, kc, 512:640],
                         [win.b, hT.b], [pv.b], start=(kc == 0), stop=(kc == 7))
            k.CP("act", vs_st[:, :, :].rearrange("p t (h e) -> p t h e", h=2)[:, :, :, 0:64],
                 pv[:, :].rearrange("p (t h d) -> p t h d", t=4, h=2), [pv.b], [vs_st.b])
            k.DMA(LT["vs"].ap()[128 + c0:128 + c0 + NT, :].rearrange("(t p) e -> p t e", p=128), vs_st[:, :, :],
                  [vs_st.b], [LT["vs"].b()], "vsst")
            run_batch(k, [dict(mm=proj_fn(384, 128), M=128, bmat=B64, gain=gl[:, 1:2], gainR=[gl.b],
                               out=ks_st[:, :], outW=[ks_st.b], ms="k"),
                          dict(mm=proj_fn(896, 128), M=128, bmat=k.cb[:, CB_ONES256:CB_ONES256 + 128],
                               gain=gA[:, 8:9], gainR=[gA.b], out=cqn[:, 0, :], outW=[cqn.b], ms="cq"),
                          dict(mm=proj_fn(1024, 128), M=128, bmat=k.cb[:, CB_ONES256:CB_ONES256 + 128],
                               gain=gA[:, 9:10], gainR=[gA.b], out=cqn[:, 1, :], outW=[cqn.b], ms="cq")],
                      NT, sqg, lng, rsg)
            k.DMA(LT["ksT"].ap()[:, 128 + c0:128 + c0 + NT], ks_st[:, :], [ks_st.b], [LT["ksT"].b()], "ksst")
            for c in range(2):
                pu = k.bank()
                proj_fn(640 + c * 128, 128)(pu)
                k.CP("act" if c == 0 else "dve", u_st[:, c, :], pu[:, 0:NT], [pu.b], [u_st.b])
            k.DMA(LT["uT"].ap()[:, 16 + c0:16 + c0 + NT].rearrange("(c p) t -> p c t", p=128), u_st[:, :, :],
                  [u_st.b], [LT["uT"].b()], "ust")
            run_batch(k, [dict(mm=proj_fn(1152, 128), M=128, bmat=k.cb[:, CB_ONES128:CB_ONES128 + 128],
                               gain=gA[:, 10:11], gainR=[gA.b], out=ckvn[:, :], outW=[ckvn.b], ms="ckv"),
                          dict(mm=proj_fn(1280, 32), M=32, bmat=k.cb[0:32, CB_ONES32:CB_ONES32 + 32],
                               gain=gl[0:32, 4:5], gainR=[gl.b], out=krn[:, :], outW=[krn.b], ms="kr")],
                      NT, sqg, lng, rsg)
            prot = k.bank()
            k.MM(prot[0:32, 0:NT], k.cb[0:32, CB_P32:CB_P32 + 32], krn[:, :], [k.cb.b, krn.b], [prot.b])
            k.TT("dve", t1[0][0:32, :], prot[0:32, 0:NT], csk_t[:, 1, :], ALU.mult, [prot.b, csk_t.b], [t1[0].b])
            k.TT("pool", t2[0][0:32, :], krn[:, :], csk_t[:, 0, :], ALU.mult, [krn.b, csk_t.b], [t2[0].b])
            k.TT("pool", kr_st[:, :], t1[0][0:32, :], t2[0][0:32, :], ALU.add, [t1[0].b, t2[0].b], [kr_st.b])
            k.DMA(LT["ksend"][hx].ap()[384:416, ch:ch + NT], kr_st[:, :], [kr_st.b], [LT["ksend"][hx].b()], "krst")
            for hb in range(2):
                def qmm(h):
                    def f(ph):
                        for c in range(2):
                            k.MM(ph[0:96, 0:NT], wqb[:, c, h * 96:(h + 1) * 96], cqn[:, c, :], [wqb.b, cqn.b], [ph.b],
                                 start=(c == 0), stop=(c == 1))
                    return f
                run_batch(k, [dict(mm=qmm(hb * 3 + a), M=96, bmat=k.cb[0:96, CB_B96:CB_B96 + 96],
                                   gain=gl[0:96, 3:4], gainR=[gl.b], out=qn[a][:, :], outW=[qn[a].b], ms="qm%d" % a)
                              for a in range(3)], NT, sqg, lng, rsg)
                prots = []
                for a in range(3):
                    prot = k.bank()
                    k.MM(prot[0:96, 0:NT], k.cb[0:96, CB_P96:CB_P96 + 96], qn[a][:, :], [k.cb.b, qn[a].b], [prot.b])
                    prots.append(prot)
                for a in range(3):
                    k.TT("dve", t1[a][:, :], prots[a][0:96, 0:NT], csq_t[:, 1, :], ALU.mult, [prots[a].b, csq_t.b], [t1[a].b])
                    k.TT("pool", t2[a][:, :], qn[a][:, :], csq_t[:, 0, :], ALU.mult, [qn[a].b, csq_t.b], [t2[a].b])
                for a in range(3):
                    k.TT("pool", qm_st[:, hb * 3 + a, :], t1[a][:, :], t2[a][:, :], ALU.add, [t1[a].b, t2[a].b], [qm_st.b])
            k.DMA(LT["qmT"].ap()[:, :, c0:c0 + NT].rearrange("h p t -> p h t"), qm_st[:, :, :], [qm_st.b],
                  [LT["qmT"].b()], "qmst")
            def knmm(c):
                def f(pn):
                    k.MM(pn[:, 0:NT], wkn[:, c * 128:(c + 1) * 128], ckvn[:, :], [wkn.b, ckvn.b], [pn.b])
                return f
            run_batch(k, [dict(mm=knmm(c), M=128, bmat=B64, gain=gl[:, 2:3], gainR=[gl.b],
                               out=kn_st[:, c, :], outW=[kn_st.b], ms="kn%d" % c) for c in range(3)],
                      NT, sqg, lng, rsg)
            k.DMA(LT["ksend"][hx].ap()[0:384, ch:ch + NT].rearrange("(c p) t -> p c t", p=128), kn_st[:, :, :],
                  [kn_st.b], [LT["ksend"][hx].b()], "knst")
            for tb in range(4):
                pvm = k.bank()
                k.MM(pvm[:, 0:384], ckvn[:, tb * 128:(tb + 1) * 128], wv[:, :], [ckvn.b, wv.b], [pvm.b])
                k.CP("act" if tb % 2 == 0 else "dve",
                     vm_st[:, tb, :].rearrange("p (h e) -> p h e", h=6)[:, :, 0:64],
                     pvm[:, 0:384].rearrange("p (h d) -> p h d", h=6), [pvm.b], [vm_st.b])
            k.DMA(LT["vsend"][hx].ap()[ch:ch + NT, :].rearrange("(t p) e -> p t e", p=128), vm_st[:, :, :],
                  [vm_st.b], [LT["vsend"][hx].b()], "vmst")
        k.DMA(LT["ssend"].ap()[:, 0:128], LT["ksT"].ap()[:, T:T + 128], [LT["ksT"].b()], [LT["ssend"].b()], "xcp")
        k.DMA(LT["ssend"].ap()[:, 128:258], LT["vs"].ap()[T:T + 128, :], [LT["vs"].b()], [LT["ssend"].b()], "xcp")
        k.DMA(LT["usend"].ap()[:, :], LT["uT"].ap()[:, T:T + 16], [LT["uT"].b()], [LT["usend"].b()], "xcp")
    k.P.barrier()


class Item:
    def __init__(self, name, p1=None, p2=None, p3=None, needs=()):
        self.name = name
        self.st = [p1, p2, p3]
        self.needs = list(needs)


def run_pipeline(items):
    pos = {it.name: i for i, it in enumerate(items)}
    for i, it in enumerate(items):
        for n_ in it.needs:
            assert pos[n_] <= i - 2, (it.name, n_, pos[n_], i)
    n = len(items)
    for t in range(n + 2):
        if 0 <= t - 2 < n and items[t - 2].st[2] is not None:
            items[t - 2].st[2]()
        if t < n and items[t].st[0] is not None:
            items[t].st[0]()
        if 0 <= t - 1 < n and items[t - 1].st[1] is not None:
            items[t - 1].st[1]()


def phase_A2(k, l, xin_mode, xT_src, xT_dst, LT):
    W = k.W
    NT = 512
    with contextlib.ExitStack() as es:
        k.es = es
        win = k.sb("win", [128, 8, IN_DIM], BF16)
        wqb = k.sb("wqb", [128, 2, 576], BF16)
        wkn = k.sb("wkn", [128, 384], BF16)
        wv = k.sb("wv", [128, 384], BF16)
        gA = k.sb("gA", [128, 16], F32)
        gl = k.sb("gl", [128, 8], F32)
        an = W["attn_norm"]
        k.DMA(gA[:, 0:8], an.raw(l * 1024, [[1, 128], [128, 8]]), [an.b()], [gA.b], "ldg", allow_slow_non_contiguous=True)
        qa = W["mla_q_a_gain"]
        k.DMA(gA[:, 8:10], qa.raw(l * 256, [[1, 128], [128, 2]]), [qa.b()], [gA.b], "ldg", allow_slow_non_contiguous=True)
        kva = W["mla_kv_a_gain"]
        k.DMA(gA[:, 10:11], kva.raw(l * 128, [[1, 128], [1, 1]]), [kva.b()], [gA.b], "ldg")
        col_gain(k, gl, 0, W["swa_q_gain"], l, 64, 2, "ldg")
        col_gain(k, gl, 1, W["swa_k_gain"], l, 64, 2, "ldg")
        col_gain(k, gl, 2, W["mla_k_nope_gain"], l, 64, 2, "ldg")
        k.DMA(gl[0:64, 3:4], W["mla_q_nope_gain"].raw(l * 64, [[1, 64], [1, 1]]), [W["mla_q_nope_gain"].b()], [gl.b], "ldg")
        k.DMA(gl[64:96, 3:4], W["mla_q_rope_gain"].raw(l * 32, [[1, 32], [1, 1]]), [W["mla_q_rope_gain"].b()], [gl.b], "ldg")
        k.DMA(gl[0:32, 4:5], W["mla_k_rope_gain"].raw(l * 32, [[1, 32], [1, 1]]), [W["mla_k_rope_gain"].b()], [gl.b], "ldg")
        wi = W["w_in"]
        for c in range(8):
            k.DMA(win[:, c, :], wi.ap()[l, c * 128:(c + 1) * 128, :], [wi.b()], [win.b], "wld", q="pool")
        wq = W["mla_w_qb"]
        for c in range(2):
            k.DMA(wqb[:, c, :], wq.ap()[l, c * 128:(c + 1) * 128, :], [wq.b()], [wqb.b], "wld", q="pool")
        wk = W["mla_w_kvb"]
        wkv_v = wk.ap()[l, :, :].rearrange("p (h t d) -> p t h d", h=6, t=2, d=64)
        k.DMA(wkn[:, :].rearrange("p (h d) -> p h d", h=6), wkv_v[:, 0, :, :], [wk.b()], [wkn.b], "wld", q="pool")
        k.DMA(wv[:, :].rearrange("p (h d) -> p h d", h=6), wkv_v[:, 1, :, :], [wk.b()], [wv.b], "wld", q="pool")

        xtok = [k.sb("xtok%d" % i, [128, 4, D], F32) for i in range(2)] if xin_mode == "transpose" else None
        xT = [k.sb("xT%d" % i, [128, 8, NT], F32) for i in range(1 if xin_mode == "transpose" else 2)]
        sqb = [k.sb("sqb%d" % i, [128, 8, NT], BF16) for i in range(1)]
        hT = [k.sb("hT%d" % i, [128, 8, NT], BF16) for i in range(2)]
        lnt = k.sb("lnt", [128, NT], F32)
        rstd = k.sb("rstd", [128, NT], F32)
        sqg = [k.sb("sqg%d" % i, [128, NT], BF16) for i in range(4)]
        lng = [k.sb("lng%d" % i, [128, NT], F32) for i in range(4)]
        rsg = [k.sb("rsg%d" % i, [128, NT], F32) for i in range(4)]
        qs_st = k.sb("qs_st", [128, 3, NT], BF16)
        ks_st = k.sb("ks_st", [128, NT], BF16)
        u_st = k.sb("u_st", [128, 2, NT], F32)
        cqn = [k.sb("cqn%d" % i, [128, 2, NT], BF16) for i in range(2)]
        ckvn = [k.sb("ckvn%d" % i, [128, NT], BF16) for i in range(2)]
        krn = [k.sb("krn%d" % i, [32, NT], BF16) for i in range(2)]
        kr_st = k.sb("kr_st", [32, NT], BF16)
        csk_t = [k.sb("csk_t%d" % i, [32, 2, NT], F32) for i in range(2)]
        csq_t = [k.sb("csq_t%d" % i, [96, 2, NT], F32) for i in range(2)]
        t1 = [k.sb("t1_%d" % i, [96, NT], F32) for i in range(4)]
        t2 = [k.sb("t2_%d" % i, [96, NT], F32) for i in range(4)]
        qn = [k.sb("qn%d" % i, [96, NT], BF16) for i in range(6)]
        qm_st = k.sb("qm_st", [96, 6, NT], BF16)
        kn_st = k.sb("kn_st", [128, 3, NT], BF16)
        vs_st = k.sb("vs_st", [128, 4, 130], BF16)
        vm_st = k.sb("vm_st", [128, 4, 390], BF16)
        k.MEMSET("pool", vs_st[:, :, :], 1.0, [vs_st.b])
        k.MEMSET("pool", vm_st[:, :, :], 1.0, [vm_st.b])
        csq = k.dts["csq"]
        csk = k.dts["csk"]
        xin = k.in_x
        B64 = k.cb[:, CB_B64:CB_B64 + 128]
        IDN = k.cf[:, CF_ID:CF_ID + 128]
        items = []
        fronts = []
        mains = []
        nrm_ctr = [0]
        krt = k.sb("krt", [32, NT], F32)

        def norm_item(name, groups, needs, after_p3=None):
            par = nrm_ctr[0] % 2
            nrm_ctr[0] += 1
            st = {}

            def p1():
                st["pbs"] = []
                for g in groups:
                    pb = k.balloc()
                    g["mm"](pb)
                    st["pbs"].append(pb)

            def p2():
                for i, g in enumerate(groups):
                    M = g["M"]
                    sq = sqg[par * 2 + i]
                    k.ACT(sq[0:M, 0:NT], st["pbs"][i][0:M, 0:NT], AF.Square, [st["pbs"][i].b], [sq.b])
                keys = []
                for g in groups:
                    if g["ms"] not in keys:
                        keys.append(g["ms"])
                st["rs"] = {}
                for ki, key in enumerate(keys):
                    mem = [i for i, g in enumerate(groups) if g["ms"] == key]
                    M = groups[mem[0]]["M"]
                    pm = k.balloc()
                    for n_, i in enumerate(mem):
                        sq = sqg[par * 2 + i]
                        k.MM(pm[0:M, 0:NT], groups[i]["bmat"], sq[0:M, 0:NT], [sq.b, k.cb.b], [pm.b],
                             start=(n_ == 0), stop=(n_ == len(mem) - 1))
                    st["rs"][key] = (par * 2 + ki, pm, M)

            def p3():
                for key, (si, pm, M) in st["rs"].items():
                    k.ACT(lng[si][0:M, 0:NT], pm[0:M, 0:NT], AF.Ln, [pm.b], [lng[si].b], bias=k.epsb[0:M, 0:1], scale=1.0)
                for key, (si, pm, M) in st["rs"].items():
                    k.ACT(rsg[si][0:M, 0:NT], lng[si][0:M, 0:NT], AF.Exp, [lng[si].b], [rsg[si].b], scale=-0.5)
                for i, g in enumerate(groups):
                    si, pm, M = st["rs"][g["ms"]]
                    k.STT("dve", g["out"], st["pbs"][i][0:M, 0:NT], g["gain"], rsg[si][0:M, 0:NT], ALU.mult, ALU.mult,
                          [st["pbs"][i].b, rsg[si].b] + list(g["gainR"]), g["outW"])
                for pb in st["pbs"]:
                    k.bfree(pb)
                for key, (si, pm, M) in st["rs"].items():
                    k.bfree(pm)
                if after_p3 is not None:
                    after_p3()
            items.append(Item(name, p1, p2, p3, needs))

        def load_x(jj):
            cc = jj * NT
            if xin_mode == "transpose":
                xk_ = xtok[jj % 2]
                for tb in range(4):
                    r0 = cc + tb * 128
                    k.DMA(xk_[:, tb, :], xin.ap()[r0:r0 + 128, :], [xin.b()], [xk_.b], "xtok%d" % (jj % 2))
            else:
                xx = xT[jj % 2]
                k.DMA(xx[:, :, :], xT_src.ap()[:, :, cc:cc + NT].rearrange("k p t -> p k t"),
                      [xT_src.b(jj)], [xx.b], "xTld%d" % (jj % 2))

        load_x(0)
        for j in range(T // NT):
            items = []
            c0 = j * NT
            jp = j % 2
            xt = xT[0] if xin_mode == "transpose" else xT[jp]
            h_ = hT[jp]
            sq_ = sqb[0]
            hx = j // 4
            ch = c0 - hx * (T // 2)
            T_ = "t%d_" % j

            if xin_mode == "transpose":
                xk = xtok[jp]
                for pair in range(4):
                    st = {}

                    def p1(pair=pair, st=st, xk=xk, c0=c0, j=j):
                        if pair == 0 and j + 1 < T // NT:
                            load_x(j + 1)
                        st["pb"] = []
                        for kc in (2 * pair, 2 * pair + 1):
                            pb = k.balloc()
                            for tb in range(4):
                                k.TR(pb[:, tb * 128:(tb + 1) * 128], xk[:, tb, kc * 128:(kc + 1) * 128], IDN,
                                     [xk.b, k.cf.b], [pb.b])
                            st["pb"].append(pb)

                    def p2(pair=pair, st=st, xt=xt, sq_=sq_, j=j, c0=c0):
                        for n_, kc in enumerate((2 * pair, 2 * pair + 1)):
                            k.CP("act" if n_ == 0 else "dve", xt[:, kc, :], st["pb"][n_][:, :], [st["pb"][n_].b], [xt.b])
                            k.bfree(st["pb"][n_])
                        for n_, kc in enumerate((2 * pair, 2 * pair + 1)):
                            if n_ == 0:
                                k.ACT(sq_[:, kc, :], xt[:, kc, :], AF.Square, [xt.b], [sq_.b])
                            else:
                                k.TT("pool", sq_[:, kc, :], xt[:, kc, :], xt[:, kc, :], ALU.mult, [xt.b], [sq_.b])
                        if pair == 3:
                            k.DMA(xT_dst.ap()[:, :, c0:c0 + NT].rearrange("k p t -> p k t"), xt[:, :, :], [xt.b],
                                  [xT_dst.b(j)], "xTst0")
                    items.append(Item(T_ + "F%d" % pair, p1, p2, None))
            else:
                for pair in range(4):
                    def p1(pair=pair, xt=xt, j=j, c0=c0):
                        if pair == 0 and j + 1 < T // NT:
                            load_x(j + 1)

                    def p2(pair=pair, xt=xt, sq_=sq_):
                        for n_, kc in enumerate((2 * pair, 2 * pair + 1)):
                            if n_ == 0:
                                k.ACT(sq_[:, kc, :], xt[:, kc, :], AF.Square, [xt.b], [sq_.b])
                            else:
                                k.TT("pool", sq_[:, kc, :], xt[:, kc, :], xt[:, kc, :], ALU.mult, [xt.b], [sq_.b])
                    items.append(Item(T_ + "F%d" % pair, p1, p2, None))
            st5 = {}

            def f5p1(st5=st5, sq_=sq_, jp=jp, c0=c0):
                k.DMA(csk_t[jp][:, :, :], csk.ap()[:, :, c0:c0 + NT].rearrange("a p t -> p a t"), [csk.b()],
                      [csk_t[jp].b], "cskld%d" % jp)
                k.DMA(csq_t[jp][:, :, :], csq.ap()[:, :, c0:c0 + NT].rearrange("a p t -> p a t"), [csq.b()],
                      [csq_t[jp].b], "csqld%d" % jp)
                pb = k.balloc()
                for kc in range(8):
                    k.MM(pb[:, 0:NT], k.cb[:, CB_ONESD:CB_ONESD + 128], sq_[:, kc, :], [k.cb.b, sq_.b], [pb.b],
                         start=(kc == 0), stop=(kc == 7))
                st5["pb"] = pb

            def f5p2(st5=st5):
                rsqrt_from_ms(k, st5["pb"][:, 0:NT], 128, NT, [st5["pb"].b], lnt, rstd[:, :], [rstd.b])
                k.bfree(st5["pb"])

            def f5p3(xt=xt, h_=h_):
                for kc in range(8):
                    k.STT("dve", h_[:, kc, :], xt[:, kc, :], gA[:, kc:kc + 1], rstd[:, :], ALU.mult, ALU.mult,
                          [xt.b, rstd.b, gA.b], [h_.b])
            items.append(Item(T_ + "F5", f5p1, f5p2, f5p3, needs=[T_ + "F%d" % p for p in range(4)]))
            fronts.append(items)
            items = []

            def proj_fn(col0, M, h_=h_):
                def f(pbx):
                    for kc in range(8):
                        k.MM(pbx[0:M, 0:NT], win[:, kc, col0:col0 + M], h_[:, kc, :], [win.b, h_.b], [pbx.b],
                             start=(kc == 0), stop=(kc == 7))
                return f

            cq_ = cqn[jp]
            ckv_ = ckvn[jp]
            kr_ = krn[jp]
            O256 = k.cb[:, CB_ONES256:CB_ONES256 + 128]
            norm_item(T_ + "cq", [dict(mm=proj_fn(896, 128), M=128, bmat=O256, gain=gA[:, 8:9], gainR=[gA.b],
                                       out=cq_[:, 0, :], outW=[cq_.b], ms="cq"),
                                  dict(mm=proj_fn(1024, 128), M=128, bmat=O256, gain=gA[:, 9:10], gainR=[gA.b],
                                       out=cq_[:, 1, :], outW=[cq_.b], ms="cq")], [T_ + "F5"])
            norm_item(T_ + "ckvkr", [dict(mm=proj_fn(1152, 128), M=128, bmat=k.cb[:, CB_ONES128:CB_ONES128 + 128],
                                          gain=gA[:, 10:11], gainR=[gA.b], out=ckv_[:, :], outW=[ckv_.b], ms="ckv"),
                                     dict(mm=proj_fn(1280, 32), M=32, bmat=k.cb[0:32, CB_ONES32:CB_ONES32 + 32],
                                          gain=gl[0:32, 4:5], gainR=[gl.b], out=kr_[:, :], outW=[kr_.b], ms="kr")],
                      [T_ + "F5"])
            norm_item(T_ + "q01", [dict(mm=proj_fn(c * 128, 128), M=128, bmat=B64, gain=gl[:, 0:1], gainR=[gl.b],
                                        out=qs_st[:, c, :], outW=[qs_st.b], ms="q%d" % c) for c in range(2)],
                      [T_ + "F5"])

            def st_qk(c0=c0):
                k.DMA(LT["qsT"].ap()[:, c0:c0 + NT].rearrange("(c p) t -> p c t", p=128), qs_st[:, :, :], [qs_st.b],
                      [LT["qsT"].b()], "qsst")
                k.DMA(LT["ksT"].ap()[:, 128 + c0:128 + c0 + NT], ks_st[:, :], [ks_st.b], [LT["ksT"].b()], "ksst")
            norm_item(T_ + "q2k", [dict(mm=proj_fn(256, 128), M=128, bmat=B64, gain=gl[:, 0:1], gainR=[gl.b],
                                        out=qs_st[:, 2, :], outW=[qs_st.b], ms="q2"),
                                   dict(mm=proj_fn(384, 128), M=128, bmat=B64, gain=gl[:, 1:2], gainR=[gl.b],
                                        out=ks_st[:, :], outW=[ks_st.b], ms="k")], [T_ + "F5"], after_p3=st_qk)
            stu = {}

            def up1(stu=stu, proj_fn=proj_fn):
                stu["pb"] = []
                for c in range(2):
                    pu = k.balloc()
                    proj_fn(640 + c * 128, 128)(pu)
                    stu["pb"].append(pu)

            def up2(stu=stu, c0=c0):
                for c in range(2):
                    k.CP("act" if c == 0 else "dve", u_st[:, c, :], stu["pb"][c][:, 0:NT], [stu["pb"][c].b], [u_st.b])
                    k.bfree(stu["pb"][c])
                k.DMA(LT["uT"].ap()[:, 16 + c0:16 + c0 + NT].rearrange("(c p) t -> p c t", p=128), u_st[:, :, :],
                      [u_st.b], [LT["uT"].b()], "ust")
            items.append(Item(T_ + "u", up1, up2, None, needs=[T_ + "F5"]))
            for hb in range(3):
                def qmm(h, cq_=cq_):
                    def f(ph):
                        for c in range(2):
                            k.MM(ph[0:96, 0:NT], wqb[:, c, h * 96:(h + 1) * 96], cq_[:, c, :], [wqb.b, cq_.b], [ph.b],
                                 start=(c == 0), stop=(c == 1))
                    return f
                norm_item(T_ + "qm%d" % hb,
                          [dict(mm=qmm(hb * 2 + a), M=96, bmat=k.cb[0:96, CB_B96:CB_B96 + 96], gain=gl[0:96, 3:4],
                                gainR=[gl.b], out=qn[hb * 2 + a][:, :], outW=[qn[hb * 2 + a].b], ms="qm%d" % a)
                           for a in range(2)], [T_ + "cq"])
            def knmm(c, ckv_=ckv_):
                def f(pn):
                    k.MM(pn[:, 0:NT], wkn[:, c * 128:(c + 1) * 128], ckv_[:, :], [wkn.b, ckv_.b], [pn.b])
                return f
            norm_item(T_ + "kn01", [dict(mm=knmm(c), M=128, bmat=B64, gain=gl[:, 2:3], gainR=[gl.b],
                                         out=kn_st[:, c, :], outW=[kn_st.b], ms="kn%d" % c) for c in range(2)],
                      [T_ + "ckvkr"])

            def st_kn(hx=hx, ch=ch):
                k.DMA(LT["ksend"][hx].ap()[0:384, ch:ch + NT].rearrange("(c p) t -> p c t", p=128), kn_st[:, :, :],
                      [kn_st.b], [LT["ksend"][hx].b()], "knst")
            norm_item(T_ + "kn2", [dict(mm=knmm(2), M=128, bmat=B64, gain=gl[:, 2:3], gainR=[gl.b],
                                        out=kn_st[:, 2, :], outW=[kn_st.b], ms="kn2")], [T_ + "ckvkr"], after_p3=st_kn)
            for ri in range(3):
                strp = {}

                def rp1(ri=ri, strp=strp, kr_=kr_):
                    strp["pb"] = []
                    for a in range(2):
                        prot = k.balloc()
                        q_ = qn[ri * 2 + a]
                        k.MM(prot[0:96, 0:NT], k.cb[0:96, CB_P96:CB_P96 + 96], q_[:, :], [k.cb.b, q_.b], [prot.b])
                        strp["pb"].append(prot)
                    if ri == 0:
                        prot = k.balloc()
                        k.MM(prot[0:32, 0:NT], k.cb[0:32, CB_P32:CB_P32 + 32], kr_[:, :], [k.cb.b, kr_.b], [prot.b])
                        strp["kr"] = prot

                def rp2(ri=ri, strp=strp, kr_=kr_, jp=jp):
                    for a in range(2):
                        q_ = qn[ri * 2 + a]
                        ts = (ri % 2) * 2 + a
                        k.TT("dve", t1[ts][:, :], strp["pb"][a][0:96, 0:NT], csq_t[jp][:, 1, :], ALU.mult,
                             [strp["pb"][a].b, csq_t[jp].b], [t1[ts].b])
                        k.TT("pool", t2[ts][:, :], q_[:, :], csq_t[jp][:, 0, :], ALU.mult, [q_.b, csq_t[jp].b], [t2[ts].b])
                        k.bfree(strp["pb"][a])
                    if ri == 0:
                        k.TT("dve", krt[:, :], strp["kr"][0:32, 0:NT], csk_t[jp][:, 1, :], ALU.mult,
                             [strp["kr"].b, csk_t[jp].b], [krt.b])
                        k.bfree(strp["kr"])

                def rp3(ri=ri, kr_=kr_, jp=jp, hx=hx, ch=ch, c0=c0):
                    for a in range(2):
                        ts = (ri % 2) * 2 + a
                        k.TT("pool", qm_st[:, ri * 2 + a, :], t1[ts][:, :], t2[ts][:, :], ALU.add, [t1[ts].b, t2[ts].b],
                             [qm_st.b])
                    if ri == 0:
                        k.STT("dve", kr_st[:, :], kr_[:, :], 1.0, csk_t[jp][:, 0, :], ALU.mult, ALU.mult,
                              [kr_.b, csk_t[jp].b], [kr_st.b])
                        k.TT("dve", kr_st[:, :], kr_st[:, :], krt[:, :], ALU.add, [kr_st.b, krt.b], [kr_st.b])
                        k.DMA(LT["ksend"][hx].ap()[384:416, ch:ch + NT], kr_st[:, :], [kr_st.b], [LT["ksend"][hx].b()],
                              "krst")
                    if ri == 2:
                        k.DMA(LT["qmT"].ap()[:, :, c0:c0 + NT].rearrange("h p t -> p h t"), qm_st[:, :, :], [qm_st.b],
                              [LT["qmT"].b()], "qmst")
                needs = [T_ + "qm%d" % ri] + ([T_ + "ckvkr"] if ri == 0 else [])
                items.append(Item(T_ + "rope%d" % ri, rp1, rp2, rp3, needs=needs))
            stv = {}

            def vp1(stv=stv, ckv_=ckv_):
                stv["pb"] = []
                for tb in range(4):
                    pvm = k.balloc()
                    k.MM(pvm[:, 0:384], ckv_[:, tb * 128:(tb + 1) * 128], wv[:, :], [ckv_.b, wv.b], [pvm.b])
                    stv["pb"].append(pvm)

            def vp2(stv=stv, hx=hx, ch=ch):
                for tb in range(4):
                    k.CP("act" if tb % 2 == 0 else "dve",
                         vm_st[:, tb, :].rearrange("p (h e) -> p h e", h=6)[:, :, 0:64],
                         stv["pb"][tb][:, 0:384].rearrange("p (h d) -> p h d", h=6), [stv["pb"][tb].b], [vm_st.b])
                    k.bfree(stv["pb"][tb])
                k.DMA(LT["vsend"][hx].ap()[ch:ch + NT, :].rearrange("(t p) e -> p t e", p=128), vm_st[:, :, :],
                      [vm_st.b], [LT["vsend"][hx].b()], "vmst")
            items.append(Item(T_ + "vm", vp1, vp2, None, needs=[T_ + "ckvkr"]))
            stw = {}

            def wp1(stw=stw, h_=h_):
                pv = k.balloc()
                for tb in range(4):
                    for kc in range(8):
                        k.MM(pv[:, tb * 128:(tb + 1) * 128], h_[:, kc, tb * 128:(tb + 1) * 128], win[:, kc, 512:640],
                             [win.b, h_.b], [pv.b], start=(kc == 0), stop=(kc == 7))
                stw["pb"] = pv

            def wp2(stw=stw, c0=c0):
                pv = stw["pb"]
                k.CP("act", vs_st[:, :, :].rearrange("p t (h e) -> p t h e", h=2)[:, :, :, 0:64],
                     pv[:, :].rearrange("p (t h d) -> p t h d", t=4, h=2), [pv.b], [vs_st.b])
                k.bfree(pv)
                k.DMA(LT["vs"].ap()[128 + c0:128 + c0 + NT, :].rearrange("(t p) e -> p t e", p=128), vs_st[:, :, :],
                      [vs_st.b], [LT["vs"].b()], "vsst")
            items.append(Item(T_ + "vs", wp1, wp2, None, needs=[T_ + "F5"]))
            mains.append(items)
        NTL = T // NT
        seq = []
        f0 = fronts[0]
        seq += [f0[0], f0[1], f0[2], f0[3], Item("nop0a"), f0[4], Item("nop0b")]
        for j in range(NTL):
            m = {it.name.split("_", 1)[1]: it for it in mains[j]}
            f = fronts[j + 1] if j + 1 < NTL else None
            order = ["cq", "F0", "ckvkr", "F1", "q01", "F2", "q2k", "F3", "u", "qm0", "F5", "qm1", "qm2",
                     "kn01", "kn2", "rope0", "rope1", "rope2", "vm", "vs"]
            for nm in order:
                if nm[0] == "F":
                    if f is not None:
                        seq.append(f[{"F0": 0, "F1": 1, "F2": 2, "F3": 3, "F5": 4}[nm]])
                else:
                    seq.append(m[nm])
        run_pipeline(seq)
        assert len(k._bfree) == 8, k._bfree
        k.DMA(LT["ssend"].ap()[:, 0:128], LT["ksT"].ap()[:, T:T + 128], [LT["ksT"].b()], [LT["ssend"].b()], "xcp")
        k.DMA(LT["ssend"].ap()[:, 128:258], LT["vs"].ap()[T:T + 128, :], [LT["vs"].b()], [LT["ssend"].b()], "xcp")
        k.DMA(LT["usend"].ap()[:, :], LT["uT"].ap()[:, T:T + 16], [LT["uT"].b()], [LT["usend"].b()], "xcp")
    k.P.barrier()


def softmax_finish_a(k, ot, esink_ap, esink_R, rrow, use_act):
    if use_act:
        if esink_ap is not None:
            k.ACT(rrow[64:65, :], ot[64:65, 0:512], AF.Ln, [ot.b] + esink_R, [rrow.b], bias=esink_ap, scale=1.0)
        else:
            k.ACT(rrow[64:65, :], ot[64:65, 0:512], AF.Ln, [ot.b], [rrow.b])
        k.ACT(rrow[64:65, :], rrow[64:65, :], AF.Exp, [rrow.b], [rrow.b], scale=-1.0)
    else:
        if esink_ap is not None:
            k.TS("dve", rrow[64:65, :], ot[64:65, 0:512], esink_ap, None, ALU.add, None, [ot.b] + esink_R, [rrow.b])
            k.RECIP(rrow[64:65, :], rrow[64:65, :], [rrow.b], [rrow.b])
        else:
            k.RECIP(rrow[64:65, :], ot[64:65, 0:512], [ot.b], [rrow.b])


def softmax_finish_b(k, ot, out_st, out_W, rrow, bcs, pb):
    k.MM(pb[0:64, 0:512], k.cf[64:65, CF_ONES:CF_ONES + 64], rrow[64:65, :], [k.cf.b, rrow.b], [pb.b])
    k.CP("act", bcs[:, :], pb[0:64, 0:512], [pb.b], [bcs.b])
    k.TT("dve", out_st, ot[0:64, 0:512], bcs[:, :], ALU.mult, [ot.b, bcs.b], out_W)


def phase_B_swa(k, l, LT, mixT):
    W = k.W
    with contextlib.ExitStack() as es:
        k.es = es
        qT = [k.sb("sqT%d" % i, [64, T], BF16) for i in range(6)]
        kT = [k.sb("skT%d" % i, [64, 128 + T], BF16) for i in range(2)]
        V = [k.sb("sV%d" % i, [128, NB + 1, 65], BF16) for i in range(2)]
        Tt = [k.sb("sTt%d" % i, [128, 2, 256], F32) for i in range(3)]
        esb = k.sb("esb", [128, 6], F32)
        tmp = [k.sb("stmp%d" % i, [128, 2, 256], F32) for i in range(3)]
        Pt = [k.sb("sP%d" % i, [128, 2, 256], BF16) for i in range(3)]
        rrow = [k.sb("srrow%d" % i, [65, 512], F32) for i in range(4)]
        bcs = k.sb("sbcs", [64, 512], F32)
        ost = [k.sb("sost%d" % i, [64, 512], BF16) for i in range(4)]
        sk = W["swa_sinks"]
        k.DMA(esb[:, :], sk.raw(l * 6, [[0, 128], [1, 6]]), [sk.b()], [esb.b], "ldg")
        k.ACT(esb[:, :], esb[:, :], AF.Exp, [esb.b], [esb.b])
        tv = k.dts["tv"]
        for kv in range(2):
            k.DMA(kT[kv][:, 128:], LT["ksT"].ap()[kv * 64:(kv + 1) * 64, 128:], [LT["ksT"].b()], [kT[kv].b], "skT%d" % kv)
            k.DMA(V[kv][:, 1:, :],
                  LT["vs"].ap()[128:, kv * 65:(kv + 1) * 65].rearrange("(b p) e -> p b e", p=128),
                  [LT["vs"].b()], [V[kv].b], "sV%d" % kv)
        for hq in range(6):
            k.DMA(qT[hq][:, :], LT["qsT"].ap()[hq * 64:(hq + 1) * 64, :], [LT["qsT"].b()], [qT[hq].b], "sqT%d" % hq)
            k.DMA(Tt[hq // 2][:, hq % 2, :], tv.raw(hq * 128 * 383 + 127, [[382, 128], [1, 256]]), [tv.b()],
                  [Tt[hq // 2].b], "sTt%d" % (hq // 2))
        for kv in range(2):
            k.DMA(kT[kv][:, 0:128], LT["srecv"].ap()[kv * 64:(kv + 1) * 64, 0:128], [LT["srecv"].b()], [kT[kv].b],
                  "skT%d" % kv)
            k.DMA(V[kv][:, 0, :], LT["srecv"].ap()[0:128, 128 + kv * 65:128 + (kv + 1) * 65], [LT["srecv"].b()],
                  [V[kv].b], "sV%d" % kv)
        SBK = [k.PB[0], k.PB[1], k.PB[2]]
        OT = [[k.PB[3], k.PB[4]], [k.PB[5], k.PB[6]]]
        BC = k.PB[7]
        steps = [(pi, kb) for pi in range(3) for kb in range(-1, NB)]
        LA = 2

        def geom(kb):
            qlo = max(kb, 0) * 128
            qhi = min(kb + 2, NB) * 128
            return qlo, qhi, qhi - qlo, (0 if kb >= 0 else 128)

        def S_emit(i):
            pi, kb = steps[i]
            qlo, qhi, N, tc = geom(kb)
            pb = SBK[i % 3]
            ks = kb + 1
            for e in range(2):
                hq = 2 * pi + e
                kv = hq // 3
                k.MM(pb[:, e * 256:e * 256 + N], kT[kv][:, ks * 128:(ks + 1) * 128], qT[hq][:, qlo:qhi],
                     [kT[kv].b, qT[hq].b], [pb.b])

        def finish_b(pi, qt):
            for e in range(2):
                hq = 2 * pi + e
                o = ost[e * 2 + qt % 2]
                softmax_finish_b(k, OT[e][qt % 2], o[:, :], [o.b], rrow[e * 2 + qt % 2], bcs, BC)
                k.DMA(mixT.ap()[hq * 64:(hq + 1) * 64, qt * 512:(qt + 1) * 512], o[:, :], [o.b],
                      [mixT.b("swa")], "sost%d" % (e * 2 + qt % 2))

        for i in range(min(LA, len(steps))):
            S_emit(i)
        pend = None
        for i, (pi, kb) in enumerate(steps):
            if i + LA < len(steps):
                S_emit(i + LA)
            ks = kb + 1
            qlo, qhi, N, tc = geom(kb)
            pb = SBK[i % 3]
            tm = tmp[i % 3]
            pt = Pt[i % 3]
            tt = Tt[pi]
            k.STT("dve", tm[:, :, 0:N], pb[:, :].rearrange("p (e n) -> p e n", e=2)[:, :, 0:N], 0.125,
                  tt[:, :, tc:tc + N], ALU.mult, ALU.add, [pb.b, tt.b], [tm.b])
            if kb == -1:
                k.ACT(pt[:, :, 0:N], tm[:, :, 0:N], AF.Exp, [tm.b, k.role.b], [pt.b], bias=k.role[:, 0:1])
            else:
                k.ACT(pt[:, :, 0:N], tm[:, :, 0:N], AF.Exp, [tm.b], [pt.b])
            for e in range(2):
                hq = 2 * pi + e
                kv = hq // 3
                if kb >= 0:
                    ot = OT[e][(kb // 4) % 2]
                    cc = (kb % 4) * 128
                    k.MM(ot[0:65, cc:cc + 128], V[kv][:, ks, :], pt[:, e, 0:128], [V[kv].b, pt.b], [ot.b],
                         start=False, stop=True)
                if kb + 1 < NB:
                    qn_ = kb + 1
                    ot2 = OT[e][(qn_ // 4) % 2]
                    cc = (qn_ % 4) * 128
                    k.MM(ot2[0:65, cc:cc + 128], V[kv][:, ks, :], pt[:, e, N - 128:N], [V[kv].b, pt.b], [ot2.b],
                         start=True, stop=False)
            if pend is not None and i >= pend[0]:
                _, ppi, pqt = pend
                pend = None
                finish_b(ppi, pqt)
            if kb >= 0 and kb % 4 == 3:
                qt = kb // 4
                for e in range(2):
                    hq = 2 * pi + e
                    softmax_finish_a(k, OT[e][qt % 2], esb[64:65, hq:hq + 1], [esb.b], rrow[e * 2 + qt % 2], True)
                pend = (i + 2, pi, qt)
        if pend is not None:
            finish_b(pend[1], pend[2])
    k.P.barrier()


def phase_B_pool(k, l, LT, mixT):
    W = k.W
    NT = 512
    with contextlib.ExitStack() as es:
        k.es = es
        pw32 = k.sb("pw32", [128, 2, 64], F32)
        pwbd = k.sb("pwbd", [128, 2, 128], BF16)
        psc = k.sb("psc", [128, 2], F32)
        ut = [k.sb("put%d" % i, [128, 2, 16 + NT], F32) for i in range(2)]
        s2s = [k.sb("ps2_%d" % i, [128, 2, 16 + NT], F32) for i in range(2)]
        s4s = [k.sb("ps4_%d" % i, [128, 2, 16 + NT], F32) for i in range(2)]
        s8s = [k.sb("ps8_%d" % i, [128, 2, 16 + NT], F32) for i in range(2)]
        s16s = [k.sb("ps16_%d" % i, [128, 2, 16 + NT], F32) for i in range(2)]
        dds = [k.sb("pdd%d" % i, [128, 2, NT], BF16) for i in range(2)]
        t16 = k.sb("pt16", [128, 16], F32)
        pst = [k.sb("ppst%d" % i, [128, 2, NT], BF16) for i in range(2)]
        pwd = W["pool_w"]
        k.DMA(pw32[:, :, :], pwd.raw(l * 4 * 64 * 64, [[64, 128], [128 * 64, 2], [1, 64]]), [pwd.b()], [pw32.b], "ldg")
        k.MEMSET("pool", pwbd[:, :, :], 0.0, [pwbd.b])
        for c in range(2):
            for half in range(2):
                sl = slice(half * 64, half * 64 + 64)
                k.CP("dve", pwbd[sl, c, half * 64:half * 64 + 64], pw32[sl, c, :], [pw32.b], [pwbd.b])
        ps_ = W["pool_scale"]
        k.DMA(psc[:, :], ps_.raw(l * 256, [[1, 128], [128, 2]]), [ps_.b()], [psc.b], "ldg", allow_slow_non_contiguous=True)
        wins = [2, 4, 8, 16]
        L = 16 + NT
        def load_u(jj):
            u_ = ut[jj % 2]
            cc = jj * NT
            if jj == 0:
                k.DMA(u_[:, :, 16:], LT["uT"].ap()[:, 16:L].rearrange("(c p) t -> p c t", p=128), [LT["uT"].b()],
                      [u_.b], "put0")
                k.DMA(u_[:, :, 0:16], LT["urecv"].ap()[0:256, :].rearrange("(c p) t -> p c t", p=128),
                      [LT["urecv"].b()], [u_.b], "put0")
            else:
                k.DMA(u_[:, :, :], LT["uT"].ap()[:, cc:cc + L].rearrange("(c p) t -> p c t", p=128),
                      [LT["uT"].b()], [u_.b], "put%d" % (jj % 2))

        load_u(0)
        for j in range(T // NT):
            c0 = j * NT
            u = ut[j % 2]
            if j + 1 < T // NT:
                load_u(j + 1)
            s2, s4, s8, s16, dd = s2s[j % 2], s4s[j % 2], s8s[j % 2], s16s[j % 2], dds[j % 2]
            if j == 0:
                k.TS("dve", u[:, :, 0:16], u[:, :, 0:16], k.role[:, 1:2], None, ALU.mult, None, [u.b, k.role.b], [u.b])
            k.TT("dve", s2[:, :, 1:L], u[:, :, 1:L], u[:, :, 0:L - 1], ALU.add, [u.b], [s2.b])
            k.TT("pool", s4[:, :, 3:L], s2[:, :, 3:L], s2[:, :, 1:L - 2], ALU.add, [s2.b], [s4.b])
            k.TT("dve", s8[:, :, 7:L], s4[:, :, 7:L], s4[:, :, 3:L - 4], ALU.add, [s4.b], [s8.b])
            k.TT("pool", s16[:, :, 15:L], s8[:, :, 15:L], s8[:, :, 7:L - 8], ALU.add, [s8.b], [s16.b])
            srcs = [s2, s4, s8, s16]
            for g in range(4):
                c = g // 2
                sl = slice((g % 2) * 64, (g % 2) * 64 + 64)
                sw = srcs[g]
                k.STT("dve", dd[sl, c, :], sw[sl, c, 16:L], 1.0 / wins[g], u[sl, c, 16:L], ALU.mult, ALU.subtract,
                      [sw.b, u.b], [dd.b])
                if j == 0:
                    k.TT("dve", t16[sl, :], sw[sl, c, 16:32], k.role[sl, 2 + c * 16:2 + c * 16 + 16], ALU.mult,
                         [sw.b, k.role.b], [t16.b])
                    k.TT("dve", dd[sl, c, 0:16], t16[sl, :], u[sl, c, 16:32], ALU.subtract, [t16.b, u.b, dd.b], [dd.b])
            o = pst[j % 2]
            for c in range(2):
                pb = k.bank()
                k.MM(pb[:, 0:NT], pwbd[:, c, :], dd[:, c, :], [pwbd.b, dd.b], [pb.b])
                k.TS("dve", o[:, c, :], pb[:, 0:NT], psc[:, c:c + 1], None, ALU.mult, None, [pb.b, psc.b], [o.b])
            k.DMA(mixT.ap()[384:640, c0:c0 + NT].rearrange("(c p) t -> p c t", p=128), o[:, :, :], [o.b],
                  [mixT.b("pool")], "ppst%d" % (j % 2))
    k.P.barrier()


def phase_B_mla(k, l, LT, mixT, pre_hook=None):
    scale = 96.0 ** -0.5
    NKB = 2 * NB
    with contextlib.ExitStack() as es:
        k.es = es
        Vall = k.sb("mV", [128, NKB, 390], BF16)
        KT = [k.sb("mKT%d" % i, [96, 2 * T], BF16) for i in range(2)]
        QT = [k.sb("mQT%d" % i, [96, T], BF16) for i in range(2)]
        Pt = [k.sb("mP%d" % i, [128, 512], BF16) for i in range(5)]
        rrow = [k.sb("mrrow%d" % i, [65, 512], F32) for i in range(2)]
        bcs = k.sb("mbcs", [64, 512], F32)
        ost = [k.sb("most%d" % i, [64, 512], BF16) for i in range(2)]
        for part in range(4):
            b0 = part * 16
            if part < 2:
                srcv = LT["vrecv"][part].ap()[0:T // 2, :]
                srcb = LT["vrecv"][part].b()
            else:
                srcv = LT["vsend"][part - 2].ap()
                srcb = LT["vsend"][part - 2].b()
            k.DMA(Vall[:, b0:b0 + 16, :], srcv.rearrange("(b p) e -> p b e", p=128),
                  [srcb], [Vall.b], "mV")
        SB_ = [k.PB[0], k.PB[1], k.PB[2], k.PB[3], k.PB[7]]
        OTs = [k.PB[4], k.PB[5]]
        LA = 3
        cnt = 0
        pend = None
        def flush_pend():
            nonlocal pend
            if pend is None:
                return
            pot, po, prr, ph, pj, pslot = pend
            pend = None
            softmax_finish_b(k, pot, po[:, :], [po.b], prr, bcs, k.PB[6])
            k.DMA(mixT.ap()[640 + ph * 64:640 + (ph + 1) * 64, pj * 512:(pj + 1) * 512], po[:, :], [po.b],
                  [mixT.b("mla")], "most%d" % pslot)

        def load_head(h):
            kt = KT[h % 2]
            qt_ = QT[h % 2]
            HT = T // 2
            for i in range(2):
                k.DMA(kt[0:64, i * HT:(i + 1) * HT], LT["krecv"][i].ap()[h * 64:(h + 1) * 64, :], [LT["krecv"][i].b()],
                      [kt.b], "mKT%d" % (h % 2))
                k.DMA(kt[0:64, T + i * HT:T + (i + 1) * HT], LT["ksend"][i].ap()[h * 64:(h + 1) * 64, :],
                      [LT["ksend"][i].b()], [kt.b], "mKT%d" % (h % 2))
                k.DMA(kt[64:96, i * HT:(i + 1) * HT], LT["krecv"][i].ap()[384:416, :], [LT["krecv"][i].b()],
                      [kt.b], "mKT%d" % (h % 2))
                k.DMA(kt[64:96, T + i * HT:T + (i + 1) * HT], LT["ksend"][i].ap()[384:416, :],
                      [LT["ksend"][i].b()], [kt.b], "mKT%d" % (h % 2))
            k.DMA(qt_[:, :], LT["qmT"].ap()[h, :, :], [LT["qmT"].b()], [qt_.b], "mQT%d" % (h % 2))

        load_head(0)
        if pre_hook is not None:
            pre_hook()
        for h in range(6):
            kt = KT[h % 2]
            qt_ = QT[h % 2]
            for j in range(T // 512):
                if j == 1 and h + 1 < 6:
                    load_head(h + 1)
                nkb = NB + 4 * j + 4
                ot = OTs[cnt % 2]
                o = ost[cnt % 2]
                cnt += 1

                def qlo_of(kb):
                    c = kb - (NB + 4 * j)
                    return 128 * max(c, 0)

                def QK(kb):
                    ql = qlo_of(kb)
                    sbk = SB_[kb % 5]
                    k.MM(sbk[:, ql:512], kt[:, kb * 128:(kb + 1) * 128], qt_[:, j * 512 + ql:(j + 1) * 512],
                         [kt.b, qt_.b], [sbk.b])

                for kb in range(min(LA, nkb)):
                    QK(kb)
                for kb in range(nkb):
                    if kb + LA < nkb:
                        QK(kb + LA)
                    if kb == 4:
                        flush_pend()
                    ql = qlo_of(kb)
                    sbk = SB_[kb % 5]
                    pt = Pt[kb % 5]
                    if kb < NB:
                        k.ACT(pt[:, ql:512], sbk[:, ql:512], AF.Exp, [sbk.b, k.role.b], [pt.b],
                              bias=k.role[:, 0:1], scale=scale)
                    else:
                        k.ACT(pt[:, ql:512], sbk[:, ql:512], AF.Exp, [sbk.b], [pt.b], scale=scale)
                    if kb >= NB + 4 * j:
                        k.TT("dve", pt[:, ql:ql + 128], pt[:, ql:ql + 128], k.cb[:, CB_TRI:CB_TRI + 128], ALU.mult,
                             [pt.b, k.cb.b], [pt.b])
                    k.MM(ot[0:65, ql:512], Vall[:, kb, h * 65:(h + 1) * 65], pt[:, ql:512], [Vall.b, pt.b], [ot.b],
                         start=(kb == 0), stop=(kb == nkb - 1))
                softmax_finish_a(k, ot, None, [], rrow[(cnt - 1) % 2], False)
                pend = (ot, o, rrow[(cnt - 1) % 2], h, j, (cnt - 1) % 2)
        flush_pend()
    k.P.barrier()


def phase_B_ffn(k, l, mixT, xT_src, xT_dst, final, pre=None):
    W = k.W
    NT = 256
    with contextlib.ExitStack() as es:
        k.es = es
        if pre is None:
            wout = k.sb("wout", [128, 8, D], BF16)
            wg = k.sb("wg", [128, 8, DFF], BF16)
        else:
            wout, wg = pre["wout"], pre["wg"]
        wu = k.sb("wu", [128, 8, DFF], BF16)
        wd = k.sb("wd", [128, NFC, D], BF16)
        gF = k.sb("gF", [128, 8], F32)
        fn = W["ffn_norm"]
        k.DMA(gF[:, 0:8], fn.raw(l * 1024, [[1, 128], [128, 8]]), [fn.b()], [gF.b], "ldg", allow_slow_non_contiguous=True)
        wlist = [(wu, W["w_up"], 8), (wd, W["w_down"], NFC)]
        if pre is None:
            wlist = [(wout, W["w_out"], 8), (wg, W["w_gate"], 8)] + wlist
        else:
            for (dst, srcd, nch) in ((wu, pre["wub"], 8), (wd, pre["wdb"], NFC)):
                for c in range(nch):
                    k.DMA(dst[:, c, :], srcd.ap()[c * 128:(c + 1) * 128, :], [srcd.b()], [dst.b],
                          "wldb%d" % (c % 2))
            wlist = []
        for (dst, src, nch) in wlist:
            for c in range(nch):
                k.DMA(dst[:, c, :], src.ap()[l, c * 128:(c + 1) * 128, :], [src.b()], [dst.b], "wld", q="pool")
        mix = [k.sb("fmix%d" % i, [128, 8, NT], BF16) for i in range(1)]
        xt = [k.sb("fxt%d" % i, [128, 8, NT], F32) for i in range(2)]
        h2 = k.sb("fh2", [128, 8, NT], BF16)
        lnt = k.sb("flnt", [128, NT], F32)
        rstd = k.sb("frstd", [128, NT], F32)
        sg = [k.sb("fsg%d" % i, [128, NT], F32) for i in range(2)]
        act = k.sb("fact", [128, NFC, NT], BF16)
        sqb = k.sb("fsqb", [128, 8, NT], BF16)
        ost = k.sb("fost", [128, D], F32) if final else None
        out = k.out if final else None
        NTI = T // NT

        def f_load(i):
            c0 = i * NT
            k.DMA(mix[0][:, :, :], mixT.ap()[:, c0:c0 + NT].rearrange("(c p) t -> p c t", p=128),
                  [mixT.b("swa"), mixT.b("pool"), mixT.b("mla")], [mix[0].b], "fmix0")
            x_ = xt[i % 2]
            k.DMA(x_[:, :, :], xT_src.ap()[:, :, c0:c0 + NT].rearrange("k p t -> p k t"),
                  [xT_src.b(i // 2)], [x_.b], "fxt%d" % (i % 2))

        def f_outproj_norm(i):
            m_ = mix[0]
            x_ = xt[i % 2]
            for m in range(8):
                pb = k.bank()
                for kc in range(8):
                    k.MM(pb[:, 0:NT], wout[:, kc, m * 128:(m + 1) * 128], m_[:, kc, :], [wout.b, m_.b], [pb.b],
                         start=(kc == 0), stop=(kc == 7))
                k.TT("dve", x_[:, m, :], x_[:, m, :], pb[:, 0:NT], ALU.add, [x_.b, pb.b], [x_.b])
            for kc in range(8):
                if kc % 2 == 0:
                    k.ACT(sqb[:, kc, :], x_[:, kc, :], AF.Square, [x_.b], [sqb.b])
                else:
                    k.TT("pool", sqb[:, kc, :], x_[:, kc, :], x_[:, kc, :], ALU.mult, [x_.b], [sqb.b])
            pb = k.bank()
            for kc in range(8):
                k.MM(pb[:, 0:NT], k.cb[:, CB_ONESD:CB_ONESD + 128], sqb[:, kc, :], [k.cb.b, sqb.b], [pb.b],
                     start=(kc == 0), stop=(kc == 7))
            rsqrt_from_ms(k, pb[:, 0:NT], 128, NT, [pb.b], lnt, rstd[:, :], [rstd.b])
            for kc in range(8):
                k.STT("dve", h2[:, kc, :], x_[:, kc, :], gF[:, kc:kc + 1], rstd[:, :],
                      ALU.mult, ALU.mult, [x_.b, rstd.b, gF.b], [h2.b])

        def f_gateup(i):
            for fc in range(NFC):
                pg = k.bank()
                pu = k.bank()
                for kc in range(8):
                    k.MM(pg[:, 0:NT], wg[:, kc, fc * 128:(fc + 1) * 128], h2[:, kc, :], [wg.b, h2.b], [pg.b],
                         start=(kc == 0), stop=(kc == 7))
                for kc in range(8):
                    k.MM(pu[:, 0:NT], wu[:, kc, fc * 128:(fc + 1) * 128], h2[:, kc, :], [wu.b, h2.b], [pu.b],
                         start=(kc == 0), stop=(kc == 7))
                s_ = sg[fc % 2]
                k.ACT(s_[:, :], pg[:, 0:NT], AF.Silu, [pg.b], [s_.b])
                k.TT("dve", act[:, fc, :], s_[:, :], pu[:, 0:NT], ALU.mult, [s_.b, pu.b], [act.b])

        def f_down_store(i):
            c0 = i * NT
            x_ = xt[i % 2]
            for m in range(8):
                pb = k.bank()
                for fc in range(NFC):
                    k.MM(pb[:, 0:NT], wd[:, fc, m * 128:(m + 1) * 128], act[:, fc, :], [wd.b, act.b], [pb.b],
                         start=(fc == 0), stop=(fc == NFC - 1))
                k.TT("dve", x_[:, m, :], x_[:, m, :], pb[:, 0:NT], ALU.add, [x_.b, pb.b], [x_.b])
            if not final:
                k.DMA(xT_dst.ap()[:, :, c0:c0 + NT].rearrange("k p t -> p k t"), x_[:, :, :], [x_.b],
                      [xT_dst.b(i // 2)], "fxst%d" % (i % 2))
            else:
                for tb in range(2):
                    for half in range(2):
                        pb = k.bank()
                        for mm in range(4):
                            m = half * 4 + mm
                            k.TR(pb[:, mm * 128:(mm + 1) * 128], x_[:, m, tb * 128:(tb + 1) * 128],
                                 k.cf[:, CF_ID:CF_ID + 128], [x_.b, k.cf.b], [pb.b])
                        k.CP("act" if half == 0 else "dve", ost[:, half * 512:(half + 1) * 512], pb[:, :],
                             [pb.b], [ost.b])
                    r0 = c0 + tb * 128
                    k.DMA(out.ap()[r0:r0 + 128, :], ost[:, :], [ost.b], [out.b()], "fost")

        f_load(0)
        f_outproj_norm(0)
        for i in range(NTI):
            if i + 1 < NTI:
                f_load(i + 1)
            f_gateup(i)
            if i + 1 < NTI:
                f_outproj_norm(i + 1)
            f_down_store(i)
    k.P.barrier()


def phase_exchange(k, l, LT):
    k.ALLGATHER(LT["ssend"], LT["srecv"], "ag_s")
    k.ALLGATHER(LT["usend"], LT["urecv"], "ag_u")
    for i in range(2):
        k.ALLGATHER(LT["ksend"][i], LT["krecv"][i], "ag_k%d" % i)
        k.ALLGATHER(LT["vsend"][i], LT["vrecv"][i], "ag_v%d" % i)


def build(cfg):
    nc = bass.Bass("TRN2", target_bir_lowering=False)
    k = K(nc, cfg)
    phases = cfg["phases"]
    k.in_x = k.dram("x", [T, D], F32, kind="ExternalInput")
    k.in_pos = k.dram("pos", [1, T], I32, kind="ExternalInput")
    k.in_cb = k.dram("cb", [128, NCB], BF16, kind="ExternalInput")
    k.in_cf = k.dram("cf", [128, NCF], F32, kind="ExternalInput")
    k.in_role = k.dram("role", [128, NROLE], F32, kind="ExternalInput")
    k.W = {n: k.dram(n, shp, F32, kind="ExternalInput") for n, shp in WEIGHT_NAMES}
    if "B1" in phases:
        k.out = k.dram("out", [T, D], F32, kind="ExternalOutput")
    xTv = [k.dram("xT_v%d" % i, [8, 128, T], F32) for i in range(2)]
    mixT = [k.dram("mixT_%d" % i, [D, T], BF16) for i in range(2)]
    LT = [layer_tensors(k, l) for l in range(DEPTH)]
    with contextlib.ExitStack() as outer:
        k.es = outer
        k.cb = k.sb("cbs", [128, NCB], BF16)
        k.cf = k.sb("cfs", [128, NCF], F32)
        k.role = k.sb("roles", [128, NROLE], F32)
        k.epsb = k.sb("epsb", [128, 1], F32)
        k.PB = [Tl(outer.enter_context(nc.psum_tensor("pb%d" % i, [128, 512], F32)), "pb%d" % i) for i in range(8)]
        for ph in phases:
            if ph == "setup":
                phase_setup(k)
            elif ph == "A0":
                (phase_A2 if cfg.get("pipeA", True) else phase_A)(k, 0, "transpose", None, xTv[0], LT[0])
            elif ph == "A1":
                (phase_A2 if cfg.get("pipeA", True) else phase_A)(k, 1, "load", xTv[1], None, LT[1])
            elif ph in ("X0", "X1"):
                phase_exchange(k, int(ph[1]), LT[int(ph[1])])
            elif ph in ("B0", "B1"):
                l = int(ph[1])
                sub = cfg.get("sub", ("swa", "pool", "mla", "ffn"))
                with contextlib.ExitStack() as es_pre:
                    k.es = es_pre
                    pre = {"wout": k.sb("wout", [128, 8, D], BF16), "wg": k.sb("wg", [128, 8, DFF], BF16)}

                    wub = k.dram("wu_bf16", [D, DFF], BF16)
                    wdb = k.dram("wd_bf16", [DFF, D], BF16)
                    pre["wub"] = wub
                    pre["wdb"] = wdb

                    def pre_hook(l=l, pre=pre, wub=wub, wdb=wdb):
                        for (dst, src, nch) in ((pre["wout"], k.W["w_out"], 8), (pre["wg"], k.W["w_gate"], 8)):
                            for c in range(nch):
                                k.DMA(dst[:, c, :], src.ap()[l, c * 128:(c + 1) * 128, :], [src.b()], [dst.b],
                                      "wld", q="pool")
                        for (dstd, src, nch) in ((wub, k.W["w_up"], 8), (wdb, k.W["w_down"], NFC)):
                            for c in range(nch):
                                k.DMA(dstd.ap()[c * 128:(c + 1) * 128, :], src.ap()[l, c * 128:(c + 1) * 128, :],
                                      [src.b()], [dstd.b()], "wcast", q="pool")
                    if "swa" in sub:
                        phase_B_swa(k, l, LT[l], mixT[l])
                    if "pool" in sub:
                        phase_B_pool(k, l, LT[l], mixT[l])
                    if "mla" in sub:
                        phase_B_mla(k, l, LT[l], mixT[l], pre_hook=pre_hook)
                    if "ffn" in sub:
                        phase_B_ffn(k, l, mixT[l], xTv[l], xTv[1] if l == 0 else None, final=(l == 1), pre=pre)
            k.es = outer
        sem_stack = contextlib.ExitStack()
        k.P.emit(sem_stack)
        sem_stack.close()
    return nc, k


def _core_inputs(inputs, cb, cf):
    maps = []
    for c in range(NCORES):
        b, hh = c // 2, c % 2
        m = {"x": np.ascontiguousarray(inputs["x"][b, hh * T:(hh + 1) * T, :]),
             "pos": np.ascontiguousarray(inputs["positions"][b, hh * T:(hh + 1) * T]).reshape(1, T).astype(np.int32),
             "cb": cb, "cf": cf, "role": host_role(c)}
        for n, _ in WEIGHT_NAMES:
            m[n] = np.ascontiguousarray(inputs[n])
        maps.append(m)
    return maps


def kernel(**inputs):
    cb, cf = host_consts()
    base = _core_inputs(inputs, cb, cf)
    ids = list(range(NCORES))
    cfg = {"phases": ["setup", "A0", "X0", "B0", "A1", "X1", "B1"], "ext_in": set(), "ext_out": set()}
    nc, _ = build(cfg)
    res = run_bass_kernel_spmd(nc, base, core_ids=ids).results
    out = np.zeros((4, S, D), np.float32)
    for c in range(NCORES):
        out[c // 2, (c % 2) * T:(c % 2 + 1) * T, :] = np.asarray(res[c]["out"], dtype=np.float32)
    return out
```

```python
import contextlib
import math
import numpy as np
import ml_dtypes
import concourse.bass as bass
import concourse.mybir as mybir
from concourse.bass_utils import run_bass_kernel_spmd

F32 = mybir.dt.float32
BF16 = mybir.dt.bfloat16
I32 = mybir.dt.int32
AF = mybir.ActivationFunctionType
ALU = mybir.AluOpType

NCORES = 8
D = 1024
S = 8192
T = 4096
NB = T // 128
DEPTH = 2
DFF = 2816
NFC = DFF // 128
EPS = 1e-6
NEG = -30000.0
IN_DIM = 1312
TWO_PI = 2.0 * math.pi
CW1 = 6.28125
CW2 = TWO_PI - CW1
MAGIC = 12582912.0


class Buf:
    __slots__ = ("name", "last_w", "readers", "sem", "ndma", "chan_readers", "inc")

    def __init__(self, name, inc=16):
        self.name = name
        self.inc = inc
        self.last_w = None
        self.readers = []
        self.sem = None
        self.ndma = 0
        self.chan_readers = []


class Prog:
    COMPUTE = ("pe", "act", "dve", "pool")

    def __init__(self, nc):
        self.nc = nc
        self.ins = []
        self.eng = {"pe": nc.tensor, "act": nc.scalar, "dve": nc.vector,
                    "pool": nc.gpsimd, "sp": nc.sync}
        self.last_on = {}
        self.all_chans = []
        self.pending_bar = {}

    def barrier(self):
        deps = set(self.last_on.values())
        for c in self.all_chans:
            if c.last_w is not None:
                deps.add(c.last_w)
        for s in self.eng:
            self.pending_bar[s] = set(deps) | self.pending_bar.get(s, set())

    def _rec(self, stream, fn, reads, writes, chan):
        j = len(self.ins)
        raw = set()
        oth = set()
        for b in reads:
            if b.last_w is not None:
                raw.add(b.last_w)
            if chan is None:
                b.readers = [r for r in b.readers
                             if not (self.ins[r]["chan"] is None and self.ins[r]["stream"] == stream)]
            else:
                b.readers = [r for r in b.readers if self.ins[r]["chan"] is not chan]
            b.readers.append(j)
        for b in writes:
            if b.last_w is not None:
                oth.add(b.last_w)
            for r in b.readers:
                if r != j:
                    oth.add(r)
            b.last_w = j
            b.readers = []
        if stream in self.pending_bar:
            raw |= self.pending_bar.pop(stream)
        if chan is not None:
            if chan.ndma == 0:
                self.all_chans.append(chan)
            for r in chan.chan_readers:
                oth.add(r)
            chan.chan_readers = []
            chan.ndma += 1
            chan.last_w = j
        self.ins.append({"stream": stream, "fn": fn, "raw": raw, "oth": oth,
                         "chan": chan, "signal": False})
        for d in raw | oth:
            c = self.ins[d]["chan"]
            if c is not None:
                if chan is None:
                    c.chan_readers = [r for r in c.chan_readers
                                      if not (self.ins[r]["chan"] is None and self.ins[r]["stream"] == stream)]
                c.chan_readers.append(j)
        if chan is None:
            self.last_on[stream] = j
        return j

    def op(self, stream, fn, reads=(), writes=()):
        return self._rec(stream, fn, list(reads), list(writes), None)

    def dma(self, stream, out_ap, in_ap, reads, writes, chan, **kw):
        eng = self.eng[stream]
        return self._rec(stream, lambda: eng.dma_start(out=out_ap, in_=in_ap, **kw),
                         list(reads), list(writes), chan)

    def emit(self, stack):
        nc = self.nc
        ins = self.ins
        n = len(ins)
        need = [None] * n
        for j, r in enumerate(ins):
            deps = set(r["raw"])
            for d in r["oth"]:
                di = ins[d]
                if not (di["chan"] is None and r["chan"] is None and di["stream"] == r["stream"] == "pe"):
                    deps.add(d)
            need[j] = deps
            for d in deps:
                ins[d]["signal"] = True
        sems = {s: stack.enter_context(nc.semaphore("prog_" + s)) for s in self.COMPUTE}
        chan_cnt = {}
        ordn = {s: 0 for s in self.COMPUTE}
        sig_ord = [0] * n
        waited = {s: {} for s in self.eng}
        nwaits = 0
        for j, r in enumerate(ins):
            st = r["stream"]
            e = self.eng[st]
            wants = {}
            for d in need[j]:
                di = ins[d]
                if di["chan"] is not None:
                    c = di["chan"]
                    key = ("c", id(c))
                    val = c.inc * chan_cnt[id(c)]
                    semh = c.sem
                else:
                    key = ("e", di["stream"])
                    val = sig_ord[d]
                    semh = sems[di["stream"]]
                if key not in wants or wants[key][1] < val:
                    wants[key] = (semh, val)
            for key, (semh, val) in wants.items():
                if waited[st].get(key, 0) < val:
                    e.wait_ge(semh, val)
                    waited[st][key] = val
                    nwaits += 1
            bi = r["fn"]()
            if r["chan"] is not None:
                c = r["chan"]
                if c.sem is None:
                    c.sem = stack.enter_context(nc.semaphore("ch_" + c.name))
                chan_cnt[id(c)] = chan_cnt.get(id(c), 0) + 1
                bi.then_inc(c.sem, c.inc)
            elif r["signal"]:
                ordn[st] += 1
                sig_ord[j] = ordn[st]
                bi.then_inc(sems[st], 1)
        for c in self.all_chans:
            nc.sync.wait_ge(c.sem, c.inc * chan_cnt[id(c)])
        self.stats = {"n": n, "waits": nwaits, "signals": dict(ordn), "nchan": len(self.all_chans)}


class Tl:
    def __init__(self, t, name):
        self.t = t
        self.b = Buf(name)

    def __getitem__(self, idx):
        return self.t[idx]


class DT:
    def __init__(self, h, name):
        self.h = h
        self.name = name
        self.bufs = {}

    def ap(self):
        return self.h.ap()

    def b(self, key=None):
        if key not in self.bufs:
            self.bufs[key] = Buf("%s/%s" % (self.name, key))
        return self.bufs[key]

    def raw(self, offset, pat):
        return bass.AP(tensor=self.h, offset=offset, ap=pat)


class K:
    def __init__(self, nc, cfg):
        self.nc = nc
        self.cfg = cfg
        self.P = Prog(nc)
        self.dts = {}
        self.chans = {}
        self.es = None
        self.rr = {"ew": 0, "bank": 0, "cast": 0}

    def dram(self, name, shape, dt, kind=None):
        if name in self.dts:
            return self.dts[name]
        if kind is None:
            if name in self.cfg["ext_in"]:
                kind = "ExternalInput"
            elif name in self.cfg["ext_out"]:
                kind = "ExternalOutput"
            else:
                kind = "Internal"
        h = self.nc.dram_tensor(name, list(shape), dt, kind=kind)
        d = DT(h, name)
        self.dts[name] = d
        return d

    def sb(self, name, shape, dt):
        self.rr["uid"] = self.rr.get("uid", 0) + 1
        name = "%s_u%d" % (name, self.rr["uid"])
        t = self.es.enter_context(self.nc.sbuf_tensor(name, list(shape), dt))
        return Tl(t, name)

    def chan(self, name):
        if name not in self.chans:
            self.chans[name] = Buf(name)
        return self.chans[name]

    def balloc(self):
        if not hasattr(self, "_bfree"):
            self._bfree = list(range(8))
        assert self._bfree, "out of PSUM banks"
        return self.PB[self._bfree.pop(0)]

    def bfree(self, pb):
        i = self.PB.index(pb)
        assert i not in self._bfree
        self._bfree.append(i)

    def bank(self):
        i = self.rr["bank"]
        self.rr["bank"] = (i + 1) % 8
        return self.PB[i]

    def DMA(self, out_ap, in_ap, R, W, chan, q="sp", **kw):
        self.P.dma(q, out_ap, in_ap, R, W, self.chan(chan + ("_sw" if q == "pool" else "")), **kw)

    def ALLGATHER(self, send, recv, name):
        nc = self.nc
        if name not in self.chans:
            self.chans[name] = Buf(name, inc=1)
        groups = [[2 * i, 2 * i + 1] for i in range(NCORES // 2)]
        self.P._rec("pool", lambda: nc.gpsimd.collective_compute(
            "AllGather", ALU.bypass, replica_groups=groups,
            ins=[send.ap().opt()], outs=[recv.ap().opt()]), [send.b()], [recv.b()], self.chans[name])

    def MM(self, out_ap, lhsT, rhs, R, W, start=True, stop=True):
        nc = self.nc
        self.P.op("pe", lambda: nc.tensor.matmul(out_ap, lhsT=lhsT, rhs=rhs, start=start, stop=stop), R, W)

    def TR(self, out_ap, in_ap, ident, R, W):
        nc = self.nc
        self.P.op("pe", lambda: nc.tensor.transpose(out=out_ap, in_=in_ap, identity=ident), R, W)

    def ACT(self, out_ap, in_ap, func, R, W, bias=0.0, scale=1.0):
        nc = self.nc
        self.P.op("act", lambda: nc.scalar.activation(out=out_ap, in_=in_ap, func=func, bias=bias, scale=scale), R, W)

    def _ve(self, eng):
        return self.nc.vector if eng == "dve" else self.nc.gpsimd

    def TT(self, eng, out_ap, in0, in1, op, R, W):
        e = self._ve(eng)
        self.P.op(eng, lambda: e.tensor_tensor(out=out_ap, in0=in0, in1=in1, op=op), R, W)

    def TS(self, eng, out_ap, in0, s1, s2, op0, op1, R, W):
        e = self._ve(eng)
        if op1 is None:
            self.P.op(eng, lambda: e.tensor_scalar(out=out_ap, in0=in0, scalar1=s1, scalar2=None, op0=op0), R, W)
        else:
            self.P.op(eng, lambda: e.tensor_scalar(out=out_ap, in0=in0, scalar1=s1, scalar2=s2, op0=op0, op1=op1), R, W)

    def STT(self, eng, out_ap, in0, scalar, in1, op0, op1, R, W):
        e = self._ve(eng)
        self.P.op(eng, lambda: e.scalar_tensor_tensor(out=out_ap, in0=in0, scalar=scalar, in1=in1, op0=op0, op1=op1), R, W)

    def CP(self, eng, out_ap, in_ap, R, W):
        if eng == "act":
            nc = self.nc
            self.P.op("act", lambda: nc.scalar.copy(out=out_ap, in_=in_ap), R, W)
        else:
            e = self._ve(eng)
            self.P.op(eng, lambda: e.tensor_copy(out=out_ap, in_=in_ap), R, W)

    def MEMSET(self, eng, ap, val, W):
        e = self._ve(eng)
        self.P.op(eng, lambda: e.memset(ap, val), [], W)

    def RECIP(self, out_ap, in_ap, R, W):
        nc = self.nc
        self.P.op("dve", lambda: nc.vector.reciprocal(out=out_ap, in_=in_ap), R, W)

    def ew(self):
        i = self.rr["ew"]
        self.rr["ew"] = i + 1
        return "dve" if i % 2 == 0 else "pool"


CB_ONESD, CB_B64, CB_ONES256, CB_ONES128, CB_ONES32, CB_B96, CB_P96, CB_P32, CB_TRI = (
    0, 128, 256, 384, 512, 544, 640, 736, 768)
NCB = 896
CF_ID, CF_ONES, CF_IFQ, CF_IFK, CF_OH, CF_MV = 0, 128, 256, 257, 258, 258 + 383
NCF = 258 + 383 + 383
NROLE = 34


def _t5_bucket(n):
    n = np.maximum(n, 0)
    nf = np.maximum(n, 1).astype(np.float32)
    large = 16 + (np.log(nf / np.float32(16)) / np.float32(math.log(8.0)) * np.float32(16)).astype(np.int32)
    large = np.minimum(large, 31)
    return np.where(n < 16, n, large)


def host_consts():
    cb = np.zeros((128, NCB), np.float32)
    cb[:, CB_ONESD:CB_ONESD + 128] = 1.0 / 1024
    cb[0:64, CB_B64:CB_B64 + 64] = 1.0 / 64
    cb[64:128, CB_B64 + 64:CB_B64 + 128] = 1.0 / 64
    cb[:, CB_ONES256:CB_ONES256 + 128] = 1.0 / 256
    cb[:, CB_ONES128:CB_ONES128 + 128] = 1.0 / 128
    cb[0:32, CB_ONES32:CB_ONES32 + 32] = 1.0 / 32
    cb[0:64, CB_B96:CB_B96 + 64] = 1.0 / 64
    cb[64:96, CB_B96 + 64:CB_B96 + 96] = 1.0 / 32
    for i in range(16):
        cb[80 + i, CB_P96 + 64 + i] = -1.0
        cb[64 + i, CB_P96 + 80 + i] = 1.0
        cb[16 + i, CB_P32 + i] = -1.0
        cb[i, CB_P32 + 16 + i] = 1.0
    p = np.arange(128)[:, None]
    f = np.arange(128)[None, :]
    cb[:, CB_TRI:CB_TRI + 128] = (p <= f).astype(np.float32)
    cf = np.zeros((128, NCF), np.float32)
    cf[:, CF_ID:CF_ID + 128] = np.eye(128, dtype=np.float32)
    cf[:, CF_ONES:CF_ONES + 128] = 1.0
    inv_freq = (np.float32(10000.0) ** (-np.arange(0, 32, 2, dtype=np.float32) / np.float32(32))).astype(np.float32)
    for i in range(16):
        cf[64 + i, CF_IFQ] = inv_freq[i]
        cf[80 + i, CF_IFQ] = inv_freq[i]
        cf[i, CF_IFK] = inv_freq[i]
        cf[16 + i, CF_IFK] = inv_freq[i]
    dist = np.arange(383) - 127
    valid = (dist >= 0) & (dist < 128)
    bk = _t5_bucket(dist)
    for i in range(383):
        if valid[i]:
            cf[bk[i], CF_OH + i] = 1.0
    cf[:, CF_MV:CF_MV + 383] = np.where(valid, 0.0, NEG)[None, :]
    return cb.astype(ml_dtypes.bfloat16), cf


def host_role(core):
    second = (core % 2) == 1
    r = np.zeros((128, NROLE), np.float32)
    r[:, 0] = 0.0 if second else NEG
    r[:, 1] = 1.0 if second else 0.0
    wins = [2, 4, 8, 16]
    for c in range(2):
        for half in range(2):
            w = wins[2 * c + half]
            for t in range(16):
                cnt = float(w) if second else float(min(t + 1, w))
                r[half * 64:(half + 1) * 64, 2 + c * 16 + t] = 1.0 / cnt
    return r


WEIGHT_NAMES = [("rel_bias", [32, 6]), ("attn_norm", [2, 1024]), ("w_in", [2, 1024, 1312]),
                ("swa_q_gain", [2, 64]), ("swa_k_gain", [2, 64]), ("swa_sinks", [2, 6]),
                ("pool_w", [2, 4, 64, 64]), ("pool_scale", [2, 256]), ("mla_q_a_gain", [2, 256]),
                ("mla_w_qb", [2, 256, 576]), ("mla_kv_a_gain", [2, 128]), ("mla_w_kvb", [2, 128, 768]),
                ("mla_q_nope_gain", [2, 64]), ("mla_q_rope_gain", [2, 32]), ("mla_k_nope_gain", [2, 64]),
                ("mla_k_rope_gain", [2, 32]), ("w_out", [2, 1024, 1024]), ("ffn_norm", [2, 1024]),
                ("w_gate", [2, 1024, 2816]), ("w_up", [2, 1024, 2816]), ("w_down", [2, 2816, 1024])]


def layer_tensors(k, l):
    d = {}
    d["qsT"] = k.dram("qsT_%d" % l, [384, T], BF16)
    d["ksT"] = k.dram("ksT_%d" % l, [128, 128 + T], BF16)
    d["vs"] = k.dram("vs_%d" % l, [128 + T, 130], BF16)
    d["uT"] = k.dram("uT_%d" % l, [256, 16 + T], F32)
    d["qmT"] = k.dram("qmT_%d" % l, [6, 96, T], BF16)
    d["ksend"] = [k.dram("ksend%d_%d" % (i, l), [416, T // 2], BF16) for i in range(2)]
    d["krecv"] = [k.dram("krecv%d_%d" % (i, l), [832, T // 2], BF16) for i in range(2)]
    d["vsend"] = [k.dram("vsend%d_%d" % (i, l), [T // 2, 390], BF16) for i in range(2)]
    d["vrecv"] = [k.dram("vrecv%d_%d" % (i, l), [T, 390], BF16) for i in range(2)]
    d["ssend"] = k.dram("ssend_%d" % l, [128, 258], BF16)
    d["srecv"] = k.dram("srecv_%d" % l, [256, 258], BF16)
    d["usend"] = k.dram("usend_%d" % l, [256, 16], F32)
    d["urecv"] = k.dram("urecv_%d" % l, [512, 16], F32)
    return d


def col_gain(k, dst_tile, col, src_dt, l, n, reps, chan):
    for r in range(reps):
        src = src_dt.raw(l * n, [[1, n], [1, 1]])
        k.DMA(dst_tile[r * n:(r + 1) * n, col:col + 1], src, [src_dt.b()], [dst_tile.b], chan)


def rsqrt_from_ms(k, ms_ap, M, N, R, tmp, out_ap, Wt):
    k.ACT(tmp[0:M, 0:N], ms_ap, AF.Ln, R, [tmp.b], bias=k.epsb[0:M, 0:1], scale=1.0)
    k.ACT(out_ap, tmp[0:M, 0:N], AF.Exp, [tmp.b], Wt, scale=-0.5)


def phase_setup(k):
    nc = k.nc
    W = k.W
    k.DMA(k.cb[:, :], k.in_cb.ap(), [k.in_cb.b()], [k.cb.b], "const")
    k.DMA(k.cf[:, :], k.in_cf.ap(), [k.in_cf.b()], [k.cf.b], "const")
    k.DMA(k.role[:, :], k.in_role.ap(), [k.in_role.b()], [k.role.b], "const")
    k.MEMSET("pool", k.epsb[:, :], EPS, [k.epsb.b])
    with contextlib.ExitStack() as es:
        k.es = es
        csq = k.dram("csq", [2, 96, T], F32)
        csk = k.dram("csk", [2, 32, T], F32)
        posi = k.sb("posi", [96, T], I32)
        ang = k.sb("ang", [96, T], F32)
        kf = k.sb("kf", [96, T], F32)
        r1 = k.sb("r1", [96, T], F32)
        r2 = k.sb("r2", [96, T], F32)
        src = k.in_pos.raw(0, [[0, 96], [1, T]])
        k.DMA(posi[:, :], src, [k.in_pos.b()], [posi.b], "ld0")
        for (npart, ifcol, dst) in ((96, CF_IFQ, csq), (32, CF_IFK, csk)):
            sl = slice(0, npart)
            k.CP("dve", ang[sl, :], posi[sl, :], [posi.b], [ang.b])
            k.TS("dve", ang[sl, :], ang[sl, :], k.cf[sl, ifcol:ifcol + 1], None, ALU.mult, None, [ang.b, k.cf.b], [ang.b])
            k.TS("dve", kf[sl, :], ang[sl, :], 1.0 / TWO_PI, MAGIC, ALU.mult, ALU.add, [ang.b], [kf.b])
            k.TS("dve", kf[sl, :], kf[sl, :], MAGIC, None, ALU.subtract, None, [kf.b], [kf.b])
            k.STT("dve", r1[sl, :], kf[sl, :], -CW1, ang[sl, :], ALU.mult, ALU.add, [kf.b, ang.b], [r1.b])
            k.STT("dve", r1[sl, :], kf[sl, :], -CW2, r1[sl, :], ALU.mult, ALU.add, [kf.b, r1.b], [r1.b])
            k.TS("dve", r2[sl, :], r1[sl, :], 3.1415925, -3.1415925, ALU.min, ALU.max, [r1.b], [r2.b])
            k.ACT(r2[sl, :], r2[sl, :], AF.Sin, [r2.b], [r2.b])
            k.DMA(dst.ap()[1, :, :], r2[sl, :], [r2.b], [dst.b()], "st0")
            k.TS("dve", r1[sl, :], r1[sl, :], math.pi / 2, None, ALU.add, None, [r1.b], [r1.b])
            k.TS("dve", kf[sl, :], r1[sl, :], math.pi, -TWO_PI, ALU.is_gt, ALU.mult, [r1.b], [kf.b])
            k.TT("dve", r1[sl, :], r1[sl, :], kf[sl, :], ALU.add, [r1.b, kf.b], [r1.b])
            k.TS("dve", r2[sl, :], r1[sl, :], 3.1415925, -3.1415925, ALU.min, ALU.max, [r1.b, r2.b], [r2.b])
            k.ACT(r2[sl, :], r2[sl, :], AF.Sin, [r2.b], [r2.b])
            k.DMA(dst.ap()[0, :, :], r2[sl, :], [r2.b], [dst.b()], "st0")
        tv = k.dram("tv", [6, 128, 383], F32)
        rb = k.sb("rb", [32, 6], F32)
        lh = k.sb("lh", [32, 128], F32)
        tvs = k.sb("tvs", [128, 383], F32)
        k.DMA(rb[:, :], W["rel_bias"].ap(), [W["rel_bias"].b()], [rb.b], "ld1")
        for h in range(6):
            k.TS("dve", lh[:, :], k.cf[0:32, CF_ONES:CF_ONES + 128], rb[:, h:h + 1], None, ALU.mult, None,
                 [k.cf.b, rb.b], [lh.b])
            pb = k.bank()
            k.MM(pb[:, 0:383], lh[:, :], k.cf[0:32, CF_OH:CF_OH + 383], [lh.b, k.cf.b], [pb.b])
            k.TT("dve", tvs[:, :], pb[:, 0:383], k.cf[:, CF_MV:CF_MV + 383], ALU.add, [pb.b, k.cf.b], [tvs.b])
            k.DMA(tv.ap()[h, :, :], tvs[:, :], [tvs.b], [tv.b()], "st1")
    k.P.barrier()


def load_cast_rows(k, dst_tile, dst_ap_fn, src_dt, src_ap_fn, nchunks, ncols, gain_tile, stg, stgname):
    W_ = stg[0].t.shape[1]
    for c in range(nchunks):
        for p0 in range(0, ncols, W_):
            p1 = min(ncols, p0 + W_)
            i = k.rr["cast"]
            k.rr["cast"] = i + 1
            s = stg[i % len(stg)]
            k.DMA(s[:, 0:p1 - p0], src_ap_fn(c)[:, p0:p1], [src_dt.b()], [s.b], "%s%d" % (stgname, i % len(stg)),
                  q=("sp" if i % 2 == 0 else "pool"))
            eng = ("dve", "act", "pool")[i % 3]
            out_ap = dst_ap_fn(c)[:, p0:p1]
            if gain_tile is None:
                k.CP(eng, out_ap, s[:, 0:p1 - p0], [s.b], [dst_tile.b])
            elif eng == "act":
                k.ACT(out_ap, s[:, 0:p1 - p0], AF.Copy, [s.b, gain_tile.b], [dst_tile.b], scale=gain_tile[:, c:c + 1])
            else:
                k.TS(eng, out_ap, s[:, 0:p1 - p0], gain_tile[:, c:c + 1], None, ALU.mult, None,
                     [s.b, gain_tile.b], [dst_tile.b])


def norm_group(k, ps, M, N, bmat_ap, gain_ap, out_ap, Wout, sqg, lnt, rst, extra_R=()):
    k.ACT(sqg[0:M, 0:N], ps[0:M, 0:N], AF.Square, [ps.b], [sqg.b])
    pb = k.bank()
    k.MM(pb[0:M, 0:N], bmat_ap, sqg[0:M, 0:N], [sqg.b, k.cb.b], [pb.b])
    rsqrt_from_ms(k, pb[0:M, 0:N], M, N, [pb.b], lnt, rst[0:M, 0:N], [rst.b])
    k.STT("dve", out_ap, ps[0:M, 0:N], gain_ap, rst[0:M, 0:N], ALU.mult, ALU.mult,
          [ps.b, rst.b] + list(extra_R), Wout)


def run_batch(k, groups, N, sqg, lng, rsg):
    pbs = []
    for g in groups:
        pb = k.bank()
        g["mm"](pb)
        pbs.append(pb)
    for i, g in enumerate(groups):
        M = g["M"]
        k.ACT(sqg[i][0:M, 0:N], pbs[i][0:M, 0:N], AF.Square, [pbs[i].b], [sqg[i].b])
    keys = []
    for g in groups:
        if g["ms"] not in keys:
            keys.append(g["ms"])
    rs_of = {}
    for ki, key in enumerate(keys):
        mem = [i for i, g in enumerate(groups) if g["ms"] == key]
        M = groups[mem[0]]["M"]
        pm = k.bank()
        for n_, i in enumerate(mem):
            k.MM(pm[0:M, 0:N], groups[i]["bmat"], sqg[i][0:M, 0:N], [sqg[i].b, k.cb.b], [pm.b],
                 start=(n_ == 0), stop=(n_ == len(mem) - 1))
        rs_of[key] = (ki, pm, M)
    for key in keys:
        ki, pm, M = rs_of[key]
        k.ACT(lng[ki][0:M, 0:N], pm[0:M, 0:N], AF.Ln, [pm.b], [lng[ki].b], bias=k.epsb[0:M, 0:1], scale=1.0)
    for key in keys:
        ki, pm, M = rs_of[key]
        k.ACT(rsg[ki][0:M, 0:N], lng[ki][0:M, 0:N], AF.Exp, [lng[ki].b], [rsg[ki].b], scale=-0.5)
    for i, g in enumerate(groups):
        ki, pm, M = rs_of[g["ms"]]
        k.STT("dve", g["out"], pbs[i][0:M, 0:N], g["gain"], rsg[ki][0:M, 0:N], ALU.mult, ALU.mult,
              [pbs[i].b, rsg[ki].b] + list(g["gainR"]), g["outW"])


def phase_A(k, l, xin_mode, xT_src, xT_dst, LT):
    nc = k.nc
    W = k.W
    NT = 512
    with contextlib.ExitStack() as es:
        k.es = es
        win = k.sb("win", [128, 8, IN_DIM], BF16)
        wqb = k.sb("wqb", [128, 2, 576], BF16)
        wkn = k.sb("wkn", [128, 384], BF16)
        wv = k.sb("wv", [128, 384], BF16)
        gA = k.sb("gA", [128, 16], F32)
        gl = k.sb("gl", [128, 8], F32)
        an = W["attn_norm"]
        k.DMA(gA[:, 0:8], an.raw(l * 1024, [[1, 128], [128, 8]]), [an.b()], [gA.b], "ldg", allow_slow_non_contiguous=True)
        qa = W["mla_q_a_gain"]
        k.DMA(gA[:, 8:10], qa.raw(l * 256, [[1, 128], [128, 2]]), [qa.b()], [gA.b], "ldg", allow_slow_non_contiguous=True)
        kva = W["mla_kv_a_gain"]
        k.DMA(gA[:, 10:11], kva.raw(l * 128, [[1, 128], [1, 1]]), [kva.b()], [gA.b], "ldg")
        col_gain(k, gl, 0, W["swa_q_gain"], l, 64, 2, "ldg")
        col_gain(k, gl, 1, W["swa_k_gain"], l, 64, 2, "ldg")
        col_gain(k, gl, 2, W["mla_k_nope_gain"], l, 64, 2, "ldg")
        k.DMA(gl[0:64, 3:4], W["mla_q_nope_gain"].raw(l * 64, [[1, 64], [1, 1]]), [W["mla_q_nope_gain"].b()], [gl.b], "ldg")
        k.DMA(gl[64:96, 3:4], W["mla_q_rope_gain"].raw(l * 32, [[1, 32], [1, 1]]), [W["mla_q_rope_gain"].b()], [gl.b], "ldg")
        k.DMA(gl[0:32, 4:5], W["mla_k_rope_gain"].raw(l * 32, [[1, 32], [1, 1]]), [W["mla_k_rope_gain"].b()], [gl.b], "ldg")
        wi = W["w_in"]
        for c in range(8):
            k.DMA(win[:, c, :], wi.ap()[l, c * 128:(c + 1) * 128, :], [wi.b()], [win.b], "wld", q="pool")
        wq = W["mla_w_qb"]
        for c in range(2):
            k.DMA(wqb[:, c, :], wq.ap()[l, c * 128:(c + 1) * 128, :], [wq.b()], [wqb.b], "wld", q="pool")
        wk = W["mla_w_kvb"]
        wkv_v = wk.ap()[l, :, :].rearrange("p (h t d) -> p t h d", h=6, t=2, d=64)
        k.DMA(wkn[:, :].rearrange("p (h d) -> p h d", h=6), wkv_v[:, 0, :, :], [wk.b()], [wkn.b], "wld", q="pool")
        k.DMA(wv[:, :].rearrange("p (h d) -> p h d", h=6), wkv_v[:, 1, :, :], [wk.b()], [wv.b], "wld", q="pool")

        xtok = k.sb("xtok", [128, 4, D], F32) if xin_mode == "transpose" else None
        xT = [k.sb("xT%d" % i, [128, 8, NT], F32) for i in range(2)]
        sqb = k.sb("sqb", [128, 8, NT], BF16)
        hT = k.sb("hT", [128, 8, NT], BF16)
        lnt = k.sb("lnt", [128, NT], F32)
        rstd = k.sb("rstd", [128, NT], F32)
        sqg = [k.sb("sqg%d" % i, [128, NT], BF16) for i in range(4)]
        lng = [k.sb("lng%d" % i, [128, NT], F32) for i in range(4)]
        rsg = [k.sb("rsg%d" % i, [128, NT], F32) for i in range(4)]
        qs_st = k.sb("qs_st", [128, 3, NT], BF16)
        ks_st = k.sb("ks_st", [128, NT], BF16)
        u_st = k.sb("u_st", [128, 2, NT], F32)
        cqn = k.sb("cqn", [128, 2, NT], BF16)
        ckvn = k.sb("ckvn", [128, NT], BF16)
        krn = k.sb("krn", [32, NT], BF16)
        kr_st = k.sb("kr_st", [32, NT], BF16)
        csk_t = k.sb("csk_t", [32, 2, NT], F32)
        csq_t = k.sb("csq_t", [96, 2, NT], F32)
        t1 = [k.sb("t1_%d" % i, [96, NT], F32) for i in range(3)]
        t2 = [k.sb("t2_%d" % i, [96, NT], F32) for i in range(3)]
        qn = [k.sb("qn%d" % i, [96, NT], BF16) for i in range(3)]
        qm_st = k.sb("qm_st", [96, 6, NT], BF16)
        kn_st = k.sb("kn_st", [128, 3, NT], BF16)
        vs_st = k.sb("vs_st", [128, 4, 130], BF16)
        vm_st = k.sb("vm_st", [128, 4, 390], BF16)
        k.MEMSET("pool", vs_st[:, :, :], 1.0, [vs_st.b])
        k.MEMSET("pool", vm_st[:, :, :], 1.0, [vm_st.b])
        csq = k.dts["csq"]
        csk = k.dts["csk"]
        xin = k.in_x
        B64 = k.cb[:, CB_B64:CB_B64 + 128]

        for j in range(T // NT):
            c0 = j * NT
            xt = xT[j % 2]
            hx = j // 4
            ch = c0 - hx * (T // 2)
            if xin_mode == "transpose":
                for tb in range(4):
                    r0 = c0 + tb * 128
                    k.DMA(xtok[:, tb, :], xin.ap()[r0:r0 + 128, :], [xin.b()], [xtok.b], "xtok")
                for kc in range(8):
                    pb = k.bank()
                    for tb in range(4):
                        k.TR(pb[:, tb * 128:(tb + 1) * 128], xtok[:, tb, kc * 128:(kc + 1) * 128],
                             k.cf[:, CF_ID:CF_ID + 128], [xtok.b, k.cf.b], [pb.b])
                    k.CP("act" if kc % 2 == 0 else "dve", xt[:, kc, :], pb[:, :], [pb.b], [xt.b])
                k.DMA(xT_dst.ap()[:, :, c0:c0 + NT].rearrange("k p t -> p k t"), xt[:, :, :], [xt.b],
                      [xT_dst.b(j)], "xTst%d" % (j % 2))
            else:
                k.DMA(xt[:, :, :], xT_src.ap()[:, :, c0:c0 + NT].rearrange("k p t -> p k t"),
                      [xT_src.b(j)], [xt.b], "xTld%d" % (j % 2))
            k.DMA(csk_t[:, :, :], csk.ap()[:, :, c0:c0 + NT].rearrange("a p t -> p a t"), [csk.b()], [csk_t.b], "cskld")
            k.DMA(csq_t[:, :, :], csq.ap()[:, :, c0:c0 + NT].rearrange("a p t -> p a t"), [csq.b()], [csq_t.b], "csqld")
            for kc in range(8):
                if kc % 2 == 0:
                    k.ACT(sqb[:, kc, :], xt[:, kc, :], AF.Square, [xt.b], [sqb.b])
                else:
                    k.TT("pool", sqb[:, kc, :], xt[:, kc, :], xt[:, kc, :], ALU.mult, [xt.b], [sqb.b])
            pb = k.bank()
            for kc in range(8):
                k.MM(pb[:, 0:NT], k.cb[:, CB_ONESD:CB_ONESD + 128], sqb[:, kc, :], [k.cb.b, sqb.b], [pb.b],
                     start=(kc == 0), stop=(kc == 7))
            rsqrt_from_ms(k, pb[:, 0:NT], 128, NT, [pb.b], lnt, rstd[:, :], [rstd.b])
            for kc in range(8):
                k.STT("dve", hT[:, kc, :], xt[:, kc, :], gA[:, kc:kc + 1], rstd[:, :], ALU.mult, ALU.mult,
                      [xt.b, rstd.b, gA.b], [hT.b])

            def proj_fn(col0, M):
                def f(pbx):
                    for kc in range(8):
                        k.MM(pbx[0:M, 0:NT], win[:, kc, col0:col0 + M], hT[:, kc, :], [win.b, hT.b], [pbx.b],
                             start=(kc == 0), stop=(kc == 7))
                return f

            run_batch(k, [dict(mm=proj_fn(c * 128, 128), M=128, bmat=B64, gain=gl[:, 0:1], gainR=[gl.b],
                               out=qs_st[:, c, :], outW=[qs_st.b], ms="q%d" % c) for c in range(3)],
                      NT, sqg, lng, rsg)
            k.DMA(LT["qsT"].ap()[:, c0:c0 + NT].rearrange("(c p) t -> p c t", p=128), qs_st[:, :, :], [qs_st.b],
                  [LT["qsT"].b()], "qsst")
            pv = k.bank()
            for tb in range(4):
                for kc in range(8):
                    k.MM(pv[:, tb * 128:(tb + 1) * 128], hT[:, kc, tb * 128:(tb + 1) * 128], win[:, kc, 512:640],
                         [win.b, hT.b], [pv.b], start=(kc == 0), stop=(kc == 7))
            k.CP("act", vs_st[:, :, :].rearrange("p t (h e) -> p t h e", h=2)[:, :, :, 0:64],
                 pv[:, :].rearrange("p (t h d) -> p t h d", t=4, h=2), [pv.b], [vs_st.b])
            k.DMA(LT["vs"].ap()[128 + c0:128 + c0 + NT, :].rearrange("(t p) e -> p t e", p=128), vs_st[:, :, :],
                  [vs_st.b], [LT["vs"].b()], "vsst")
            run_batch(k, [dict(mm=proj_fn(384, 128), M=128, bmat=B64, gain=gl[:, 1:2], gainR=[gl.b],
                               out=ks_st[:, :], outW=[ks_st.b], ms="k"),
                          dict(mm=proj_fn(896, 128), M=128, bmat=k.cb[:, CB_ONES256:CB_ONES256 + 128],
                               gain=gA[:, 8:9], gainR=[gA.b], out=cqn[:, 0, :], outW=[cqn.b], ms="cq"),
                          dict(mm=proj_fn(1024, 128), M=128, bmat=k.cb[:, CB_ONES256:CB_ONES256 + 128],
                               gain=gA[:, 9:10], gainR=[gA.b], out=cqn[:, 1, :], outW=[cqn.b], ms="cq")],
                      NT, sqg, lng, rsg)
            k.DMA(LT["ksT"].ap()[:, 128 + c0:128 + c0 + NT], ks_st[:, :], [ks_st.b], [LT["ksT"].b()], "ksst")
            for c in range(2):
                pu = k.bank()
                proj_fn(640 + c * 128, 128)(pu)
                k.CP("act" if c == 0 else "dve", u_st[:, c, :], pu[:, 0:NT], [pu.b], [u_st.b])
            k.DMA(LT["uT"].ap()[:, 16 + c0:16 + c0 + NT].rearrange("(c p) t -> p c t", p=128), u_st[:, :, :],
                  [u_st.b], [LT["uT"].b()], "ust")
            run_batch(k, [dict(mm=proj_fn(1152, 128), M=128, bmat=k.cb[:, CB_ONES128:CB_ONES128 + 128],
                               gain=gA[:, 10:11], gainR=[gA.b], out=ckvn[:, :], outW=[ckvn.b], ms="ckv"),
                          dict(mm=proj_fn(1280, 32), M=32, bmat=k.cb[0:32, CB_ONES32:CB_ONES32 + 32],
                               gain=gl[0:32, 4:5], gainR=[gl.b], out=krn[:, :], outW=[krn.b], ms="kr")],
                      NT, sqg, lng, rsg)
            prot = k.bank()
            k.MM(prot[0:32, 0:NT], k.cb[0:32, CB_P32:CB_P32 + 32], krn[:, :], [k.cb.b, krn.b], [prot.b])
            k.TT("dve", t1[0][0:32, :], prot[0:32, 0:NT], csk_t[:, 1, :], ALU.mult, [prot.b, csk_t.b], [t1[0].b])
            k.TT("pool", t2[0][0:32, :], krn[:, :], csk_t[:, 0, :], ALU.mult, [krn.b, csk_t.b], [t2[0].b])
            k.TT("pool", kr_st[:, :], t1[0][0:32, :], t2[0][0:32, :], ALU.add, [t1[0].b, t2[0].b], [kr_st.b])
            k.DMA(LT["ksend"][hx].ap()[384:416, ch:ch + NT], kr_st[:, :], [kr_st.b], [LT["ksend"][hx].b()], "krst")
            for hb in range(2):
                def qmm(h):
                    def f(ph):
                        for c in range(2):
                            k.MM(ph[0:96, 0:NT], wqb[:, c, h * 96:(h + 1) * 96], cqn[:, c, :], [wqb.b, cqn.b], [ph.b],
                                 start=(c == 0), stop=(c == 1))
                    return f
                run_batch(k, [dict(mm=qmm(hb * 3 + a), M=96, bmat=k.cb[0:96, CB_B96:CB_B96 + 96],
                                   gain=gl[0:96, 3:4], gainR=[gl.b], out=qn[a][:, :], outW=[qn[a].b], ms="qm%d" % a)
                              for a in range(3)], NT, sqg, lng, rsg)
                prots = []
                for a in range(3):
                    prot = k.bank()
                    k.MM(prot[0:96, 0:NT], k.cb[0:96, CB_P96:CB_P96 + 96], qn[a][:, :], [k.cb.b, qn[a].b], [prot.b])
                    prots.append(prot)
                for a in range(3):
                    k.TT("dve", t1[a][:, :], prots[a][0:96, 0:NT], csq_t[:, 1, :], ALU.mult, [prots[a].b, csq_t.b], [t1[a].b])
                    k.TT("pool", t2[a][:, :], qn[a][:, :], csq_t[:, 0, :], ALU.mult, [qn[a].b, csq_t.b], [t2[a].b])
                for a in range(3):
                    k.TT("pool", qm_st[:, hb * 3 + a, :], t1[a][:, :], t2[a][:, :], ALU.add, [t1[a].b, t2[a].b], [qm_st.b])
            k.DMA(LT["qmT"].ap()[:, :, c0:c0 + NT].rearrange("h p t -> p h t"), qm_st[:, :, :], [qm_st.b],
                  [LT["qmT"].b()], "qmst")
            def knmm(c):
                def f(pn):
                    k.MM(pn[:, 0:NT], wkn[:, c * 128:(c + 1) * 128], ckvn[:, :], [wkn.b, ckvn.b], [pn.b])
                return f
            run_batch(k, [dict(mm=knmm(c), M=128, bmat=B64, gain=gl[:, 2:3], gainR=[gl.b],
                               out=kn_st[:, c, :], outW=[kn_st.b], ms="kn%d" % c) for c in range(3)],
                      NT, sqg, lng, rsg)
            k.DMA(LT["ksend"][hx].ap()[0:384, ch:ch + NT].rearrange("(c p) t -> p c t", p=128), kn_st[:, :, :],
                  [kn_st.b], [LT["ksend"][hx].b()], "knst")
            for tb in range(4):
                pvm = k.bank()
                k.MM(pvm[:, 0:384], ckvn[:, tb * 128:(tb + 1) * 128], wv[:, :], [ckvn.b, wv.b], [pvm.b])
                k.CP("act" if tb % 2 == 0 else "dve",
                     vm_st[:, tb, :].rearrange("p (h e) -> p h e", h=6)[:, :, 0:64],
                     pvm[:, 0:384].rearrange("p (h d) -> p h d", h=6), [pvm.b], [vm_st.b])
            k.DMA(LT["vsend"][hx].ap()[ch:ch + NT, :].rearrange("(t p) e -> p t e", p=128), vm_st[:, :, :],
                  [vm_st.b], [LT["vsend"][hx].b()], "vmst")
        k.DMA(LT["ssend"].ap()[:, 0:128], LT["ksT"].ap()[:, T:T + 128], [LT["ksT"].b()], [LT["ssend"].b()], "xcp")
        k.DMA(LT["ssend"].ap()[:, 128:258], LT["vs"].ap()[T:T + 128, :], [LT["vs"].b()], [LT["ssend"].b()], "xcp")
        k.DMA(LT["usend"].ap()[:, :], LT["uT"].ap()[:, T:T + 16], [LT["uT"].b()], [LT["usend"].b()], "xcp")
    k.P.barrier()


class Item:
    def __init__(self, name, p1=None, p2=None, p3=None, needs=()):
        self.name = name
        self.st = [p1, p2, p3]
        self.needs = list(needs)


def run_pipeline(items):
    pos = {it.name: i for i, it in enumerate(items)}
    for i, it in enumerate(items):
        for n_ in it.needs:
            assert pos[n_] <= i - 2, (it.name, n_, pos[n_], i)
    n = len(items)
    for t in range(n + 2):
        if 0 <= t - 2 < n and items[t - 2].st[2] is not None:
            items[t - 2].st[2]()
        if t < n and items[t].st[0] is not None:
            items[t].st[0]()
        if 0 <= t - 1 < n and items[t - 1].st[1] is not None:
            items[t - 1].st[1]()


def phase_A2(k, l, xin_mode, xT_src, xT_dst, LT):
    W = k.W
    NT = 512
    with contextlib.ExitStack() as es:
        k.es = es
        win = k.sb("win", [128, 8, IN_DIM], BF16)
        wqb = k.sb("wqb", [128, 2, 576], BF16)
        wkn = k.sb("wkn", [128, 384], BF16)
        wv = k.sb("wv", [128, 384], BF16)
        gA = k.sb("gA", [128, 16], F32)
        gl = k.sb("gl", [128, 8], F32)
        an = W["attn_norm"]
        k.DMA(gA[:, 0:8], an.raw(l * 1024, [[1, 128], [128, 8]]), [an.b()], [gA.b], "ldg", allow_slow_non_contiguous=True)
        qa = W["mla_q_a_gain"]
        k.DMA(gA[:, 8:10], qa.raw(l * 256, [[1, 128], [128, 2]]), [qa.b()], [gA.b], "ldg", allow_slow_non_contiguous=True)
        kva = W["mla_kv_a_gain"]
        k.DMA(gA[:, 10:11], kva.raw(l * 128, [[1, 128], [1, 1]]), [kva.b()], [gA.b], "ldg")
        col_gain(k, gl, 0, W["swa_q_gain"], l, 64, 2, "ldg")
        col_gain(k, gl, 1, W["swa_k_gain"], l, 64, 2, "ldg")
        col_gain(k, gl, 2, W["mla_k_nope_gain"], l, 64, 2, "ldg")
        k.DMA(gl[0:64, 3:4], W["mla_q_nope_gain"].raw(l * 64, [[1, 64], [1, 1]]), [W["mla_q_nope_gain"].b()], [gl.b], "ldg")
        k.DMA(gl[64:96, 3:4], W["mla_q_rope_gain"].raw(l * 32, [[1, 32], [1, 1]]), [W["mla_q_rope_gain"].b()], [gl.b], "ldg")
        k.DMA(gl[0:32, 4:5], W["mla_k_rope_gain"].raw(l * 32, [[1, 32], [1, 1]]), [W["mla_k_rope_gain"].b()], [gl.b], "ldg")
        wi = W["w_in"]
        for c in range(8):
            k.DMA(win[:, c, :], wi.ap()[l, c * 128:(c + 1) * 128, :], [wi.b()], [win.b], "wld", q="pool")
        wq = W["mla_w_qb"]
        for c in range(2):
            k.DMA(wqb[:, c, :], wq.ap()[l, c * 128:(c + 1) * 128, :], [wq.b()], [wqb.b], "wld", q="pool")
        wk = W["mla_w_kvb"]
        wkv_v = wk.ap()[l, :, :].rearrange("p (h t d) -> p t h d", h=6, t=2, d=64)
        k.DMA(wkn[:, :].rearrange("p (h d) -> p h d", h=6), wkv_v[:, 0, :, :], [wk.b()], [wkn.b], "wld", q="pool")
        k.DMA(wv[:, :].rearrange("p (h d) -> p h d", h=6), wkv_v[:, 1, :, :], [wk.b()], [wv.b], "wld", q="pool")

        xtok = [k.sb("xtok%d" % i, [128, 4, D], F32) for i in range(2)] if xin_mode == "transpose" else None
        xT = [k.sb("xT%d" % i, [128, 8, NT], F32) for i in range(1 if xin_mode == "transpose" else 2)]
        sqb = [k.sb("sqb%d" % i, [128, 8, NT], BF16) for i in range(1)]
        hT = [k.sb("hT%d" % i, [128, 8, NT], BF16) for i in range(2)]
        lnt = k.sb("lnt", [128, NT], F32)
        rstd = k.sb("rstd", [128, NT], F32)
        sqg = [k.sb("sqg%d" % i, [128, NT], BF16) for i in range(4)]
        lng = [k.sb("lng%d" % i, [128, NT], F32) for i in range(4)]
        rsg = [k.sb("rsg%d" % i, [128, NT], F32) for i in range(4)]
        qs_st = k.sb("qs_st", [128, 3, NT], BF16)
        ks_st = k.sb("ks_st", [128, NT], BF16)
        u_st = k.sb("u_st", [128, 2, NT], F32)
        cqn = [k.sb("cqn%d" % i, [128, 2, NT], BF16) for i in range(2)]
        ckvn = [k.sb("ckvn%d" % i, [128, NT], BF16) for i in range(2)]
        krn = [k.sb("krn%d" % i, [32, NT], BF16) for i in range(2)]
        kr_st = k.sb("kr_st", [32, NT], BF16)
        csk_t = [k.sb("csk_t%d" % i, [32, 2, NT], F32) for i in range(2)]
        csq_t = [k.sb("csq_t%d" % i, [96, 2, NT], F32) for i in range(2)]
        t1 = [k.sb("t1_%d" % i, [96, NT], F32) for i in range(4)]
        t2 = [k.sb("t2_%d" % i, [96, NT], F32) for i in range(4)]
        qn = [k.sb("qn%d" % i, [96, NT], BF16) for i in range(6)]
        qm_st = k.sb("qm_st", [96, 6, NT], BF16)
        kn_st = k.sb("kn_st", [128, 3, NT], BF16)
        vs_st = k.sb("vs_st", [128, 4, 130], BF16)
        vm_st = k.sb("vm_st", [128, 4, 390], BF16)
        k.MEMSET("pool", vs_st[:, :, :], 1.0, [vs_st.b])
        k.MEMSET("pool", vm_st[:, :, :], 1.0, [vm_st.b])
        csq = k.dts["csq"]
        csk = k.dts["csk"]
        xin = k.in_x
        B64 = k.cb[:, CB_B64:CB_B64 + 128]
        IDN = k.cf[:, CF_ID:CF_ID + 128]
        items = []
        fronts = []
        mains = []
        nrm_ctr = [0]
        krt = k.sb("krt", [32, NT], F32)

        def norm_item(name, groups, needs, after_p3=None):
            par = nrm_ctr[0] % 2
            nrm_ctr[0] += 1
            st = {}

            def p1():
                st["pbs"] = []
                for g in groups:
                    pb = k.balloc()
                    g["mm"](pb)
                    st["pbs"].append(pb)

            def p2():
                for i, g in enumerate(groups):
                    M = g["M"]
                    sq = sqg[par * 2 + i]
                    k.ACT(sq[0:M, 0:NT], st["pbs"][i][0:M, 0:NT], AF.Square, [st["pbs"][i].b], [sq.b])
                keys = []
                for g in groups:
                    if g["ms"] not in keys:
                        keys.append(g["ms"])
                st["rs"] = {}
                for ki, key in enumerate(keys):
                    mem = [i for i, g in enumerate(groups) if g["ms"] == key]
                    M = groups[mem[0]]["M"]
                    pm = k.balloc()
                    for n_, i in enumerate(mem):
                        sq = sqg[par * 2 + i]
                        k.MM(pm[0:M, 0:NT], groups[i]["bmat"], sq[0:M, 0:NT], [sq.b, k.cb.b], [pm.b],
                             start=(n_ == 0), stop=(n_ == len(mem) - 1))
                    st["rs"][key] = (par * 2 + ki, pm, M)

            def p3():
                for key, (si, pm, M) in st["rs"].items():
                    k.ACT(lng[si][0:M, 0:NT], pm[0:M, 0:NT], AF.Ln, [pm.b], [lng[si].b], bias=k.epsb[0:M, 0:1], scale=1.0)
                for key, (si, pm, M) in st["rs"].items():
                    k.ACT(rsg[si][0:M, 0:NT], lng[si][0:M, 0:NT], AF.Exp, [lng[si].b], [rsg[si].b], scale=-0.5)
                for i, g in enumerate(groups):
                    si, pm, M = st["rs"][g["ms"]]
                    k.STT("dve", g["out"], st["pbs"][i][0:M, 0:NT], g["gain"], rsg[si][0:M, 0:NT], ALU.mult, ALU.mult,
                          [st["pbs"][i].b, rsg[si].b] + list(g["gainR"]), g["outW"])
                for pb in st["pbs"]:
                    k.bfree(pb)
                for key, (si, pm, M) in st["rs"].items():
                    k.bfree(pm)
                if after_p3 is not None:
                    after_p3()
            items.append(Item(name, p1, p2, p3, needs))

        def load_x(jj):
            cc = jj * NT
            if xin_mode == "transpose":
                xk_ = xtok[jj % 2]
                for tb in range(4):
                    r0 = cc + tb * 128
                    k.DMA(xk_[:, tb, :], xin.ap()[r0:r0 + 128, :], [xin.b()], [xk_.b], "xtok%d" % (jj % 2))
            else:
                xx = xT[jj % 2]
                k.DMA(xx[:, :, :], xT_src.ap()[:, :, cc:cc + NT].rearrange("k p t -> p k t"),
                      [xT_src.b(jj)], [xx.b], "xTld%d" % (jj % 2))

        load_x(0)
        for j in range(T // NT):
            items = []
            c0 = j * NT
            jp = j % 2
            xt = xT[0] if xin_mode == "transpose" else xT[jp]
            h_ = hT[jp]
            sq_ = sqb[0]
            hx = j // 4
            ch = c0 - hx * (T // 2)
            T_ = "t%d_" % j

            if xin_mode == "transpose":
                xk = xtok[jp]
                for pair in range(4):
                    st = {}

                    def p1(pair=pair, st=st, xk=xk, c0=c0, j=j):
                        if pair == 0 and j + 1 < T // NT:
                            load_x(j + 1)
                        st["pb"] = []
                        for kc in (2 * pair, 2 * pair + 1):
                            pb = k.balloc()
                            for tb in range(4):
                                k.TR(pb[:, tb * 128:(tb + 1) * 128], xk[:, tb, kc * 128:(kc + 1) * 128], IDN,
                                     [xk.b, k.cf.b], [pb.b])
                            st["pb"].append(pb)

                    def p2(pair=pair, st=st, xt=xt, sq_=sq_, j=j, c0=c0):
                        for n_, kc in enumerate((2 * pair, 2 * pair + 1)):
                            k.CP("act" if n_ == 0 else "dve", xt[:, kc, :], st["pb"][n_][:, :], [st["pb"][n_].b], [xt.b])
                            k.bfree(st["pb"][n_])
                        for n_, kc in enumerate((2 * pair, 2 * pair + 1)):
                            if n_ == 0:
                                k.ACT(sq_[:, kc, :], xt[:, kc, :], AF.Square, [xt.b], [sq_.b])
                            else:
                                k.TT("pool", sq_[:, kc, :], xt[:, kc, :], xt[:, kc, :], ALU.mult, [xt.b], [sq_.b])
                        if pair == 3:
                            k.DMA(xT_dst.ap()[:, :, c0:c0 + NT].rearrange("k p t -> p k t"), xt[:, :, :], [xt.b],
                                  [xT_dst.b(j)], "xTst0")
                    items.append(Item(T_ + "F%d" % pair, p1, p2, None))
            else:
                for pair in range(4):
                    def p1(pair=pair, xt=xt, j=j, c0=c0):
                        if pair == 0 and j + 1 < T // NT:
                            load_x(j + 1)

                    def p2(pair=pair, xt=xt, sq_=sq_):
                        for n_, kc in enumerate((2 * pair, 2 * pair + 1)):
                            if n_ == 0:
                                k.ACT(sq_[:, kc, :], xt[:, kc, :], AF.Square, [xt.b], [sq_.b])
                            else:
                                k.TT("pool", sq_[:, kc, :], xt[:, kc, :], xt[:, kc, :], ALU.mult, [xt.b], [sq_.b])
                    items.append(Item(T_ + "F%d" % pair, p1, p2, None))
            st5 = {}

            def f5p1(st5=st5, sq_=sq_, jp=jp, c0=c0):
                k.DMA(csk_t[jp][:, :, :], csk.ap()[:, :, c0:c0 + NT].rearrange("a p t -> p a t"), [csk.b()],
                      [csk_t[jp].b], "cskld%d" % jp)
                k.DMA(csq_t[jp][:, :, :], csq.ap()[:, :, c0:c0 + NT].rearrange("a p t -> p a t"), [csq.b()],
                      [csq_t[jp].b], "csqld%d" % jp)
                pb = k.balloc()
                for kc in range(8):
                    k.MM(pb[:, 0:NT], k.cb[:, CB_ONESD:CB_ONESD + 128], sq_[:, kc, :], [k.cb.b, sq_.b], [pb.b],
                         start=(kc == 0), stop=(kc == 7))
                st5["pb"] = pb

            def f5p2(st5=st5):
                rsqrt_from_ms(k, st5["pb"][:, 0:NT], 128, NT, [st5["pb"].b], lnt, rstd[:, :], [rstd.b])
                k.bfree(st5["pb"])

            def f5p3(xt=xt, h_=h_):
                for kc in range(8):
                    k.STT("dve", h_[:, kc, :], xt[:, kc, :], gA[:, kc:kc + 1], rstd[:, :], ALU.mult, ALU.mult,
                          [xt.b, rstd.b, gA.b], [h_.b])
            items.append(Item(T_ + "F5", f5p1, f5p2, f5p3, needs=[T_ + "F%d" % p for p in range(4)]))
            fronts.append(items)
            items = []

            def proj_fn(col0, M, h_=h_):
                def f(pbx):
                    for kc in range(8):
                        k.MM(pbx[0:M, 0:NT], win[:, kc, col0:col0 + M], h_[:, kc, :], [win.b, h_.b], [pbx.b],
                             start=(kc == 0), stop=(kc == 7))
                return f

            cq_ = cqn[jp]
            ckv_ = ckvn[jp]
            kr_ = krn[jp]
            O256 = k.cb[:, CB_ONES256:CB_ONES256 + 128]
            norm_item(T_ + "cq", [dict(mm=proj_fn(896, 128), M=128, bmat=O256, gain=gA[:, 8:9], gainR=[gA.b],
                                       out=cq_[:, 0, :], outW=[cq_.b], ms="cq"),
                                  dict(mm=proj_fn(1024, 128), M=128, bmat=O256, gain=gA[:, 9:10], gainR=[gA.b],
                                       out=cq_[:, 1, :], outW=[cq_.b], ms="cq")], [T_ + "F5"])
            norm_item(T_ + "ckvkr", [dict(mm=proj_fn(1152, 128), M=128, bmat=k.cb[:, CB_ONES128:CB_ONES128 + 128],
                                          gain=gA[:, 10:11], gainR=[gA.b], out=ckv_[:, :], outW=[ckv_.b], ms="ckv"),
                                     dict(mm=proj_fn(1280, 32), M=32, bmat=k.cb[0:32, CB_ONES32:CB_ONES32 + 32],
                                          gain=gl[0:32, 4:5], gainR=[gl.b], out=kr_[:, :], outW=[kr_.b], ms="kr")],
                      [T_ + "F5"])
            norm_item(T_ + "q01", [dict(mm=proj_fn(c * 128, 128), M=128, bmat=B64, gain=gl[:, 0:1], gainR=[gl.b],
                                        out=qs_st[:, c, :], outW=[qs_st.b], ms="q%d" % c) for c in range(2)],
                      [T_ + "F5"])

            def st_qk(c0=c0):
                k.DMA(LT["qsT"].ap()[:, c0:c0 + NT].rearrange("(c p) t -> p c t", p=128), qs_st[:, :, :], [qs_st.b],
                      [LT["qsT"].b()], "qsst")
                k.DMA(LT["ksT"].ap()[:, 128 + c0:128 + c0 + NT], ks_st[:, :], [ks_st.b], [LT["ksT"].b()], "ksst")
            norm_item(T_ + "q2k", [dict(mm=proj_fn(256, 128), M=128, bmat=B64, gain=gl[:, 0:1], gainR=[gl.b],
                                        out=qs_st[:, 2, :], outW=[qs_st.b], ms="q2"),
                                   dict(mm=proj_fn(384, 128), M=128, bmat=B64, gain=gl[:, 1:2], gainR=[gl.b],
                                        out=ks_st[:, :], outW=[ks_st.b], ms="k")], [T_ + "F5"], after_p3=st_qk)
            stu = {}

            def up1(stu=stu, proj_fn=proj_fn):
                stu["pb"] = []
                for c in range(2):
                    pu = k.balloc()
                    proj_fn(640 + c * 128, 128)(pu)
                    stu["pb"].append(pu)

            def up2(stu=stu, c0=c0):
                for c in range(2):
                    k.CP("act" if c == 0 else "dve", u_st[:, c, :], stu["pb"][c][:, 0:NT], [stu["pb"][c].b], [u_st.b])
                    k.bfree(stu["pb"][c])
                k.DMA(LT["uT"].ap()[:, 16 + c0:16 + c0 + NT].rearrange("(c p) t -> p c t", p=128), u_st[:, :, :],
                      [u_st.b], [LT["uT"].b()], "ust")
            items.append(Item(T_ + "u", up1, up2, None, needs=[T_ + "F5"]))
            for hb in range(3):
                def qmm(h, cq_=cq_):
                    def f(ph):
                        for c in range(2):
                            k.MM(ph[0:96, 0:NT], wqb[:, c, h * 96:(h + 1) * 96], cq_[:, c, :], [wqb.b, cq_.b], [ph.b],
                                 start=(c == 0), stop=(c == 1))
                    return f
                norm_item(T_ + "qm%d" % hb,
                          [dict(mm=qmm(hb * 2 + a), M=96, bmat=k.cb[0:96, CB_B96:CB_B96 + 96], gain=gl[0:96, 3:4],
                                gainR=[gl.b], out=qn[hb * 2 + a][:, :], outW=[qn[hb * 2 + a].b], ms="qm%d" % a)
                           for a in range(2)], [T_ + "cq"])
            def knmm(c, ckv_=ckv_):
                def f(pn):
                    k.MM(pn[:, 0:NT], wkn[:, c * 128:(c + 1) * 128], ckv_[:, :], [wkn.b, ckv_.b], [pn.b])
                return f
            norm_item(T_ + "kn01", [dict(mm=knmm(c), M=128, bmat=B64, gain=gl[:, 2:3], gainR=[gl.b],
                                         out=kn_st[:, c, :], outW=[kn_st.b], ms="kn%d" % c) for c in range(2)],
                      [T_ + "ckvkr"])

            def st_kn(hx=hx, ch=ch):
                k.DMA(LT["ksend"][hx].ap()[0:384, ch:ch + NT].rearrange("(c p) t -> p c t", p=128), kn_st[:, :, :],
                      [kn_st.b], [LT["ksend"][hx].b()], "knst")
            norm_item(T_ + "kn2", [dict(mm=knmm(2), M=128, bmat=B64, gain=gl[:, 2:3], gainR=[gl.b],
                                        out=kn_st[:, 2, :], outW=[kn_st.b], ms="kn2")], [T_ + "ckvkr"], after_p3=st_kn)
            for ri in range(3):
                strp = {}

                def rp1(ri=ri, strp=strp, kr_=kr_):
                    strp["pb"] = []
                    for a in range(2):
                        prot = k.balloc()
                        q_ = qn[ri * 2 + a]
                        k.MM(prot[0:96, 0:NT], k.cb[0:96, CB_P96:CB_P96 + 96], q_[:, :], [k.cb.b, q_.b], [prot.b])
                        strp["pb"].append(prot)
                    if ri == 0:
                        prot = k.balloc()
                        k.MM(prot[0:32, 0:NT], k.cb[0:32, CB_P32:CB_P32 + 32], kr_[:, :], [k.cb.b, kr_.b], [prot.b])
                        strp["kr"] = prot

                def rp2(ri=ri, strp=strp, kr_=kr_, jp=jp):
                    for a in range(2):
                        q_ = qn[ri * 2 + a]
                        ts = (ri % 2) * 2 + a
                        k.TT("dve", t1[ts][:, :], strp["pb"][a][0:96, 0:NT], csq_t[jp][:, 1, :], ALU.mult,
                             [strp["pb"][a].b, csq_t[jp].b], [t1[ts].b])
                        k.TT("pool", t2[ts][:, :], q_[:, :], csq_t[jp][:, 0, :], ALU.mult, [q_.b, csq_t[jp].b], [t2[ts].b])
                        k.bfree(strp["pb"][a])
                    if ri == 0:
                        k.TT("dve", krt[:, :], strp["kr"][0:32, 0:NT], csk_t[jp][:, 1, :], ALU.mult,
                             [strp["kr"].b, csk_t[jp].b], [krt.b])
                        k.bfree(strp["kr"])

                def rp3(ri=ri, kr_=kr_, jp=jp, hx=hx, ch=ch, c0=c0):
                    for a in range(2):
                        ts = (ri % 2) * 2 + a
                        k.TT("pool", qm_st[:, ri * 2 + a, :], t1[ts][:, :], t2[ts][:, :], ALU.add, [t1[ts].b, t2[ts].b],
                             [qm_st.b])
                    if ri == 0:
                        k.STT("dve", kr_st[:, :], kr_[:, :], 1.0, csk_t[jp][:, 0, :], ALU.mult, ALU.mult,
                              [kr_.b, csk_t[jp].b], [kr_st.b])
                        k.TT("dve", kr_st[:, :], kr_st[:, :], krt[:, :], ALU.add, [kr_st.b, krt.b], [kr_st.b])
                        k.DMA(LT["ksend"][hx].ap()[384:416, ch:ch + NT], kr_st[:, :], [kr_st.b], [LT["ksend"][hx].b()],
                              "krst")
                    if ri == 2:
                        k.DMA(LT["qmT"].ap()[:, :, c0:c0 + NT].rearrange("h p t -> p h t"), qm_st[:, :, :], [qm_st.b],
                              [LT["qmT"].b()], "qmst")
                needs = [T_ + "qm%d" % ri] + ([T_ + "ckvkr"] if ri == 0 else [])
                items.append(Item(T_ + "rope%d" % ri, rp1, rp2, rp3, needs=needs))
            stv = {}

            def vp1(stv=stv, ckv_=ckv_):
                stv["pb"] = []
                for tb in range(4):
                    pvm = k.balloc()
                    k.MM(pvm[:, 0:384], ckv_[:, tb * 128:(tb + 1) * 128], wv[:, :], [ckv_.b, wv.b], [pvm.b])
                    stv["pb"].append(pvm)

            def vp2(stv=stv, hx=hx, ch=ch):
                for tb in range(4):
                    k.CP("act" if tb % 2 == 0 else "dve",
                         vm_st[:, tb, :].rearrange("p (h e) -> p h e", h=6)[:, :, 0:64],
                         stv["pb"][tb][:, 0:384].rearrange("p (h d) -> p h d", h=6), [stv["pb"][tb].b], [vm_st.b])
                    k.bfree(stv["pb"][tb])
                k.DMA(LT["vsend"][hx].ap()[ch:ch + NT, :].rearrange("(t p) e -> p t e", p=128), vm_st[:, :, :],
                      [vm_st.b], [LT["vsend"][hx].b()], "vmst")
            items.append(Item(T_ + "vm", vp1, vp2, None, needs=[T_ + "ckvkr"]))
            stw = {}

            def wp1(stw=stw, h_=h_):
                pv = k.balloc()
                for tb in range(4):
                    for kc in range(8):
                        k.MM(pv[:, tb * 128:(tb + 1) * 128], h_[:, kc, tb * 128:(tb + 1) * 128], win[:, kc, 512:640],
                             [win.b, h_.b], [pv.b], start=(kc == 0), stop=(kc == 7))
                stw["pb"] = pv

            def wp2(stw=stw, c0=c0):
                pv = stw["pb"]
                k.CP("act", vs_st[:, :, :].rearrange("p t (h e) -> p t h e", h=2)[:, :, :, 0:64],
                     pv[:, :].rearrange("p (t h d) -> p t h d", t=4, h=2), [pv.b], [vs_st.b])
                k.bfree(pv)
                k.DMA(LT["vs"].ap()[128 + c0:128 + c0 + NT, :].rearrange("(t p) e -> p t e", p=128), vs_st[:, :, :],
                      [vs_st.b], [LT["vs"].b()], "vsst")
            items.append(Item(T_ + "vs", wp1, wp2, None, needs=[T_ + "F5"]))
            mains.append(items)
        NTL = T // NT
        seq = []
        f0 = fronts[0]
        seq += [f0[0], f0[1], f0[2], f0[3], Item("nop0a"), f0[4], Item("nop0b")]
        for j in range(NTL):
            m = {it.name.split("_", 1)[1]: it for it in mains[j]}
            f = fronts[j + 1] if j + 1 < NTL else None
            order = ["cq", "F0", "ckvkr", "F1", "q01", "F2", "q2k", "F3", "u", "qm0", "F5", "qm1", "qm2",
                     "kn01", "kn2", "rope0", "rope1", "rope2", "vm", "vs"]
            for nm in order:
                if nm[0] == "F":
                    if f is not None:
                        seq.append(f[{"F0": 0, "F1": 1, "F2": 2, "F3": 3, "F5": 4}[nm]])
                else:
                    seq.append(m[nm])
        run_pipeline(seq)
        assert len(k._bfree) == 8, k._bfree
        k.DMA(LT["ssend"].ap()[:, 0:128], LT["ksT"].ap()[:, T:T + 128], [LT["ksT"].b()], [LT["ssend"].b()], "xcp")
        k.DMA(LT["ssend"].ap()[:, 128:258], LT["vs"].ap()[T:T + 128, :], [LT["vs"].b()], [LT["ssend"].b()], "xcp")
        k.DMA(LT["usend"].ap()[:, :], LT["uT"].ap()[:, T:T + 16], [LT["uT"].b()], [LT["usend"].b()], "xcp")
    k.P.barrier()


def softmax_finish_a(k, ot, esink_ap, esink_R, rrow, use_act):
    if use_act:
        if esink_ap is not None:
            k.ACT(rrow[64:65, :], ot[64:65, 0:512], AF.Ln, [ot.b] + esink_R, [rrow.b], bias=esink_ap, scale=1.0)
        else:
            k.ACT(rrow[64:65, :], ot[64:65, 0:512], AF.Ln, [ot.b], [rrow.b])
        k.ACT(rrow[64:65, :], rrow[64:65, :], AF.Exp, [rrow.b], [rrow.b], scale=-1.0)
    else:
        if esink_ap is not None:
            k.TS("dve", rrow[64:65, :], ot[64:65, 0:512], esink_ap, None, ALU.add, None, [ot.b] + esink_R, [rrow.b])
            k.RECIP(rrow[64:65, :], rrow[64:65, :], [rrow.b], [rrow.b])
        else:
            k.RECIP(rrow[64:65, :], ot[64:65, 0:512], [ot.b], [rrow.b])


def softmax_finish_b(k, ot, out_st, out_W, rrow, bcs, pb):
    k.MM(pb[0:64, 0:512], k.cf[64:65, CF_ONES:CF_ONES + 64], rrow[64:65, :], [k.cf.b, rrow.b], [pb.b])
    k.CP("act", bcs[:, :], pb[0:64, 0:512], [pb.b], [bcs.b])
    k.TT("dve", out_st, ot[0:64, 0:512], bcs[:, :], ALU.mult, [ot.b, bcs.b], out_W)


def phase_B_swa(k, l, LT, mixT):
    W = k.W
    with contextlib.ExitStack() as es:
        k.es = es
        qT = [k.sb("sqT%d" % i, [64, T], BF16) for i in range(6)]
        kT = [k.sb("skT%d" % i, [64, 128 + T], BF16) for i in range(2)]
        V = [k.sb("sV%d" % i, [128, NB + 1, 65], BF16) for i in range(2)]
        Tt = [k.sb("sTt%d" % i, [128, 2, 256], F32) for i in range(3)]
        esb = k.sb("esb", [128, 6], F32)
        tmp = [k.sb("stmp%d" % i, [128, 2, 256], F32) for i in range(3)]
        Pt = [k.sb("sP%d" % i, [128, 2, 256], BF16) for i in range(3)]
        rrow = [k.sb("srrow%d" % i, [65, 512], F32) for i in range(4)]
        bcs = k.sb("sbcs", [64, 512], F32)
        ost = [k.sb("sost%d" % i, [64, 512], BF16) for i in range(4)]
        sk = W["swa_sinks"]
        k.DMA(esb[:, :], sk.raw(l * 6, [[0, 128], [1, 6]]), [sk.b()], [esb.b], "ldg")
        k.ACT(esb[:, :], esb[:, :], AF.Exp, [esb.b], [esb.b])
        tv = k.dts["tv"]
        for kv in range(2):
            k.DMA(kT[kv][:, 128:], LT["ksT"].ap()[kv * 64:(kv + 1) * 64, 128:], [LT["ksT"].b()], [kT[kv].b], "skT%d" % kv)
            k.DMA(V[kv][:, 1:, :],
                  LT["vs"].ap()[128:, kv * 65:(kv + 1) * 65].rearrange("(b p) e -> p b e", p=128),
                  [LT["vs"].b()], [V[kv].b], "sV%d" % kv)
        for hq in range(6):
            k.DMA(qT[hq][:, :], LT["qsT"].ap()[hq * 64:(hq + 1) * 64, :], [LT["qsT"].b()], [qT[hq].b], "sqT%d" % hq)
            k.DMA(Tt[hq // 2][:, hq % 2, :], tv.raw(hq * 128 * 383 + 127, [[382, 128], [1, 256]]), [tv.b()],
                  [Tt[hq // 2].b], "sTt%d" % (hq // 2))
        for kv in range(2):
            k.DMA(kT[kv][:, 0:128], LT["srecv"].ap()[kv * 64:(kv + 1) * 64, 0:128], [LT["srecv"].b()], [kT[kv].b],
                  "skT%d" % kv)
            k.DMA(V[kv][:, 0, :], LT["srecv"].ap()[0:128, 128 + kv * 65:128 + (kv + 1) * 65], [LT["srecv"].b()],
                  [V[kv].b], "sV%d" % kv)
        SBK = [k.PB[0], k.PB[1], k.PB[2]]
        OT = [[k.PB[3], k.PB[4]], [k.PB[5], k.PB[6]]]
        BC = k.PB[7]
        steps = [(pi, kb) for pi in range(3) for kb in range(-1, NB)]
        LA = 2

        def geom(kb):
            qlo = max(kb, 0) * 128
            qhi = min(kb + 2, NB) * 128
            return qlo, qhi, qhi - qlo, (0 if kb >= 0 else 128)

        def S_emit(i):
            pi, kb = steps[i]
            qlo, qhi, N, tc = geom(kb)
            pb = SBK[i % 3]
            ks = kb + 1
            for e in range(2):
                hq = 2 * pi + e
                kv = hq // 3
                k.MM(pb[:, e * 256:e * 256 + N], kT[kv][:, ks * 128:(ks + 1) * 128], qT[hq][:, qlo:qhi],
                     [kT[kv].b, qT[hq].b], [pb.b])

        def finish_b(pi, qt):
            for e in range(2):
                hq = 2 * pi + e
                o = ost[e * 2 + qt % 2]
                softmax_finish_b(k, OT[e][qt % 2], o[:, :], [o.b], rrow[e * 2 + qt % 2], bcs, BC)
                k.DMA(mixT.ap()[hq * 64:(hq + 1) * 64, qt * 512:(qt + 1) * 512], o[:, :], [o.b],
                      [mixT.b("swa")], "sost%d" % (e * 2 + qt % 2))

        for i in range(min(LA, len(steps))):
            S_emit(i)
        pend = None
        for i, (pi, kb) in enumerate(steps):
            if i + LA < len(steps):
                S_emit(i + LA)
            ks = kb + 1
            qlo, qhi, N, tc = geom(kb)
            pb = SBK[i % 3]
            tm = tmp[i % 3]
            pt = Pt[i % 3]
            tt = Tt[pi]
            k.STT("dve", tm[:, :, 0:N], pb[:, :].rearrange("p (e n) -> p e n", e=2)[:, :, 0:N], 0.125,
                  tt[:, :, tc:tc + N], ALU.mult, ALU.add, [pb.b, tt.b], [tm.b])
            if kb == -1:
                k.ACT(pt[:, :, 0:N], tm[:, :, 0:N], AF.Exp, [tm.b, k.role.b], [pt.b], bias=k.role[:, 0:1])
            else:
                k.ACT(pt[:, :, 0:N], tm[:, :, 0:N], AF.Exp, [tm.b], [pt.b])
            for e in range(2):
                hq = 2 * pi + e
                kv = hq // 3
                if kb >= 0:
                    ot = OT[e][(kb // 4) % 2]
                    cc = (kb % 4) * 128
                    k.MM(ot[0:65, cc:cc + 128], V[kv][:, ks, :], pt[:, e, 0:128], [V[kv].b, pt.b], [ot.b],
                         start=False, stop=True)
                if kb + 1 < NB:
                    qn_ = kb + 1
                    ot2 = OT[e][(qn_ // 4) % 2]
                    cc = (qn_ % 4) * 128
                    k.MM(ot2[0:65, cc:cc + 128], V[kv][:, ks, :], pt[:, e, N - 128:N], [V[kv].b, pt.b], [ot2.b],
                         start=True, stop=False)
            if pend is not None and i >= pend[0]:
                _, ppi, pqt = pend
                pend = None
                finish_b(ppi, pqt)
            if kb >= 0 and kb % 4 == 3:
                qt = kb // 4
                for e in range(2):
                    hq = 2 * pi + e
                    softmax_finish_a(k, OT[e][qt % 2], esb[64:65, hq:hq + 1], [esb.b], rrow[e * 2 + qt % 2], True)
                pend = (i + 2, pi, qt)
        if pend is not None:
            finish_b(pend[1], pend[2])
    k.P.barrier()


def phase_B_pool(k, l, LT, mixT):
    W = k.W
    NT = 512
    with contextlib.ExitStack() as es:
        k.es = es
        pw32 = k.sb("pw32", [128, 2, 64], F32)
        pwbd = k.sb("pwbd", [128, 2, 128], BF16)
        psc = k.sb("psc", [128, 2], F32)
        ut = [k.sb("put%d" % i, [128, 2, 16 + NT], F32) for i in range(2)]
        s2s = [k.sb("ps2_%d" % i, [128, 2, 16 + NT], F32) for i in range(2)]
        s4s = [k.sb("ps4_%d" % i, [128, 2, 16 + NT], F32) for i in range(2)]
        s8s = [k.sb("ps8_%d" % i, [128, 2, 16 + NT], F32) for i in range(2)]
        s16s = [k.sb("ps16_%d" % i, [128, 2, 16 + NT], F32) for i in range(2)]
        dds = [k.sb("pdd%d" % i, [128, 2, NT], BF16) for i in range(2)]
        t16 = k.sb("pt16", [128, 16], F32)
        pst = [k.sb("ppst%d" % i, [128, 2, NT], BF16) for i in range(2)]
        pwd = W["pool_w"]
        k.DMA(pw32[:, :, :], pwd.raw(l * 4 * 64 * 64, [[64, 128], [128 * 64, 2], [1, 64]]), [pwd.b()], [pw32.b], "ldg")
        k.MEMSET("pool", pwbd[:, :, :], 0.0, [pwbd.b])
        for c in range(2):
            for half in range(2):
                sl = slice(half * 64, half * 64 + 64)
                k.CP("dve", pwbd[sl, c, half * 64:half * 64 + 64], pw32[sl, c, :], [pw32.b], [pwbd.b])
        ps_ = W["pool_scale"]
        k.DMA(psc[:, :], ps_.raw(l * 256, [[1, 128], [128, 2]]), [ps_.b()], [psc.b], "ldg", allow_slow_non_contiguous=True)
        wins = [2, 4, 8, 16]
        L = 16 + NT
        def load_u(jj):
            u_ = ut[jj % 2]
            cc = jj * NT
            if jj == 0:
                k.DMA(u_[:, :, 16:], LT["uT"].ap()[:, 16:L].rearrange("(c p) t -> p c t", p=128), [LT["uT"].b()],
                      [u_.b], "put0")
                k.DMA(u_[:, :, 0:16], LT["urecv"].ap()[0:256, :].rearrange("(c p) t -> p c t", p=128),
                      [LT["urecv"].b()], [u_.b], "put0")
            else:
                k.DMA(u_[:, :, :], LT["uT"].ap()[:, cc:cc + L].rearrange("(c p) t -> p c t", p=128),
                      [LT["uT"].b()], [u_.b], "put%d" % (jj % 2))

        load_u(0)
        for j in range(T // NT):
            c0 = j * NT
            u = ut[j % 2]
            if j + 1 < T // NT:
                load_u(j + 1)
            s2, s4, s8, s16, dd = s2s[j % 2], s4s[j % 2], s8s[j % 2], s16s[j % 2], dds[j % 2]
            if j == 0:
                k.TS("dve", u[:, :, 0:16], u[:, :, 0:16], k.role[:, 1:2], None, ALU.mult, None, [u.b, k.role.b], [u.b])
            k.TT("dve", s2[:, :, 1:L], u[:, :, 1:L], u[:, :, 0:L - 1], ALU.add, [u.b], [s2.b])
            k.TT("pool", s4[:, :, 3:L], s2[:, :, 3:L], s2[:, :, 1:L - 2], ALU.add, [s2.b], [s4.b])
            k.TT("dve", s8[:, :, 7:L], s4[:, :, 7:L], s4[:, :, 3:L - 4], ALU.add, [s4.b], [s8.b])
            k.TT("pool", s16[:, :, 15:L], s8[:, :, 15:L], s8[:, :, 7:L - 8], ALU.add, [s8.b], [s16.b])
            srcs = [s2, s4, s8, s16]
            for g in range(4):
                c = g // 2
                sl = slice((g % 2) * 64, (g % 2) * 64 + 64)
                sw = srcs[g]
                k.STT("dve", dd[sl, c, :], sw[sl, c, 16:L], 1.0 / wins[g], u[sl, c, 16:L], ALU.mult, ALU.subtract,
                      [sw.b, u.b], [dd.b])
                if j == 0:
                    k.TT("dve", t16[sl, :], sw[sl, c, 16:32], k.role[sl, 2 + c * 16:2 + c * 16 + 16], ALU.mult,
                         [sw.b, k.role.b], [t16.b])
                    k.TT("dve", dd[sl, c, 0:16], t16[sl, :], u[sl, c, 16:32], ALU.subtract, [t16.b, u.b, dd.b], [dd.b])
            o = pst[j % 2]
            for c in range(2):
                pb = k.bank()
                k.MM(pb[:, 0:NT], pwbd[:, c, :], dd[:, c, :], [pwbd.b, dd.b], [pb.b])
                k.TS("dve", o[:, c, :], pb[:, 0:NT], psc[:, c:c + 1], None, ALU.mult, None, [pb.b, psc.b], [o.b])
            k.DMA(mixT.ap()[384:640, c0:c0 + NT].rearrange("(c p) t -> p c t", p=128), o[:, :, :], [o.b],
                  [mixT.b("pool")], "ppst%d" % (j % 2))
    k.P.barrier()


def phase_B_mla(k, l, LT, mixT, pre_hook=None):
    scale = 96.0 ** -0.5
    NKB = 2 * NB
    with contextlib.ExitStack() as es:
        k.es = es
        Vall = k.sb("mV", [128, NKB, 390], BF16)
        KT = [k.sb("mKT%d" % i, [96, 2 * T], BF16) for i in range(2)]
        QT = [k.sb("mQT%d" % i, [96, T], BF16) for i in range(2)]
        Pt = [k.sb("mP%d" % i, [128, 512], BF16) for i in range(5)]
        rrow = [k.sb("mrrow%d" % i, [65, 512], F32) for i in range(2)]
        bcs = k.sb("mbcs", [64, 512], F32)
        ost = [k.sb("most%d" % i, [64, 512], BF16) for i in range(2)]
        for part in range(4):
            b0 = part * 16
            if part < 2:
                srcv = LT["vrecv"][part].ap()[0:T // 2, :]
                srcb = LT["vrecv"][part].b()
            else:
                srcv = LT["vsend"][part - 2].ap()
                srcb = LT["vsend"][part - 2].b()
            k.DMA(Vall[:, b0:b0 + 16, :], srcv.rearrange("(b p) e -> p b e", p=128),
                  [srcb], [Vall.b], "mV")
        SB_ = [k.PB[0], k.PB[1], k.PB[2], k.PB[3], k.PB[7]]
        OTs = [k.PB[4], k.PB[5]]
        LA = 3
        cnt = 0
        pend = None
        def flush_pend():
            nonlocal pend
            if pend is None:
                return
            pot, po, prr, ph, pj, pslot = pend
            pend = None
            softmax_finish_b(k, pot, po[:, :], [po.b], prr, bcs, k.PB[6])
            k.DMA(mixT.ap()[640 + ph * 64:640 + (ph + 1) * 64, pj * 512:(pj + 1) * 512], po[:, :], [po.b],
                  [mixT.b("mla")], "most%d" % pslot)

        def load_head(h):
            kt = KT[h % 2]
            qt_ = QT[h % 2]
            HT = T // 2
            for i in range(2):
                k.DMA(kt[0:64, i * HT:(i + 1) * HT], LT["krecv"][i].ap()[h * 64:(h + 1) * 64, :], [LT["krecv"][i].b()],
                      [kt.b], "mKT%d" % (h % 2))
                k.DMA(kt[0:64, T + i * HT:T + (i + 1) * HT], LT["ksend"][i].ap()[h * 64:(h + 1) * 64, :],
                      [LT["ksend"][i].b()], [kt.b], "mKT%d" % (h % 2))
                k.DMA(kt[64:96, i * HT:(i + 1) * HT], LT["krecv"][i].ap()[384:416, :], [LT["krecv"][i].b()],
                      [kt.b], "mKT%d" % (h % 2))
                k.DMA(kt[64:96, T + i * HT:T + (i + 1) * HT], LT["ksend"][i].ap()[384:416, :],
                      [LT["ksend"][i].b()], [kt.b], "mKT%d" % (h % 2))
            k.DMA(qt_[:, :], LT["qmT"].ap()[h, :, :], [LT["qmT"].b()], [qt_.b], "mQT%d" % (h % 2))

        load_head(0)
        for h in range(6):
            kt = KT[h % 2]
            qt_ = QT[h % 2]
            for j in range(T // 512):
                if h == 0 and j == 2 and pre_hook is not None:
                    pre_hook([ost[0].b])
                if j == 1 and h + 1 < 6:
                    load_head(h + 1)
                nkb = NB + 4 * j + 4
                ot = OTs[cnt % 2]
                o = ost[cnt % 2]
                cnt += 1

                def qlo_of(kb):
                    c = kb - (NB + 4 * j)
                    return 128 * max(c, 0)

                def QK(kb):
                    ql = qlo_of(kb)
                    sbk = SB_[kb % 5]
                    k.MM(sbk[:, ql:512], kt[:, kb * 128:(kb + 1) * 128], qt_[:, j * 512 + ql:(j + 1) * 512],
                         [kt.b, qt_.b], [sbk.b])

                for kb in range(min(LA, nkb)):
                    QK(kb)
                for kb in range(nkb):
                    if kb + LA < nkb:
                        QK(kb + LA)
                    if kb == 4:
                        flush_pend()
                    ql = qlo_of(kb)
                    sbk = SB_[kb % 5]
                    pt = Pt[kb % 5]
                    if kb < NB:
                        k.ACT(pt[:, ql:512], sbk[:, ql:512], AF.Exp, [sbk.b, k.role.b], [pt.b],
                              bias=k.role[:, 0:1], scale=scale)
                    else:
                        k.ACT(pt[:, ql:512], sbk[:, ql:512], AF.Exp, [sbk.b], [pt.b], scale=scale)
                    if kb >= NB + 4 * j:
                        k.TT("dve", pt[:, ql:ql + 128], pt[:, ql:ql + 128], k.cb[:, CB_TRI:CB_TRI + 128], ALU.mult,
                             [pt.b, k.cb.b], [pt.b])
                    k.MM(ot[0:65, ql:512], Vall[:, kb, h * 65:(h + 1) * 65], pt[:, ql:512], [Vall.b, pt.b], [ot.b],
                         start=(kb == 0), stop=(kb == nkb - 1))
                softmax_finish_a(k, ot, None, [], rrow[(cnt - 1) % 2], False)
                pend = (ot, o, rrow[(cnt - 1) % 2], h, j, (cnt - 1) % 2)
        flush_pend()
    k.P.barrier()


def phase_B_ffn(k, l, mixT, xT_src, xT_dst, final, pre=None):
    W = k.W
    NT = 256
    with contextlib.ExitStack() as es:
        k.es = es
        if pre is None:
            wout = k.sb("wout", [128, 8, D], BF16)
            wg = k.sb("wg", [128, 8, DFF], BF16)
        else:
            wout, wg = pre["wout"], pre["wg"]
        wu = k.sb("wu", [128, 8, DFF], BF16)
        wd = k.sb("wd", [128, NFC, D], BF16)
        gF = k.sb("gF", [128, 8], F32)
        fn = W["ffn_norm"]
        k.DMA(gF[:, 0:8], fn.raw(l * 1024, [[1, 128], [128, 8]]), [fn.b()], [gF.b], "ldg", allow_slow_non_contiguous=True)
        wlist = [(wu, W["w_up"], 8), (wd, W["w_down"], NFC)]
        if pre is None:
            wlist = [(wout, W["w_out"], 8), (wg, W["w_gate"], 8)] + wlist
        else:
            for (dst, srcd, nch) in ((wu, pre["wub"], 8), (wd, pre["wdb"], NFC)):
                for c in range(nch):
                    k.DMA(dst[:, c, :], srcd.ap()[c * 128:(c + 1) * 128, :], [srcd.b()], [dst.b],
                          "wldb%d" % (c % 2))
            wlist = []
        for (dst, src, nch) in wlist:
            for c in range(nch):
                k.DMA(dst[:, c, :], src.ap()[l, c * 128:(c + 1) * 128, :], [src.b()], [dst.b], "wld", q="pool")
        mix = [k.sb("fmix%d" % i, [128, 8, NT], BF16) for i in range(1)]
        xt = [k.sb("fxt%d" % i, [128, 8, NT], F32) for i in range(2)]
        h2 = k.sb("fh2", [128, 8, NT], BF16)
        lnt = k.sb("flnt", [128, NT], F32)
        rstd = k.sb("frstd", [128, NT], F32)
        sg = [k.sb("fsg%d" % i, [128, NT], F32) for i in range(2)]
        act = k.sb("fact", [128, NFC, NT], BF16)
        sqb = k.sb("fsqb", [128, 8, NT], BF16)
        ost = k.sb("fost", [128, D], F32) if final else None
        out = k.out if final else None
        NTI = T // NT

        def f_load(i):
            c0 = i * NT
            k.DMA(mix[0][:, :, :], mixT.ap()[:, c0:c0 + NT].rearrange("(c p) t -> p c t", p=128),
                  [mixT.b("swa"), mixT.b("pool"), mixT.b("mla")], [mix[0].b], "fmix0")
            x_ = xt[i % 2]
            k.DMA(x_[:, :, :], xT_src.ap()[:, :, c0:c0 + NT].rearrange("k p t -> p k t"),
                  [xT_src.b(i // 2)], [x_.b], "fxt%d" % (i % 2))

        def f_outproj_norm(i):
            m_ = mix[0]
            x_ = xt[i % 2]
            for m in range(8):
                pb = k.bank()
                for kc in range(8):
                    k.MM(pb[:, 0:NT], wout[:, kc, m * 128:(m + 1) * 128], m_[:, kc, :], [wout.b, m_.b], [pb.b],
                         start=(kc == 0), stop=(kc == 7))
                k.TT("dve", x_[:, m, :], x_[:, m, :], pb[:, 0:NT], ALU.add, [x_.b, pb.b], [x_.b])
            for kc in range(8):
                if kc % 2 == 0:
                    k.ACT(sqb[:, kc, :], x_[:, kc, :], AF.Square, [x_.b], [sqb.b])
                else:
                    k.TT("pool", sqb[:, kc, :], x_[:, kc, :], x_[:, kc, :], ALU.mult, [x_.b], [sqb.b])
            pb = k.bank()
            for kc in range(8):
                k.MM(pb[:, 0:NT], k.cb[:, CB_ONESD:CB_ONESD + 128], sqb[:, kc, :], [k.cb.b, sqb.b], [pb.b],
                     start=(kc == 0), stop=(kc == 7))
            rsqrt_from_ms(k, pb[:, 0:NT], 128, NT, [pb.b], lnt, rstd[:, :], [rstd.b])
            for kc in range(8):
                k.STT("dve", h2[:, kc, :], x_[:, kc, :], gF[:, kc:kc + 1], rstd[:, :],
                      ALU.mult, ALU.mult, [x_.b, rstd.b, gF.b], [h2.b])

        def f_gateup(i):
            for fc in range(NFC):
                pg = k.bank()
                pu = k.bank()
                for kc in range(8):
                    k.MM(pg[:, 0:NT], wg[:, kc, fc * 128:(fc + 1) * 128], h2[:, kc, :], [wg.b, h2.b], [pg.b],
                         start=(kc == 0), stop=(kc == 7))
                for kc in range(8):
                    k.MM(pu[:, 0:NT], wu[:, kc, fc * 128:(fc + 1) * 128], h2[:, kc, :], [wu.b, h2.b], [pu.b],
                         start=(kc == 0), stop=(kc == 7))
                s_ = sg[fc % 2]
                k.ACT(s_[:, :], pg[:, 0:NT], AF.Silu, [pg.b], [s_.b])
                k.TT("dve", act[:, fc, :], s_[:, :], pu[:, 0:NT], ALU.mult, [s_.b, pu.b], [act.b])

        def f_down_store(i):
            c0 = i * NT
            x_ = xt[i % 2]
            for m in range(8):
                pb = k.bank()
                for fc in range(NFC):
                    k.MM(pb[:, 0:NT], wd[:, fc, m * 128:(m + 1) * 128], act[:, fc, :], [wd.b, act.b], [pb.b],
                         start=(fc == 0), stop=(fc == NFC - 1))
                k.TT("dve", x_[:, m, :], x_[:, m, :], pb[:, 0:NT], ALU.add, [x_.b, pb.b], [x_.b])
            if not final:
                k.DMA(xT_dst.ap()[:, :, c0:c0 + NT].rearrange("k p t -> p k t"), x_[:, :, :], [x_.b],
                      [xT_dst.b(i // 2)], "fxst%d" % (i % 2))
            else:
                for tb in range(2):
                    for half in range(2):
                        pb = k.bank()
                        for mm in range(4):
                            m = half * 4 + mm
                            k.TR(pb[:, mm * 128:(mm + 1) * 128], x_[:, m, tb * 128:(tb + 1) * 128],
                                 k.cf[:, CF_ID:CF_ID + 128], [x_.b, k.cf.b], [pb.b])
                        k.CP("act" if half == 0 else "dve", ost[:, half * 512:(half + 1) * 512], pb[:, :],
                             [pb.b], [ost.b])
                    r0 = c0 + tb * 128
                    k.DMA(out.ap()[r0:r0 + 128, :], ost[:, :], [ost.b], [out.b()], "fost")

        f_load(0)
        f_outproj_norm(0)
        for i in range(NTI):
            if i + 1 < NTI:
                f_load(i + 1)
            f_gateup(i)
            if i + 1 < NTI:
                f_outproj_norm(i + 1)
            f_down_store(i)
    k.P.barrier()


def phase_exchange(k, l, LT):
    k.ALLGATHER(LT["ssend"], LT["srecv"], "ag_s")
    k.ALLGATHER(LT["usend"], LT["urecv"], "ag_u")
    for i in range(2):
        k.ALLGATHER(LT["ksend"][i], LT["krecv"][i], "ag_k%d" % i)
        k.ALLGATHER(LT["vsend"][i], LT["vrecv"][i], "ag_v%d" % i)


def build(cfg):
    nc = bass.Bass("TRN2", target_bir_lowering=False)
    k = K(nc, cfg)
    phases = cfg["phases"]
    k.in_x = k.dram("x", [T, D], F32, kind="ExternalInput")
    k.in_pos = k.dram("pos", [1, T], I32, kind="ExternalInput")
    k.in_cb = k.dram("cb", [128, NCB], BF16, kind="ExternalInput")
    k.in_cf = k.dram("cf", [128, NCF], F32, kind="ExternalInput")
    k.in_role = k.dram("role", [128, NROLE], F32, kind="ExternalInput")
    k.W = {n: k.dram(n, shp, F32, kind="ExternalInput") for n, shp in WEIGHT_NAMES}
    if "B1" in phases:
        k.out = k.dram("out", [T, D], F32, kind="ExternalOutput")
    xTv = [k.dram("xT_v%d" % i, [8, 128, T], F32) for i in range(2)]
    mixT = [k.dram("mixT_%d" % i, [D, T], BF16) for i in range(2)]
    LT = [layer_tensors(k, l) for l in range(DEPTH)]
    with contextlib.ExitStack() as outer:
        k.es = outer
        k.cb = k.sb("cbs", [128, NCB], BF16)
        k.cf = k.sb("cfs", [128, NCF], F32)
        k.role = k.sb("roles", [128, NROLE], F32)
        k.epsb = k.sb("epsb", [128, 1], F32)
        k.PB = [Tl(outer.enter_context(nc.psum_tensor("pb%d" % i, [128, 512], F32)), "pb%d" % i) for i in range(8)]
        for ph in phases:
            if ph == "setup":
                phase_setup(k)
            elif ph == "A0":
                (phase_A2 if cfg.get("pipeA", True) else phase_A)(k, 0, "transpose", None, xTv[0], LT[0])
            elif ph == "A1":
                (phase_A2 if cfg.get("pipeA", True) else phase_A)(k, 1, "load", xTv[1], None, LT[1])
            elif ph in ("X0", "X1"):
                phase_exchange(k, int(ph[1]), LT[int(ph[1])])
            elif ph in ("B0", "B1"):
                l = int(ph[1])
                sub = cfg.get("sub", ("swa", "pool", "mla", "ffn"))
                with contextlib.ExitStack() as es_pre:
                    k.es = es_pre
                    pre = {"wout": k.sb("wout", [128, 8, D], BF16), "wg": k.sb("wg", [128, 8, DFF], BF16)}

                    wub = k.dram("wu_bf16", [D, DFF], BF16)
                    wdb = k.dram("wd_bf16", [DFF, D], BF16)
                    pre["wub"] = wub
                    pre["wdb"] = wdb

                    def pre_hook(extra_R, l=l, pre=pre, wub=wub, wdb=wdb):
                        for (dst, src, nch) in ((pre["wout"], k.W["w_out"], 8), (pre["wg"], k.W["w_gate"], 8)):
                            for c in range(nch):
                                k.DMA(dst[:, c, :], src.ap()[l, c * 128:(c + 1) * 128, :], [src.b()] + list(extra_R),
                                      [dst.b], "wld", q="pool")
                        for (dstd, src, nch) in ((wub, k.W["w_up"], 8), (wdb, k.W["w_down"], NFC)):
                            for c in range(nch):
                                k.DMA(dstd.ap()[c * 128:(c + 1) * 128, :], src.ap()[l, c * 128:(c + 1) * 128, :],
                                      [src.b()], [dstd.b()], "wcast", q="pool")
                    if "swa" in sub:
                        phase_B_swa(k, l, LT[l], mixT[l])
                    if "pool" in sub:
                        phase_B_pool(k, l, LT[l], mixT[l])
                    if "mla" in sub:
                        phase_B_mla(k, l, LT[l], mixT[l], pre_hook=pre_hook)
                    if "ffn" in sub:
                        phase_B_ffn(k, l, mixT[l], xTv[l], xTv[1] if l == 0 else None, final=(l == 1), pre=pre)
            k.es = outer
        sem_stack = contextlib.ExitStack()
        k.P.emit(sem_stack)
        sem_stack.close()
    return nc, k


def _core_inputs(inputs, cb, cf):
    maps = []
    for c in range(NCORES):
        b, hh = c // 2, c % 2
        m = {"x": np.ascontiguousarray(inputs["x"][b, hh * T:(hh + 1) * T, :]),
             "pos": np.ascontiguousarray(inputs["positions"][b, hh * T:(hh + 1) * T]).reshape(1, T).astype(np.int32),
             "cb": cb, "cf": cf, "role": host_role(c)}
        for n, _ in WEIGHT_NAMES:
            m[n] = np.ascontiguousarray(inputs[n])
        maps.append(m)
    return maps


def kernel(**inputs):
    cb, cf = host_consts()
    base = _core_inputs(inputs, cb, cf)
    ids = list(range(NCORES))
    cfg = {"phases": ["setup", "A0", "X0", "B0", "A1", "X1", "B1"], "ext_in": set(), "ext_out": set()}
    nc, _ = build(cfg)
    res = run_bass_kernel_spmd(nc, base, core_ids=ids).results
    out = np.zeros((4, S, D), np.float32)
    for c in range(NCORES):
        out[c // 2, (c % 2) * T:(c % 2 + 1) * T, :] = np.asarray(res[c]["out"], dtype=np.float32)
    return out
```
